# Optimizing a Trainium2 kernel written in Bass

```python
import math
import jax, jax.numpy as jnp
from jax import lax
import numpy as np

D_MODEL = 1024
BATCH = 4
SEQ = 8192
DEPTH = 1

HEAD_DIM = 64
DA_HEADS = 4
DA_QK = DA_HEADS * 2 * HEAD_DIM
DA_V = DA_HEADS * 2 * HEAD_DIM
NA_HEADS = 8
NA_W = NA_HEADS * HEAD_DIM
GRID_W = 64
NA_ROWS = 8
NA_COLS = 16
Q_BLOCK = 128
D_FF = int(math.ceil(8 * D_MODEL / 3 / 256) * 256)
IN_COLS = 2 * DA_QK + DA_V + 3 * NA_W + 2 * D_MODEL
EPS = 1e-6
NEG = -1e30

kernel_name = "hybrid_diffattn_natten_gated_encoder"


def rmsnorm(x, w):
    xf = x.astype(jnp.float32)
    y = xf * lax.rsqrt(jnp.mean(xf * xf, axis=-1, keepdims=True) + EPS)
    return (y * w.astype(jnp.float32)).astype(x.dtype)


def alibi_slopes(n):
    return jnp.asarray([2.0 ** (-8.0 * (i + 1) / n) for i in range(n)], dtype=jnp.float32)


def diff_attention(q, k, v, lam, subln_w, lambda_init):
    B, S, H, _, d = q.shape
    nb = S // Q_BLOCK
    scale = d ** -0.5
    slopes = alibi_slopes(H)
    kpos = jnp.arange(S)
    qb = jnp.moveaxis(q.reshape(B, nb, Q_BLOCK, H, 2, d), 1, 0)

    def block(args):
        qi, i = args
        s = jnp.einsum('bqhcd,bkhcd->bhcqk', qi, k).astype(jnp.float32) * scale
        qpos = i * Q_BLOCK + jnp.arange(Q_BLOCK)
        dist = jnp.abs(qpos[:, None] - kpos[None, :]).astype(jnp.float32)
        s = s - slopes[None, :, None, None, None] * dist[None, None, None]
        p = jax.nn.softmax(s, axis=-1)
        a = p[:, :, 0] - lam * p[:, :, 1]
        return jnp.einsum('bhqk,bkhe->bqhe', a.astype(v.dtype), v)

    o = lax.map(block, (qb, jnp.arange(nb)))
    o = jnp.moveaxis(o, 0, 1).reshape(B, S, H, 2 * d)
    o = rmsnorm(o, subln_w) * (1.0 - lambda_init)
    return o.reshape(B, S, H * 2 * d)


def neighbourhood_attention_2d(q, k, v, rpb):
    B, S, H, d = q.shape
    rows = S // GRID_W
    kr = min(NA_ROWS, rows)
    kc = NA_COLS
    scale = d ** -0.5
    qg = q.reshape(B, rows, GRID_W, H, d)
    kg = k.reshape(B, rows, GRID_W, H, d)
    vg = v.reshape(B, rows, GRID_W, H, d)
    col = jnp.arange(GRID_W)
    col_start = jnp.clip(col - kc // 2, 0, GRID_W - kc)
    col_in = (col[None, :] >= col_start[:, None]) & (col[None, :] < col_start[:, None] + kc)
    col_off = jnp.clip(col[None, :] - col[:, None] + NA_COLS - 1, 0, 2 * NA_COLS - 2)
    rpb_cols = rpb[:, :, col_off]

    def row_step(args):
        qr, r = args
        rs = jnp.clip(r - kr // 2, 0, rows - kr)
        kb = lax.dynamic_slice_in_dim(kg, rs, kr, axis=1)
        vb = lax.dynamic_slice_in_dim(vg, rs, kr, axis=1)
        s = jnp.einsum('bqhd,bikhd->bhqik', qr, kb).astype(jnp.float32) * scale
        row_off = rs + jnp.arange(kr) - r + NA_ROWS - 1
        bias = jnp.transpose(rpb_cols[:, row_off], (0, 2, 1, 3))
        s = s + bias[None].astype(jnp.float32)
        s = jnp.where(col_in[None, None, :, None, :], s, NEG)
        p = jax.nn.softmax(s, axis=(-2, -1))
        return jnp.einsum('bhqik,bikhd->bqhd', p.astype(v.dtype), vb)

    o = lax.map(row_step, (jnp.moveaxis(qg, 1, 0), jnp.arange(rows)))
    return jnp.moveaxis(o, 0, 1).reshape(B, S, H * d)


def setup_inputs(seed: int = 0) -> dict:
    key = jax.random.key(seed)
    ks = jax.random.split(key, 24)
    f32 = jnp.float32

    def nrm(k, shape, scale):
        return jax.random.normal(k, shape, f32) * scale

    def gain(k):
        return 1.0 + nrm(k, (DEPTH, D_MODEL), 0.02)

    return {
        "x": nrm(ks[0], (BATCH, SEQ, D_MODEL), 1.0),
        "pre_mix_w": gain(ks[1]),
        "w_in": nrm(ks[2], (DEPTH, D_MODEL, IN_COLS), D_MODEL ** -0.5),
        "b_gate": nrm(ks[3], (DEPTH, 2 * D_MODEL), 0.1),
        "lambda_q1": nrm(ks[4], (DEPTH, HEAD_DIM), 0.1),
        "lambda_k1": nrm(ks[5], (DEPTH, HEAD_DIM), 0.1),
        "lambda_q2": nrm(ks[6], (DEPTH, HEAD_DIM), 0.1),
        "lambda_k2": nrm(ks[7], (DEPTH, HEAD_DIM), 0.1),
        "subln_w": 1.0 + nrm(ks[8], (DEPTH, 2 * HEAD_DIM), 0.02),
        "rpb": nrm(ks[9], (DEPTH, NA_HEADS, 2 * NA_ROWS - 1, 2 * NA_COLS - 1), 0.1),
        "w_branch_a": nrm(ks[10], (DEPTH, DA_V, D_MODEL), DA_V ** -0.5),
        "w_branch_b": nrm(ks[11], (DEPTH, NA_W, D_MODEL), NA_W ** -0.5),
        "w_out": nrm(ks[12], (DEPTH, D_MODEL, D_MODEL), D_MODEL ** -0.5),
        "post_mix_w": gain(ks[13]),
        "pre_ffn_w": gain(ks[14]),
        "w_gate": nrm(ks[15], (DEPTH, D_MODEL, D_FF), D_MODEL ** -0.5),
        "w_up": nrm(ks[16], (DEPTH, D_MODEL, D_FF), D_MODEL ** -0.5),
        "w_down": nrm(ks[17], (DEPTH, D_FF, D_MODEL), D_FF ** -0.5),
        "post_ffn_w": gain(ks[18]),
    }


def reference(x, pre_mix_w, w_in, b_gate, lambda_q1, lambda_k1, lambda_q2, lambda_k2,
              subln_w, rpb, w_branch_a, w_branch_b, w_out, post_mix_w, pre_ffn_w,
              w_gate, w_up, w_down, post_ffn_w):
    B, S, _ = x.shape
    for l in range(DEPTH):
        lambda_init = 0.8 - 0.6 * math.exp(-0.3 * l)
        h = rmsnorm(x, pre_mix_w[l])
        proj = h @ w_in[l]
        o0 = 0
        q_a = proj[..., o0:o0 + DA_QK].reshape(B, S, DA_HEADS, 2, HEAD_DIM); o0 += DA_QK
        k_a = proj[..., o0:o0 + DA_QK].reshape(B, S, DA_HEADS, 2, HEAD_DIM); o0 += DA_QK
        v_a = proj[..., o0:o0 + DA_V].reshape(B, S, DA_HEADS, 2 * HEAD_DIM); o0 += DA_V
        q_b = proj[..., o0:o0 + NA_W].reshape(B, S, NA_HEADS, HEAD_DIM); o0 += NA_W
        k_b = proj[..., o0:o0 + NA_W].reshape(B, S, NA_HEADS, HEAD_DIM); o0 += NA_W
        v_b = proj[..., o0:o0 + NA_W].reshape(B, S, NA_HEADS, HEAD_DIM); o0 += NA_W
        gates = jax.nn.sigmoid(proj[..., o0:] + b_gate[l])
        g_a, g_b = gates[..., :D_MODEL], gates[..., D_MODEL:]

        lq1 = lambda_q1[l].astype(jnp.float32); lk1 = lambda_k1[l].astype(jnp.float32)
        lq2 = lambda_q2[l].astype(jnp.float32); lk2 = lambda_k2[l].astype(jnp.float32)
        lam = jnp.exp(jnp.sum(lq1 * lk1)) - jnp.exp(jnp.sum(lq2 * lk2)) + lambda_init

        y_a = diff_attention(q_a, k_a, v_a, lam, subln_w[l], lambda_init)
        y_b = neighbourhood_attention_2d(q_b, k_b, v_b, rpb[l])
        merged = g_a * (y_a @ w_branch_a[l]) + g_b * (y_b @ w_branch_b[l])
        x = x + rmsnorm(merged @ w_out[l], post_mix_w[l])
        h = rmsnorm(x, pre_ffn_w[l])
        f = (jax.nn.silu(h @ w_gate[l]) * (h @ w_up[l])) @ w_down[l]
        x = x + rmsnorm(f, post_ffn_w[l])
    return x
```

```python
import contextlib
import math
import numpy as np
import concourse.bass as bass
import concourse.mybir as mybir
from concourse.bass_utils import run_bass_kernel_spmd

F32 = mybir.dt.float32
BF16 = mybir.dt.bfloat16
AF = mybir.ActivationFunctionType
ALU = mybir.AluOpType
AX = mybir.AxisListType

D = 1024
S = 8192
HALF = 4096
B = 4
DFF = 2816
NJ = DFF // 128
INC = 5120
EPS = 1e-6
NEG = -1e30
SLOPES = [2.0 ** (-8.0 * (i + 1) / 4) for i in range(4)]
LAMBDA_INIT = 0.8 - 0.6 * math.exp(-0.3 * 0)
NA_LOC_TOK = 4608

ENGS = ("pe", "act", "dve", "pool", "sp")
SAME_ENGINE_SYNC = ("act", "dve", "pool")


class Prog:
    def __init__(self, nc):
        self.nc = nc
        self.ops = {e: [] for e in ENGS}
        self.eng_cnt = {e: 0 for e in ENGS}
        self.dma_cnt = {}
        self.waited = {e: {} for e in ENGS}
        self.res = {}
        self.sems = {}
        self.nins = 0

    def _sem(self, key):
        if key not in self.sems:
            self.sems[key] = self.nc.alloc_semaphore(name="s%d" % len(self.sems))
        return self.sems[key]

    def op(self, eng, fns, reads=(), writes=(), dma_key=None):
        if callable(fns):
            fns = [fns]
        fns = list(fns)
        need = {}

        def want(t):
            if t is None:
                return
            k, v = t
            if k == ("eng", eng) and eng not in SAME_ENGINE_SYNC:
                return
            if self.waited[eng].get(k, 0) >= v:
                return
            if need.get(k, 0) < v:
                need[k] = v

        for r in reads:
            st = self.res.get(r)
            if st:
                want(st[0])
        for w in writes:
            st = self.res.get(w)
            if st:
                want(st[0])
                for t in st[1]:
                    want(t)
        for k, v in need.items():
            self.waited[eng][k] = v
        if dma_key is not None:
            k = ("dma", dma_key)
            self.dma_cnt[k] = self.dma_cnt.get(k, 0) + 16 * len(fns)
            ticket = (k, self.dma_cnt[k])
            mode = "dma"
        else:
            k = ("eng", eng)
            self.eng_cnt[eng] += 1
            ticket = (k, self.eng_cnt[eng])
            mode = "cmp"
        self._sem(k)
        for kk in need:
            self._sem(kk)
        for r in reads:
            st = self.res.setdefault(r, [None, []])
            st[1].append(ticket)
        for w in writes:
            self.res[w] = [ticket, []]
        self.ops[eng].append((sorted(need.items(), key=str), fns, k, mode))
        self.nins += len(fns)
        return ticket

    def barrier(self):
        allt = []
        for e in ENGS:
            if self.eng_cnt[e]:
                allt.append((("eng", e), self.eng_cnt[e]))
        for k, v in self.dma_cnt.items():
            allt.append((k, v))
        self.res = {"__bar__": [None, list(allt)]}
        self.op("sp", lambda e: e.nop(), writes=["__bar__"])
        for e in ENGS:
            if e != "sp":
                self.op(e, lambda en: en.nop(), reads=["__bar__"])
        self.res = {}

    def _emit_engine(self, name, eng):
        for waits, fns, k, mode in self.ops[name]:
            for sk, v in waits:
                eng.wait_ge(self.sems[sk], v)
            n = len(fns)
            for i, fn in enumerate(fns):
                ins = fn(eng)
                if mode == "dma":
                    ins.then_inc(self.sems[k], 16)
                elif i == n - 1:
                    ins.then_inc(self.sems[k], 1)
        self.ops[name] = []

    def emit(self):
        with self.nc.Block() as block:
            @block.tensor
            def _(e):
                self._emit_engine("pe", e)

            @block.scalar
            def _(e):
                self._emit_engine("act", e)

            @block.vector
            def _(e):
                self._emit_engine("dve", e)

            @block.gpsimd
            def _(e):
                self._emit_engine("pool", e)

            @block.sync
            def _(e):
                self._emit_engine("sp", e)


def MM(out, lhsT, rhs, start=True, stop=True):
    return lambda e: e.matmul(out, lhsT=lhsT, rhs=rhs, start=start, stop=stop)


def DMA(out, in_):
    return lambda e: e.dma_start(out=out, in_=in_)


def TT(out, a, b, op):
    return lambda e: e.tensor_tensor(out, a, b, op)


def TS(out, a, s1, s2, op0, op1=None):
    if op1 is None:
        return lambda e: e.tensor_scalar(out, a, s1, None, op0)
    return lambda e: e.tensor_scalar(out, a, s1, s2, op0, op1)


def STT(out, in0, scalar, in1, op0, op1):
    return lambda e: e.scalar_tensor_tensor(out, in0, scalar, in1, op0, op1)


def ACTF(out, in_, func, bias=None, scale=1.0):
    if bias is None:
        return lambda e: e.activation(out, in_, func, scale=scale)
    return lambda e: e.activation(out, in_, func, bias=bias, scale=scale)


def COPY(out, in_):
    return lambda e: e.tensor_copy(out, in_)


def RECIP(out, in_):
    return lambda e: e.reciprocal(out, in_)


def build_program(stop_after=99, dbg=False):
    nc = bass.Bass("TRN2", target_bir_lowering=False)

    def din(name, shape, dt=F32):
        return nc.dram_tensor(name, list(shape), dt, kind="ExternalInput").ap()

    def dscr(name, shape, dt):
        return nc.dram_tensor(name, list(shape), dt, kind=("ExternalOutput" if dbg else "Internal")).ap()

    xT = din("xT", [D, S])
    w_in = din("w_in", [D, INC])
    w_a = din("w_a", [512, D])
    w_b = din("w_b", [512, D])
    w_o = din("w_o", [D, D])
    w_gate = din("w_gate", [D, DFF])
    w_up = din("w_up", [D, DFF])
    w_down = din("w_down", [DFF, D])
    gains_d = din("gains", [128, 4, 8])
    bgate_d = din("bgate", [128, 16])
    subln_d = din("subln", [128, 1])
    lamv_d = din("lamv", [128, 4, 64])
    sublnT_d = din("sublnT", [128, 128])
    nab_d = din("nab", [3, 5, 128, 1024])
    outT = nc.dram_tensor("outT", [D, HALF], F32, kind="ExternalOutput").ap()

    QaT = dscr("QaT", [4, 128, HALF], BF16)
    KaT = dscr("KaT", [4, 128, S], BF16)
    Va = dscr("Va", [S, 512], BF16)
    QbT = dscr("QbT", [4, 128, HALF], BF16)
    KbT = dscr("KbT", [4, 128, NA_LOC_TOK], BF16)
    Vb = dscr("Vb", [NA_LOC_TOK, 512], BF16)
    yaT = dscr("yaT", [4, 128, HALF], BF16)
    ybT = dscr("ybT", [8, 64, HALF], BF16)
    x1T = dscr("x1T", [D, HALF], F32)

    P = Prog(nc)
    ps = nc.alloc_psum_tensor("ps", [128, 8, 512], F32).ap()

    def sb(es, name, shape, dt):
        return es.enter_context(nc.sbuf_tensor(name, list(shape), dt)).ap()

    with contextlib.ExitStack() as G:
        ones = sb(G, "ones", [128, 128], BF16)
        neghalf = sb(G, "neghalf", [128, 512], F32)
        gains = sb(G, "gains_sb", [128, 4, 8], F32)
        bgate = sb(G, "bgate_sb", [128, 16], F32)
        subln = sb(G, "subln_sb", [128, 1], F32)
        gsub = sb(G, "gsub", [128, 1], F32)
        neglam = sb(G, "neglam", [128, 1], F32)

        P.op("pool", lambda e: e.memset(ones, 1.0), writes=["ones"])
        P.op("pool", lambda e: e.memset(neghalf, -0.5), writes=["neghalf"])
        P.op("sp", DMA(gains, gains_d), writes=["gains"], dma_key="c0")
        P.op("sp", DMA(bgate, bgate_d), writes=["bgate"], dma_key="c1")
        P.op("sp", DMA(subln, subln_d), writes=["subln"], dma_key="c2")

        def cast_fns(dst, src, ncols):
            fns = []
            for c0 in range(0, ncols, 2048):
                c1 = min(ncols, c0 + 2048)
                fns.append(DMA(dst[:, c0:c1], src[:, c0:c1]))
            return fns

        def cast_load_all(pairs, key):
            fns = []
            for dst, src, ncols in pairs:
                fns += cast_fns(dst, src, ncols)
            P.op("pool", fns, writes=[key], dma_key=key)

        def rmsnorm_in(xt, gidx, hT, sq, rt, rstd, N, tag, ss_bank):
            P.op("dve", TT(sq[:, :, :N], xt[:, :, :N], xt[:, :, :N], ALU.mult), reads=[xt_key(tag)], writes=[("sq", k) for k in range(8)])
            P.op("pe", [MM(ps[:, ss_bank, :N], ones, sq[:, kc, :N], start=(kc == 0), stop=(kc == 7)) for kc in range(8)],
                 reads=[("sq", k) for k in range(8)] + ["ones"], writes=[("ps", ss_bank)])
            P.op("dve", TS(rt[:, :N], ps[:, ss_bank, :N], 1.0 / D, EPS, ALU.mult, ALU.add),
                 reads=[("ps", ss_bank)], writes=["rt"])
            P.op("act", ACTF(rt[:, :N], rt[:, :N], AF.Ln), reads=["rt"], writes=["rt"])
            P.op("act", ACTF(rstd[:, :N], rt[:, :N], AF.Exp, scale=-0.5), reads=["rt"], writes=["rstd"])
            for kc in range(8):
                P.op("dve", STT(hT[:, kc, :N], xt[:, kc, :N], gains[:, gidx, kc:kc + 1], rstd[:, :N], ALU.mult, ALU.mult),
                     reads=[xt_key(tag), "rstd", "gains"], writes=[("hT", kc)])

        def xt_key(tag):
            return ("xt", tag)

        with contextlib.ExitStack() as E1:
            win = sb(E1, "win", [128, 8, 3072], BF16)
            xts = [sb(E1, "xt%d" % i, [128, 8, 512], F32) for i in range(2)]
            sq = sb(E1, "sq1", [128, 8, 512], BF16)
            hT = sb(E1, "hT1", [128, 8, 512], BF16)
            rt = sb(E1, "rt1", [128, 512], F32)
            rstd = sb(E1, "rstd1", [128, 512], F32)
            NSTG = 6
            stg = [sb(E1, "stg%d" % i, [128, 512], BF16) for i in range(NSTG)]
            w_in_v = w_in.rearrange("(kc p) n -> p kc n", p=128)
            cast_load_all([(win[:, kc, :], w_in_v[:, kc, 0:3072], 3072) for kc in range(8)], "win")
            xT_v = xT.rearrange("(kc p) t -> p kc t", p=128)
            banks = [1, 2, 3, 4, 5, 6, 7]
            st = {"b": 0, "s": 0, "ev": 0}

            def load_x(t):
                P.op("sp", DMA(xts[t % 2], xT_v[:, :, t * 512:(t + 1) * 512]), writes=[xt_key(t % 2)], dma_key=("xt", t % 2))

            def evac_store(bank, dst, scale=None):
                k = st["s"] % NSTG
                st["s"] += 1
                eng = "act" if st["ev"] % 2 == 0 else "dve"
                st["ev"] += 1
                if eng == "act":
                    fn = ACTF(stg[k], ps[:, bank, :], AF.Copy, scale=(1.0 if scale is None else scale))
                else:
                    if scale is None:
                        fn = COPY(stg[k], ps[:, bank, :])
                    else:
                        fn = TS(stg[k], ps[:, bank, :], scale, None, ALU.mult)
                P.op(eng, fn, reads=[("ps", bank)], writes=[("stg", k)])
                P.op("sp", DMA(dst, stg[k]), reads=[("stg", k)], dma_key=("stg", k))

            def fm_chunk(col0, dst, scale=None):
                bank = banks[st["b"] % len(banks)]
                st["b"] += 1
                P.op("pe", [MM(ps[:, bank, :], win[:, kc, col0:col0 + 128], hT[:, kc, :], start=(kc == 0), stop=(kc == 7))
                            for kc in range(8)],
                     reads=[("hT", kc) for kc in range(8)] + ["win"], writes=[("ps", bank)])
                evac_store(bank, dst, scale)

            def tm_chunk(col0, sub, dst):
                bank = banks[st["b"] % len(banks)]
                st["b"] += 1
                P.op("pe", [MM(ps[:, bank, :], hT[:, kc, sub * 128:(sub + 1) * 128], win[:, kc, col0:col0 + 512],
                               start=(kc == 0), stop=(kc == 7)) for kc in range(8)],
                     reads=[("hT", kc) for kc in range(8)] + ["win"], writes=[("ps", bank)])
                evac_store(bank, dst)

            load_x(0)
            for t in range(16):
                if t + 1 < 16:
                    load_x(t + 1)
                rmsnorm_in(xts[t % 2], 0, hT, sq, rt, rstd, 512, t % 2, 0)
                tok = slice(t * 512, (t + 1) * 512)
                mine = t < 8
                nb = t <= 8
                for h in range(4):
                    if mine:
                        fm_chunk(h * 128, QaT[h, :, tok], scale=0.125)
                    fm_chunk(512 + h * 128, KaT[h, :, tok])
                for sub in range(4):
                    tm_chunk(1024, sub, Va[t * 512 + sub * 128: t * 512 + (sub + 1) * 128, :])
                if nb:
                    for hp in range(4):
                        if mine:
                            fm_chunk(1536 + hp * 128, QbT[hp, :, tok], scale=0.125)
                        fm_chunk(2048 + hp * 128, KbT[hp, :, tok])
                    for sub in range(4):
                        tm_chunk(2560, sub, Vb[t * 512 + sub * 128: t * 512 + (sub + 1) * 128, :])
            P.barrier()
            P.emit()
            if stop_after == 1:
                return nc

        with contextlib.ExitStack() as E2:
            kb = sb(E2, "kb", [128, 4, NA_LOC_TOK], BF16)
            vb = sb(E2, "vb", [128, NA_LOC_TOK // 128, 512], BF16)
            qb = sb(E2, "qb", [128, 4, HALF], BF16)
            nbias = [sb(E2, "nbias%d" % i, [128, 5, 1024], F32) for i in range(2)]
            tmpa = [sb(E2, "tmpa%d" % i, [128, 1024], F32) for i in range(2)]
            pTa = [sb(E2, "pTa%d" % i, [128, 8, 128], BF16) for i in range(3)]
            lnz = sb(E2, "lnz", [64, 1024], F32)
            rz = sb(E2, "rz", [64, 1024], F32)
            ystg = [sb(E2, "ystg%d" % i, [64, 8, 256], BF16) for i in range(2)]
            P.op("sp", DMA(kb, KbT.rearrange("h p t -> p h t")), writes=["kb"], dma_key="kb")
            P.op("sp", DMA(qb, QbT.rearrange("h p t -> p h t")), writes=["qb"], dma_key="qb")
            Vb_v = Vb.rearrange("(t p) c -> p t c", p=128)
            P.op("sp", [DMA(vb[:, i * 9:(i + 1) * 9, :], Vb_v[:, i * 9:(i + 1) * 9, :]) for i in range(4)], writes=["vb"], dma_key="vb")
            nab_v = nab_d.rearrange("c j p n -> c p j n")
            P.op("sp", DMA(nbias[0], nab_v[2]), writes=[("nbias", 0)], dma_key="nb0")
            ybT_pv = ybT.rearrange("(hp par) d q -> par d hp q", par=2)
            nsteps = [(p, j) for p in range(32) for j in range(5)]

            def na_qk(i):
                p, j = nsteps[i]
                sset = i % 2
                t0 = max(2 * p - 4, 0)
                ktok = (t0 + 2 * j) * 64
                qs = slice(p * 128, (p + 1) * 128)
                fns = []
                for hp in range(4):
                    fns.append(MM(ps[:, 2 * sset, hp * 128:(hp + 1) * 128], kb[0:64, hp, ktok:ktok + 128], qb[0:64, hp, qs]))
                    fns.append(MM(ps[:, 2 * sset + 1, hp * 128:(hp + 1) * 128], kb[64:128, hp, ktok:ktok + 128], qb[64:128, hp, qs]))
                P.op("pe", fns, reads=["kb", "qb"], writes=[("S", sset)])

            def na_av(i):
                p, j = nsteps[i]
                sset = i % 2
                t0 = max(2 * p - 4, 0)
                ktile = t0 // 2 + j
                if p < 2:
                    if j == 0:
                        P.op("sp", DMA(nbias[1], nab_v[p]), writes=[("nbias", 1)], dma_key="nb1")
                    bslot = 1
                else:
                    bslot = 0
                S2 = ps[:, 2 * sset:2 * sset + 2, :]
                pt = pTa[i % 3]
                tm = tmpa[i % 2]
                ptk = ("pTa", i % 3)
                tmk = ("tmpa", i % 2)
                P.op("dve", TT(tm.rearrange("p (a b) -> p a b", a=2), S2, nbias[bslot][:, j, :].rearrange("p (a b) -> p a b", a=2), ALU.add),
                     reads=[("S", sset), ("nbias", bslot)], writes=[tmk])
                P.op("act", ACTF(pt.rearrange("p h q -> p (h q)"), tm, AF.Exp), reads=[tmk], writes=[ptk])
                fns = [MM(ps[0:64, 6, :], ones[:, 0:64], pt[:, 0:4, :].rearrange("p h q -> p (h q)"), start=(j == 0), stop=(j == 4)),
                       MM(ps[0:64, 7, :], ones[:, 0:64], pt[:, 4:8, :].rearrange("p h q -> p (h q)"), start=(j == 0), stop=(j == 4))]
                for hq in range(8):
                    h = 2 * (hq % 4) + hq // 4
                    ob = 4 + hq // 4
                    fns.append(lambda e, hq=hq, h=h, j=j, ob=ob, ktile=ktile, pt=pt: e.matmul(
                        ps[0:64, ob, (hq % 4) * 128:(hq % 4 + 1) * 128], lhsT=vb[:, ktile, h * 64:(h + 1) * 64], rhs=pt[:, hq, :],
                        start=(j == 0 and hq % 4 == 0), stop=(j == 4), skip_group_check=True))
                P.op("pe", fns, reads=[ptk, "vb", "ones"], writes=["O", "Z"])
                if j == 4:
                    yk = ("ystg", (p // 2) % 2)
                    ys = ystg[(p // 2) % 2]
                    P.op("act", ACTF(lnz.rearrange("p (a b) -> p a b", a=2), ps[0:64, 6:8, :], AF.Ln), reads=["Z"], writes=["lnz"])
                    P.op("act", ACTF(rz, lnz, AF.Exp, scale=-1.0), reads=["lnz"], writes=["rz"])
                    for half in range(2):
                        P.op("dve", TT(ys[:, 4 * half:4 * half + 4, (p % 2) * 128:(p % 2 + 1) * 128],
                                       ps[0:64, 4 + half, :].rearrange("p (h q) -> p h q", h=4),
                                       rz[:, half * 512:(half + 1) * 512].rearrange("p (h q) -> p h q", h=4), ALU.mult),
                             reads=["O", "rz", yk], writes=[yk])
                    if p % 2 == 1:
                        q0 = (p - 1) * 128
                        P.op("sp", [DMA(ybT_pv[par, :, :, q0:q0 + 256], ys[:, par * 4:(par + 1) * 4, :]) for par in range(2)],
                             reads=[yk], dma_key=yk)

            na_qk(0)
            for i in range(len(nsteps)):
                if i + 1 < len(nsteps):
                    na_qk(i + 1)
                na_av(i)
            P.barrier()
            P.emit()
            if stop_after == 2:
                return nc

        with contextlib.ExitStack() as E3:
            ktb = [sb(E3, "ktb%d" % i, [128, S], BF16) for i in range(1)]
            vab = [sb(E3, "vab%d" % i, [128, 64, 130], BF16) for i in range(1)]
            qtb = [sb(E3, "qtb%d" % i, [128, HALF], BF16) for i in range(1)]
            btb = sb(E3, "btb", [128, 4, 32], F32)
            bta = sb(E3, "bta", [128, 4, 64], F32)
            dist_b = sb(E3, "dist_b", [128, 32], F32)
            dist_a = sb(E3, "dist_a", [128, 64], F32)
            qp_b = sb(E3, "qp_b", [128, 4], F32)
            qp_a = sb(E3, "qp_a", [128, 4], F32)
            wpb = sb(E3, "wpb", [128, 4, 4], F32)
            wpa = sb(E3, "wpa", [128, 4, 4], F32)
            dabs = sb(E3, "dabs", [128, 4, 512], F32)
            dneg = sb(E3, "dneg", [128, 4, 512], F32)
            identf = sb(E3, "identf", [128, 128], F32)
            identb = sb(E3, "identb", [128, 128], BF16)
            gsubT = sb(E3, "gsubT", [128, 128], F32)
            lamv = sb(E3, "lamv_sb", [128, 4, 64], F32)
            lprod = sb(E3, "lprod", [128, 2, 64], F32)
            lsum = sb(E3, "lsum", [128, 2], F32)
            lexp = sb(E3, "lexp", [128, 2], F32)
            pTd = [sb(E3, "pTd%d" % i, [128, 2, 512], BF16) for i in range(3)]
            tmpd = sb(E3, "tmpd", [128, 2, 512], F32)
            accs = sb(E3, "accs", [128, 4, 2, 129], F32)
            rr = sb(E3, "rr", [128, 4, 2], F32)
            nl = sb(E3, "nl", [128, 4], F32)
            t1 = sb(E3, "t1", [128, 4, 128], F32)
            o_sb = sb(E3, "o_sb", [128, 4, 128], F32)
            junk = sb(E3, "junk", [128, 128], F32)
            ssq = sb(E3, "ssq", [128, 4], F32)
            rs = sb(E3, "rs", [128, 4], F32)
            yq = sb(E3, "yq", [128, 4, 128], BF16)
            ydst = [sb(E3, "ydst%d" % i, [128, 512], BF16) for i in range(2)]

            def iota(out, pattern, base, cm):
                return lambda e: e.iota(out, pattern, base=base, channel_multiplier=cm, allow_small_or_imprecise_dtypes=True)

            P.op("pool", iota(dist_b, [[128, 32]], 0, -1), writes=["dist_b"])
            P.op("pool", iota(dist_a, [[128, 64]], -511, 1), writes=["dist_a"])
            P.op("pool", iota(qp_b, [[128, 4]], 0, 1), writes=["qp_b"])
            P.op("pool", iota(qp_a, [[-128, 4]], 511, -1), writes=["qp_a"])
            P.op("pool", iota(dabs, [[-128, 4], [1, 512]], 0, -1), writes=["dabs"])
            P.op("pool", iota(identf, [[1, 128]], 0, -1), writes=["identf"])
            P.op("dve", TS(dneg, dabs, -1.0, None, ALU.mult), reads=["dabs"], writes=["dneg"])
            P.op("dve", TT(dabs, dabs, dneg, ALU.max), reads=["dabs", "dneg"], writes=["dabs"])
            P.op("dve", TS(identb, identf, 0.0, None, ALU.is_equal), reads=["identf"], writes=["identb"])
            for h in range(4):
                P.op("dve", TS(btb[:, h, :], dist_b, -SLOPES[h], None, ALU.mult), reads=["dist_b"], writes=[("btb", h)])
                P.op("dve", TS(bta[:, h, :], dist_a, -SLOPES[h], None, ALU.mult), reads=["dist_a"], writes=[("bta", h)])
                P.op("act", ACTF(wpb[:, h, :], qp_b, AF.Exp, scale=-SLOPES[h]), reads=["qp_b"], writes=[("wpb", h)])
                P.op("act", ACTF(wpa[:, h, :], qp_a, AF.Exp, scale=-SLOPES[h]), reads=["qp_a"], writes=[("wpa", h)])
            for sl in range(1):
                P.op("pool", lambda e, sl=sl: e.memset(vab[sl][:, :, 128:130], 1.0), writes=[("vab1", sl)])
            P.op("sp", DMA(lamv, lamv_d), writes=["lamv"], dma_key="c3")
            P.op("sp", DMA(gsubT, sublnT_d), writes=["gsubT"], dma_key="c4")
            P.op("dve", TT(lprod[:, 0, :], lamv[:, 0, :], lamv[:, 1, :], ALU.mult), reads=["lamv"], writes=["lprod0"])
            P.op("dve", TT(lprod[:, 1, :], lamv[:, 2, :], lamv[:, 3, :], ALU.mult), reads=["lamv"], writes=["lprod1"])
            P.op("dve", lambda e: e.reduce_sum(lsum, lprod, AX.X), reads=["lprod0", "lprod1"], writes=["lsum"])
            P.op("act", ACTF(lexp, lsum, AF.Exp), reads=["lsum"], writes=["lexp"])
            P.op("dve", TT(neglam, lexp[:, 1:2], lexp[:, 0:1], ALU.subtract), reads=["lexp"], writes=["neglam"])
            P.op("dve", TS(neglam, neglam, -LAMBDA_INIT, None, ALU.add), reads=["neglam"], writes=["neglam"])
            P.op("dve", TS(gsubT, gsubT, 1.0 - LAMBDA_INIT, None, ALU.mult), reads=["gsubT"], writes=["gsubT"])

            Va_v = Va.rearrange("(t p) c -> p t c", p=128)

            def load_head(h):
                sl = 0
                part = "qkv"
                if "k" in part:
                    P.op("sp", DMA(ktb[sl], KaT[h]), writes=[("ktb", sl)], dma_key=("ktb", sl))
                for half in range(2 if "v" in part else 0):
                    P.op("sp", [DMA(vab[sl][:, half * 32 + q * 8:half * 32 + (q + 1) * 8, 0:128],
                                    Va_v[:, half * 32 + q * 8:half * 32 + (q + 1) * 8, h * 128:(h + 1) * 128]) for q in range(4)],
                         writes=[("vab", sl, half)], dma_key=("vab", sl, half))
                if "q" in part:
                    P.op("sp", DMA(qtb[sl], QaT[h]), writes=[("qtb", sl)], dma_key=("qtb", sl))

            steps = []
            for h in range(4):
                for c in range(8):
                    groups = [("b", list(range(0, 4 * c))), ("d", list(range(4 * c, 4 * c + 4))), ("a", list(range(4 * c + 4, 64)))]
                    groups = [g for g in groups if g[1]]
                    for gi, (g, kts) in enumerate(groups):
                        for ki, kt in enumerate(kts):
                            steps.append(dict(h=h, c=c, g=g, kt=kt, first=(ki == 0), last=(ki == len(kts) - 1),
                                              gfirst=(gi == 0), glast=(gi == len(groups) - 1)))

            def qk(i):
                s = steps[i]
                h, c, kt = s["h"], s["c"], s["kt"]
                sl = 0
                sset = i % 2
                P.op("pe", [MM(ps[:, 2 * sset, :], ktb[sl][0:64, kt * 128:(kt + 1) * 128], qtb[sl][0:64, c * 512:(c + 1) * 512]),
                            MM(ps[:, 2 * sset + 1, :], ktb[sl][64:128, kt * 128:(kt + 1) * 128], qtb[sl][64:128, c * 512:(c + 1) * 512])],
                     reads=[("ktb", sl), ("qtb", sl)], writes=[("S", sset)])

            def accv(qs):
                return ps[:, 4 + qs, :].rearrange("p (m c) -> p m c", m=2)[:, :, 0:129]

            def softmax_av(i):
                s = steps[i]
                h, c, kt, g = s["h"], s["c"], s["kt"], s["g"]
                sl = 0
                sset = i % 2
                pslot = i % 3
                pt = pTd[pslot]
                S2 = ps[:, 2 * sset:2 * sset + 2, :]
                if g == "b":
                    m = 4 * c - kt
                    P.op("act", ACTF(pt, S2, AF.Exp, bias=btb[:, h, m:m + 1]), reads=[("S", sset), ("btb", h)], writes=[("pTd", pslot)])
                elif g == "a":
                    m = kt - 4 * c
                    P.op("act", ACTF(pt, S2, AF.Exp, bias=bta[:, h, m:m + 1]), reads=[("S", sset), ("bta", h)], writes=[("pTd", pslot)])
                else:
                    j = kt - 4 * c
                    for mp in range(2):
                        P.op("dve", STT(tmpd[:, mp, :], dabs[:, j, :], -SLOPES[h], ps[:, 2 * sset + mp, :], ALU.mult, ALU.add),
                             reads=[("S", sset), "dabs"], writes=[("tmpd", mp)])
                    P.op("act", ACTF(pt, tmpd, AF.Exp), reads=[("tmpd", 0), ("tmpd", 1)], writes=[("pTd", pslot)])
                first, last = s["first"], s["last"]
                vh = ("vab", sl, kt // 32)
                fns = []
                for qs in range(4):
                    for m in range(2):
                        fns.append(lambda e, qs=qs, m=m, pt=pt, sl=sl, kt=kt, first=first, last=last: e.matmul(
                            ps[:, 4 + qs, m * 256:m * 256 + 129], lhsT=pt[:, m, qs * 128:(qs + 1) * 128], rhs=vab[sl][:, kt, 0:129],
                            start=(first and m == 0), stop=last, skip_group_check=True))
                P.op("pe", fns, reads=[("pTd", pslot), vh, ("vab1", sl)], writes=["acc"])
                if last:
                    for qs in range(4):
                        ak = ("accs", qs)
                        av = accv(qs)
                        dst = accs[:, qs, :, :]
                        if s["gfirst"]:
                            if g == "b":
                                P.op("dve", TS(dst, av, wpb[:, h, qs:qs + 1], None, ALU.mult), reads=["acc", ("wpb", h)], writes=[ak])
                            elif g == "d":
                                P.op("dve", COPY(dst, av), reads=["acc"], writes=[ak])
                            else:
                                P.op("dve", TS(dst, av, wpa[:, h, qs:qs + 1], None, ALU.mult), reads=["acc", ("wpa", h)], writes=[ak])
                        else:
                            if g == "d":
                                P.op("dve", TT(dst, av, dst, ALU.add), reads=["acc", ak], writes=[ak])
                            else:
                                P.op("dve", STT(dst, av, wpa[:, h, qs:qs + 1], dst, ALU.mult, ALU.add), reads=["acc", ("wpa", h), ak], writes=[ak])
                    if s["glast"] and DA_LVL >= 2:
                        finalize(h, c, i)

            fin = {"n": 0}
            pending = []
            allacc = [("accs", qs) for qs in range(4)]

            def finalize(h, c, i):
                ysl = fin["n"] % 2
                fin["n"] += 1
                P.op("dve", RECIP(rr, accs[:, :, :, 128]), reads=allacc, writes=["rr"])
                P.op("dve", TS(nl, rr[:, :, 1], neglam[:, 0:1], None, ALU.mult), reads=["rr", "neglam"], writes=["nl"])
                for qs in range(4):
                    P.op("dve", TS(t1[:, qs, :], accs[:, qs, 0, 0:128], rr[:, qs, 0:1], None, ALU.mult), reads=allacc + ["rr"], writes=[("t1", qs)])
                    P.op("dve", STT(o_sb[:, qs, :], accs[:, qs, 1, 0:128], nl[:, qs:qs + 1], t1[:, qs, :], ALU.mult, ALU.add),
                         reads=allacc + ["nl", ("t1", qs)], writes=[("o", qs)])
                    P.op("dve", lambda e, qs=qs: e.scalar_tensor_tensor(junk, o_sb[:, qs, :], 1.0, o_sb[:, qs, :], ALU.mult, ALU.mult,
                                                                         accum_out=ssq[:, qs:qs + 1]),
                         reads=[("o", qs)], writes=["junk", ("ssq", qs)])
                P.op("dve", TS(rs, ssq, 1.0 / 128, EPS, ALU.mult, ALU.add), reads=[("ssq", qs) for qs in range(4)], writes=["rs"])

                def stage_b(k):
                    P.op("act", ACTF(rs, rs, AF.Ln), reads=["rs"], writes=["rs"])
                    P.op("act", ACTF(rs, rs, AF.Exp, scale=-0.5), reads=["rs"], writes=["rs"])
                    for qs in range(4):
                        P.op("dve", STT(yq[:, qs, :], o_sb[:, qs, :], rs[:, qs:qs + 1], gsubT, ALU.mult, ALU.mult),
                             reads=[("o", qs), "rs", "gsubT"], writes=[("yq", qs)])

                def stage_c(k):
                    sset = k % 2
                    psb = ps[:, 2 * sset, 0:256].bitcast(BF16)
                    P.op("pe", [lambda e, qs=qs, psb=psb: e.transpose(psb[:, qs * 128:(qs + 1) * 128], yq[:, qs, :], identb) for qs in range(4)],
                         reads=[("yq", qs) for qs in range(4)] + ["identb"], writes=[("S", sset)])
                    P.op("dve", COPY(ydst[ysl], psb), reads=[("S", sset)], writes=[("ydst", ysl)])
                    P.op("sp", DMA(yaT[h, :, c * 512:(c + 1) * 512], ydst[ysl]), reads=[("ydst", ysl)], dma_key=("ydst", ysl))

                if DA_LVL >= 3:
                    pending.append((i + 3, stage_b))
                if DA_LVL >= 4:
                    pending.append((i + 6, stage_c))

            import os
            load_head(0)
            n = min(len(steps), int(os.environ.get("DA_STEPS", "100000")))
            DA_LVL = int(os.environ.get("DA_LVL", "9"))
            qk(0)
            for i in range(n):
                boundary = (i + 1 < n and steps[i + 1]["h"] != steps[i]["h"])
                if i + 1 < n and not boundary:
                    qk(i + 1)
                softmax_av(i)
                if boundary:
                    load_head(steps[i + 1]["h"])
                    qk(i + 1)
                while pending and pending[0][0] <= i:
                    pending.pop(0)[1](i)
            while pending:
                pending.pop(0)[1](n - 1)
            P.barrier()
            P.emit()
            if stop_after == 3:
                return nc

        with contextlib.ExitStack() as E4:
            wg = sb(E4, "wg", [128, 8, 2048], BF16)
            wa = sb(E4, "wa", [128, 4, 1024], BF16)
            wb = sb(E4, "wb", [64, 8, 1024], BF16)
            wo = sb(E4, "wo", [128, 8, 1024], BF16)
            xts = [sb(E4, "xta%d" % i, [128, 8, 512], F32) for i in range(2)]
            sq = sb(E4, "sq4", [128, 8, 512], BF16)
            hT = sb(E4, "hT4", [128, 8, 512], BF16)
            rt = sb(E4, "rt4", [128, 512], F32)
            rstd = sb(E4, "rstd4", [128, 512], F32)
            yat = sb(E4, "yat", [128, 4, 512], BF16)
            ybt = sb(E4, "ybt", [64, 8, 512], BF16)
            ga = sb(E4, "ga", [128, 512], F32)
            gb = sb(E4, "gb", [128, 512], F32)
            m1 = sb(E4, "m1", [128, 512], F32)
            m2 = sb(E4, "m2", [128, 512], F32)
            mg = sb(E4, "mg", [128, 8, 512], BF16)
            mo = sb(E4, "mo", [128, 8, 512], F32)
            w_in_v = w_in.rearrange("(kc p) n -> p kc n", p=128)
            cast_load_all([(wg[:, kc, :], w_in_v[:, kc, 3072:5120], 2048) for kc in range(8)], "wg")
            w_a_v = w_a.rearrange("(h p) n -> p h n", p=128)
            w_b_v = w_b.rearrange("(h p) n -> p h n", p=64)
            w_o_v = w_o.rearrange("(kc p) n -> p kc n", p=128)
            cast_load_all([(wa[:, h, :], w_a_v[:, h, :], 1024) for h in range(4)], "wa")
            cast_load_all([(wb[:, h, :], w_b_v[:, h, :], 1024) for h in range(8)], "wb")
            cast_load_all([(wo[:, kc, :], w_o_v[:, kc, :], 1024) for kc in range(8)], "wo")
            xT_v = xT.rearrange("(kc p) t -> p kc t", p=128)
            x1T_v = x1T.rearrange("(kc p) t -> p kc t", p=128)
            yaT_v = yaT.rearrange("h p t -> p h t")
            ybT_v2 = ybT.rearrange("h d q -> d h q")
            wgk = ["wg"]

            def load_x3(t):
                P.op("sp", DMA(xts[t % 2], xT_v[:, :, t * 512:(t + 1) * 512]), writes=[xt_key(t % 2)], dma_key=("xta", t % 2))

            load_x3(0)
            ob_i = 0
            for t in range(8):
                tok = slice(t * 512, (t + 1) * 512)
                xt = xts[t % 2]
                P.op("sp", DMA(yat, yaT_v[:, :, tok]), writes=["yat"], dma_key="yat")
                P.op("sp", DMA(ybt, ybT_v2[:, :, tok]), writes=["ybt"], dma_key="ybt")
                if t + 1 < 8:
                    load_x3(t + 1)
                rmsnorm_in(xt, 0, hT, sq, rt, rstd, 512, t % 2, 0)
                for oc in range(8):
                    ocs = slice(oc * 128, (oc + 1) * 128)
                    P.op("pe", [MM(ps[:, 1, :], wa[:, h, ocs], yat[:, h, :], start=(h == 0), stop=(h == 3)) for h in range(4)],
                         reads=["yat", "wa"], writes=[("ps", 1)])
                    P.op("pe", [MM(ps[:, 2, :], wb[:, h, ocs], ybt[:, h, :], start=(h == 0), stop=(h == 7)) for h in range(8)],
                         reads=["ybt", "wb"], writes=[("ps", 2)])
                    P.op("pe", [MM(ps[:, 3, :], wg[:, kc, oc * 128:(oc + 1) * 128], hT[:, kc, :], start=(kc == 0), stop=(kc == 7))
                                for kc in range(8)], reads=[("hT", kc) for kc in range(8)] + wgk, writes=[("ps", 3)])
                    P.op("pe", [MM(ps[:, 4, :], wg[:, kc, 1024 + oc * 128:1024 + (oc + 1) * 128], hT[:, kc, :], start=(kc == 0), stop=(kc == 7))
                                for kc in range(8)], reads=[("hT", kc) for kc in range(8)] + wgk, writes=[("ps", 4)])
                    P.op("act", ACTF(ga, ps[:, 3, :], AF.Sigmoid, bias=bgate[:, oc:oc + 1]), reads=[("ps", 3), "bgate"], writes=["ga"])
                    P.op("act", ACTF(gb, ps[:, 4, :], AF.Sigmoid, bias=bgate[:, 8 + oc:9 + oc]), reads=[("ps", 4), "bgate"], writes=["gb"])
                    P.op("dve", TT(m1, ps[:, 1, :], ga, ALU.mult), reads=[("ps", 1), "ga"], writes=["m1"])
                    P.op("dve", TT(m2, ps[:, 2, :], gb, ALU.mult), reads=[("ps", 2), "gb"], writes=["m2"])
                    P.op("pool", TT(mg[:, oc, :], m1, m2, ALU.add), reads=["m1", "m2"], writes=[("mg", oc)])
                for oc2 in range(8):
                    bank = 5 + (ob_i % 2)
                    ob_i += 1
                    P.op("pe", [MM(ps[:, bank, :], wo[:, oc, oc2 * 128:(oc2 + 1) * 128], mg[:, oc, :], start=(oc == 0), stop=(oc == 7))
                                for oc in range(8)], reads=[("mg", oc) for oc in range(8)] + ["wo"],
                         writes=[("ps", bank)])
                    P.op("act", ACTF(mo[:, oc2, :], ps[:, bank, :], AF.Copy), reads=[("ps", bank)], writes=[("mo", oc2)])
                    P.op("dve", TT(sq[:, oc2, :], mo[:, oc2, :], mo[:, oc2, :], ALU.mult), reads=[("mo", oc2)], writes=[("sq", oc2)])
                P.op("pe", [MM(ps[:, 7, :], ones, sq[:, oc2, :], start=(oc2 == 0), stop=(oc2 == 7)) for oc2 in range(8)],
                     reads=[("sq", k) for k in range(8)] + ["ones"], writes=[("ps", 7)])
                P.op("dve", TS(rt, ps[:, 7, :], 1.0 / D, EPS, ALU.mult, ALU.add), reads=[("ps", 7)], writes=["rt"])
                P.op("act", ACTF(rt, rt, AF.Ln), reads=["rt"], writes=["rt"])
                P.op("act", ACTF(rstd, rt, AF.Exp, scale=-0.5), reads=["rt"], writes=["rstd"])
                for oc2 in range(8):
                    P.op("dve", STT(mo[:, oc2, :], mo[:, oc2, :], gains[:, 1, oc2:oc2 + 1], rstd, ALU.mult, ALU.mult),
                         reads=[("mo", oc2), "rstd", "gains"], writes=[("mo", oc2)])
                    P.op("pool", TT(xt[:, oc2, :], xt[:, oc2, :], mo[:, oc2, :], ALU.add), reads=[("mo", oc2), xt_key(t % 2)],
                         writes=[xt_key(t % 2)])
                P.op("sp", DMA(x1T_v[:, :, tok], xt), reads=[xt_key(t % 2)], dma_key=("x1o", t % 2))
            P.barrier()
            P.emit()
            if stop_after == 4:
                return nc

        NT = 256
        with contextlib.ExitStack() as E5:
            wgt = sb(E5, "wgt", [128, 8, DFF], BF16)
            wup = sb(E5, "wup", [128, 8, DFF], BF16)
            wdn = sb(E5, "wdn", [128, NJ, 1024], BF16)
            xts = [sb(E5, "xtb%d" % i, [128, 8, NT], F32) for i in range(2)]
            sq = sb(E5, "sq5", [128, 8, NT], BF16)
            hT = sb(E5, "hT5", [128, 8, NT], BF16)
            rt = sb(E5, "rt5", [128, NT], F32)
            rstd = sb(E5, "rstd5", [128, NT], F32)
            sg = [sb(E5, "sg%d" % i, [128, NT], F32) for i in range(2)]
            aT = sb(E5, "aT", [128, NJ, NT], BF16)
            fo = sb(E5, "fo", [128, 8, NT], F32)
            wg_v = w_gate.rearrange("(kc p) n -> p kc n", p=128)
            wu_v = w_up.rearrange("(kc p) n -> p kc n", p=128)
            wd_v = w_down.rearrange("(j p) n -> p j n", p=128)
            cast_load_all([(wgt[:, kc, :], wg_v[:, kc, :], DFF) for kc in range(8)], "wgt")
            cast_load_all([(wup[:, kc, :], wu_v[:, kc, :], DFF) for kc in range(8)], "wup")
            cast_load_all([(wdn[:, j, :], wd_v[:, j, :], 1024) for j in range(NJ)], "wdn")
            x1T_v = x1T.rearrange("(kc p) t -> p kc t", p=128)
            outT_v = outT.rearrange("(kc p) t -> p kc t", p=128)
            NTL = HALF // NT

            def load_x4(t):
                P.op("sp", DMA(xts[t % 2], x1T_v[:, :, t * NT:(t + 1) * NT]), writes=[xt_key(t % 2)], dma_key=("xtb", t % 2))

            def pst(idx):
                return ps[:, idx, 0:NT]

            load_x4(0)
            gi = 0
            di = 0
            wk_g = ["wgt"]
            wk_u = ["wup"]
            for t in range(NTL):
                xt = xts[t % 2]
                if t + 1 < NTL:
                    load_x4(t + 1)
                rmsnorm_in(xt, 2, hT, sq, rt, rstd, NT, t % 2, 0)
                for j in range(NJ):
                    gslot = 1 + (gi % 2)
                    uslot = 3 + (gi % 2)
                    gi += 1
                    js = slice(j * 128, (j + 1) * 128)
                    P.op("pe", [MM(pst(gslot), wgt[:, kc, js], hT[:, kc, :], start=(kc == 0), stop=(kc == 7)) for kc in range(8)],
                         reads=[("hT", kc) for kc in range(8)] + wk_g, writes=[("pt", gslot)])
                    P.op("pe", [MM(pst(uslot), wup[:, kc, js], hT[:, kc, :], start=(kc == 0), stop=(kc == 7)) for kc in range(8)],
                         reads=[("hT", kc) for kc in range(8)] + wk_u, writes=[("pt", uslot)])
                    sgs = sg[j % 2]
                    P.op("act", ACTF(sgs, pst(gslot), AF.Silu), reads=[("pt", gslot)], writes=[("sg", j % 2)])
                    P.op("dve", TT(aT[:, j, :], pst(uslot), sgs, ALU.mult), reads=[("pt", uslot), ("sg", j % 2)],
                         writes=[("aT", j)])
                for oc in range(8):
                    dslot = 5 + (di % 2)
                    di += 1
                    P.op("pe", [MM(pst(dslot), wdn[:, j, oc * 128:(oc + 1) * 128], aT[:, j, :], start=(j == 0), stop=(j == NJ - 1))
                                for j in range(NJ)], reads=[("aT", j) for j in range(NJ)] + ["wdn"],
                         writes=[("pt", dslot)])
                    P.op("act", ACTF(fo[:, oc, :], pst(dslot), AF.Copy), reads=[("pt", dslot)], writes=[("fo", oc)])
                    P.op("dve", TT(sq[:, oc, :], fo[:, oc, :], fo[:, oc, :], ALU.mult), reads=[("fo", oc)], writes=[("sq", oc)])
                P.op("pe", [MM(ps[:, 7, :NT], ones, sq[:, oc, :], start=(oc == 0), stop=(oc == 7)) for oc in range(8)],
                     reads=[("sq", k) for k in range(8)] + ["ones"], writes=[("ps", 7)])
                P.op("dve", TS(rt, ps[:, 7, :NT], 1.0 / D, EPS, ALU.mult, ALU.add), reads=[("ps", 7)], writes=["rt"])
                P.op("act", ACTF(rt, rt, AF.Ln), reads=["rt"], writes=["rt"])
                P.op("act", ACTF(rstd, rt, AF.Exp, scale=-0.5), reads=["rt"], writes=["rstd"])
                for oc in range(8):
                    P.op("dve", STT(fo[:, oc, :], fo[:, oc, :], gains[:, 3, oc:oc + 1], rstd, ALU.mult, ALU.mult),
                         reads=[("fo", oc), "rstd", "gains"], writes=[("fo", oc)])
                    P.op("pool", TT(xt[:, oc, :], xt[:, oc, :], fo[:, oc, :], ALU.add), reads=[("fo", oc), xt_key(t % 2)],
                         writes=[xt_key(t % 2)])
                P.op("sp", DMA(outT_v[:, :, t * NT:(t + 1) * NT], xt), reads=[xt_key(t % 2)], dma_key=("outo", t % 2))
            P.barrier()
            P.emit()
            if stop_after == 5:
                return nc
    return nc


def _na_bias_tables(rpb, hf):
    out = np.full((3, 5, 128, 8, 2, 64), NEG, dtype=np.float32)
    lc = np.arange(64)
    qc = lc if hf == 0 else 63 - lc
    kc = qc
    cstart = np.clip(qc - 8, 0, 48)
    col_in = (kc[:, None] >= cstart[None, :]) & (kc[:, None] < cstart[None, :] + 16)
    col_off = np.clip(kc[:, None] - qc[None, :] + 15, 0, 30)
    for cls, lr0 in enumerate([0, 2, 4]):
        t0 = max(lr0 - 4, 0)
        for rr in range(2):
            lr = lr0 + rr
            r = lr if hf == 0 else 127 - lr
            rs = min(max(r - 4, 0), 120)
            for j in range(5):
                for i in range(2):
                    krow_l = t0 + 2 * j + i
                    g = krow_l if hf == 0 else 127 - krow_l
                    if not (rs <= g < rs + 8):
                        continue
                    row_off = g - r + 7
                    vals = rpb[:, row_off, :][:, col_off]
                    vals = np.transpose(vals, (1, 0, 2))
                    vals = vals[:, [0, 2, 4, 6, 1, 3, 5, 7], :]
                    blk = np.where(col_in[:, None, :], vals, np.float32(NEG))
                    out[cls, j, i * 64:(i + 1) * 64, :, rr, :] = blk
    return out.reshape(3, 5, 128, 1024)


_CACHE = {}


def kernel(x, pre_mix_w, w_in, b_gate, lambda_q1, lambda_k1, lambda_q2, lambda_k2,
           subln_w, rpb, w_branch_a, w_branch_b, w_out, post_mix_w, pre_ffn_w,
           w_gate, w_up, w_down, post_ffn_w):
    f32 = np.float32
    x = np.asarray(x, dtype=f32)
    if "nc" not in _CACHE:
        _CACHE["nc"] = build_program()
    nc = _CACHE["nc"]

    def vec8(v):
        return np.asarray(v, f32).reshape(8, 128).T

    gains = np.ascontiguousarray(np.stack([vec8(pre_mix_w[0]), vec8(post_mix_w[0]), vec8(pre_ffn_w[0]), vec8(post_ffn_w[0])], axis=1))
    bgate = np.ascontiguousarray(np.asarray(b_gate[0], f32).reshape(16, 128).T)
    subln = np.ascontiguousarray(np.asarray(subln_w[0], f32).reshape(128, 1))
    lamv = np.stack([np.asarray(v[0], f32) for v in (lambda_q1, lambda_k1, lambda_q2, lambda_k2)], axis=0)
    lamv = np.ascontiguousarray(np.broadcast_to(lamv[None], (128, 4, 64)))
    rpb0 = np.asarray(rpb[0], f32)
    nab = [_na_bias_tables(rpb0, 0), _na_bias_tables(rpb0, 1)]
    common = {
        "w_in": np.ascontiguousarray(np.asarray(w_in[0], f32)),
        "w_a": np.ascontiguousarray(np.asarray(w_branch_a[0], f32)),
        "w_b": np.ascontiguousarray(np.asarray(w_branch_b[0], f32)),
        "w_o": np.ascontiguousarray(np.asarray(w_out[0], f32)),
        "w_gate": np.ascontiguousarray(np.asarray(w_gate[0], f32)),
        "w_up": np.ascontiguousarray(np.asarray(w_up[0], f32)),
        "w_down": np.ascontiguousarray(np.asarray(w_down[0], f32)),
        "gains": gains, "bgate": bgate, "subln": subln, "lamv": lamv,
        "sublnT": np.ascontiguousarray(np.broadcast_to(np.asarray(subln_w[0], f32)[None, :], (128, 128))),
    }
    in_maps = []
    for core in range(8):
        b, hf = core // 2, core % 2
        xb = x[b] if hf == 0 else x[b, ::-1]
        m = dict(common)
        m["xT"] = np.ascontiguousarray(xb.T)
        m["nab"] = nab[hf]
        in_maps.append(m)
    res = run_bass_kernel_spmd(nc, in_maps, core_ids=list(range(8)))
    out = np.empty((B, S, D), dtype=f32)
    for core in range(8):
        b, hf = core // 2, core % 2
        o = res.results[core]["outT"].T
        if hf == 0:
            out[b, :HALF] = o
        else:
            out[b, HALF:] = o[::-1]
    return out
```

```python
import contextlib
import math
import numpy as np
import concourse.bass as bass
import concourse.mybir as mybir
from concourse.bass_utils import run_bass_kernel_spmd

F32 = mybir.dt.float32
BF16 = mybir.dt.bfloat16
AF = mybir.ActivationFunctionType
ALU = mybir.AluOpType
AX = mybir.AxisListType

D = 1024
S = 8192
HALF = 4096
B = 4
DFF = 2816
NJ = DFF // 128
INC = 5120
EPS = 1e-6
NEG = -1e30
SLOPES = [2.0 ** (-8.0 * (i + 1) / 4) for i in range(4)]
LAMBDA_INIT = 0.8 - 0.6 * math.exp(-0.3 * 0)
NA_LOC_TOK = 4608

ENGS = ("pe", "act", "dve", "pool", "sp")
SAME_ENGINE_SYNC = ("act", "dve", "pool")


class Prog:
    def __init__(self, nc):
        self.nc = nc
        self.ops = {e: [] for e in ENGS}
        self.eng_cnt = {e: 0 for e in ENGS}
        self.dma_cnt = {}
        self.waited = {e: {} for e in ENGS}
        self.res = {}
        self.sems = {}
        self.nins = 0

    def _sem(self, key):
        if key not in self.sems:
            self.sems[key] = self.nc.alloc_semaphore(name="s%d" % len(self.sems))
        return self.sems[key]

    def op(self, eng, fns, reads=(), writes=(), dma_key=None):
        if callable(fns):
            fns = [fns]
        fns = list(fns)
        need = {}

        def want(t):
            if t is None:
                return
            k, v = t
            if k == ("eng", eng) and eng not in SAME_ENGINE_SYNC:
                return
            if self.waited[eng].get(k, 0) >= v:
                return
            if need.get(k, 0) < v:
                need[k] = v

        for r in reads:
            st = self.res.get(r)
            if st:
                want(st[0])
        for w in writes:
            st = self.res.get(w)
            if st:
                want(st[0])
                for t in st[1]:
                    want(t)
        for k, v in need.items():
            self.waited[eng][k] = v
        if dma_key is not None:
            k = ("dma", dma_key)
            self.dma_cnt[k] = self.dma_cnt.get(k, 0) + 16 * len(fns)
            ticket = (k, self.dma_cnt[k])
            mode = "dma"
        else:
            k = ("eng", eng)
            self.eng_cnt[eng] += 1
            ticket = (k, self.eng_cnt[eng])
            mode = "cmp"
        self._sem(k)
        for kk in need:
            self._sem(kk)
        for r in reads:
            st = self.res.setdefault(r, [None, []])
            st[1].append(ticket)
        for w in writes:
            self.res[w] = [ticket, []]
        self.ops[eng].append((sorted(need.items(), key=str), fns, k, mode))
        self.nins += len(fns)
        return ticket

    def barrier(self):
        allt = []
        for e in ENGS:
            if self.eng_cnt[e]:
                allt.append((("eng", e), self.eng_cnt[e]))
        for k, v in self.dma_cnt.items():
            allt.append((k, v))
        self.res = {"__bar__": [None, list(allt)]}
        self.op("sp", lambda e: e.nop(), writes=["__bar__"])
        for e in ENGS:
            if e != "sp":
                self.op(e, lambda en: en.nop(), reads=["__bar__"])
        self.res = {}

    def _emit_engine(self, name, eng):
        for waits, fns, k, mode in self.ops[name]:
            for sk, v in waits:
                eng.wait_ge(self.sems[sk], v)
            n = len(fns)
            for i, fn in enumerate(fns):
                ins = fn(eng)
                if mode == "dma":
                    ins.then_inc(self.sems[k], 16)
                elif i == n - 1:
                    ins.then_inc(self.sems[k], 1)
        self.ops[name] = []

    def emit(self):
        with self.nc.Block() as block:
            @block.tensor
            def _(e):
                self._emit_engine("pe", e)

            @block.scalar
            def _(e):
                self._emit_engine("act", e)

            @block.vector
            def _(e):
                self._emit_engine("dve", e)

            @block.gpsimd
            def _(e):
                self._emit_engine("pool", e)

            @block.sync
            def _(e):
                self._emit_engine("sp", e)


def MM(out, lhsT, rhs, start=True, stop=True):
    return lambda e: e.matmul(out, lhsT=lhsT, rhs=rhs, start=start, stop=stop)


def DMA(out, in_):
    return lambda e: e.dma_start(out=out, in_=in_)


def TT(out, a, b, op):
    return lambda e: e.tensor_tensor(out, a, b, op)


def TS(out, a, s1, s2, op0, op1=None):
    if op1 is None:
        return lambda e: e.tensor_scalar(out, a, s1, None, op0)
    return lambda e: e.tensor_scalar(out, a, s1, s2, op0, op1)


def STT(out, in0, scalar, in1, op0, op1):
    return lambda e: e.scalar_tensor_tensor(out, in0, scalar, in1, op0, op1)


def ACTF(out, in_, func, bias=None, scale=1.0):
    if bias is None:
        return lambda e: e.activation(out, in_, func, scale=scale)
    return lambda e: e.activation(out, in_, func, bias=bias, scale=scale)


def COPY(out, in_):
    return lambda e: e.tensor_copy(out, in_)


def RECIP(out, in_):
    return lambda e: e.reciprocal(out, in_)


def build_program(stop_after=99, dbg=False):
    nc = bass.Bass("TRN2", target_bir_lowering=False)

    def din(name, shape, dt=F32):
        return nc.dram_tensor(name, list(shape), dt, kind="ExternalInput").ap()

    def dscr(name, shape, dt):
        return nc.dram_tensor(name, list(shape), dt, kind=("ExternalOutput" if dbg else "Internal")).ap()

    xT = din("xT", [D, S])
    w_in = din("w_in", [D, INC])
    w_a = din("w_a", [512, D])
    w_b = din("w_b", [512, D])
    w_o = din("w_o", [D, D])
    w_gate = din("w_gate", [D, DFF])
    w_up = din("w_up", [D, DFF])
    w_down = din("w_down", [DFF, D])
    gains_d = din("gains", [128, 4, 8])
    bgate_d = din("bgate", [128, 16])
    subln_d = din("subln", [128, 1])
    lamv_d = din("lamv", [128, 4, 64])
    sublnT_d = din("sublnT", [128, 128])
    nab_d = din("nab", [3, 5, 128, 1024])
    outT = nc.dram_tensor("outT", [D, HALF], F32, kind="ExternalOutput").ap()

    QaT = dscr("QaT", [4, 128, HALF], BF16)
    KaT = dscr("KaT", [4, 128, S], BF16)
    Va = dscr("Va", [S, 512], BF16)
    QbT = dscr("QbT", [4, 128, HALF], BF16)
    KbT = dscr("KbT", [4, 128, NA_LOC_TOK], BF16)
    Vb = dscr("Vb", [NA_LOC_TOK, 512], BF16)
    yaT = dscr("yaT", [4, 128, HALF], BF16)
    ybT = dscr("ybT", [8, 64, HALF], BF16)
    x1T = dscr("x1T", [D, HALF], F32)

    P = Prog(nc)
    ps = nc.alloc_psum_tensor("ps", [128, 8, 512], F32).ap()

    def sb(es, name, shape, dt):
        return es.enter_context(nc.sbuf_tensor(name, list(shape), dt)).ap()

    with contextlib.ExitStack() as G:
        ones = sb(G, "ones", [128, 128], BF16)
        neghalf = sb(G, "neghalf", [128, 512], F32)
        gains = sb(G, "gains_sb", [128, 4, 8], F32)
        bgate = sb(G, "bgate_sb", [128, 16], F32)
        subln = sb(G, "subln_sb", [128, 1], F32)
        gsub = sb(G, "gsub", [128, 1], F32)
        neglam = sb(G, "neglam", [128, 1], F32)

        P.op("pool", lambda e: e.memset(ones, 1.0), writes=["ones"])
        P.op("pool", lambda e: e.memset(neghalf, -0.5), writes=["neghalf"])
        P.op("sp", DMA(gains, gains_d), writes=["gains"], dma_key="c0")
        P.op("sp", DMA(bgate, bgate_d), writes=["bgate"], dma_key="c1")
        P.op("sp", DMA(subln, subln_d), writes=["subln"], dma_key="c2")

        def cast_fns(dst, src, ncols):
            fns = []
            for c0 in range(0, ncols, 2048):
                c1 = min(ncols, c0 + 2048)
                fns.append(DMA(dst[:, c0:c1], src[:, c0:c1]))
            return fns

        def cast_load_all(pairs, key):
            fns = []
            for dst, src, ncols in pairs:
                fns += cast_fns(dst, src, ncols)
            P.op("pool", fns, writes=[key], dma_key=key)

        def rmsnorm_in(xt, gidx, hT, sq, rt, rstd, N, tag, ss_bank, hp=0):
            P.op("dve", TT(sq[:, :, :N], xt[:, :, :N], xt[:, :, :N], ALU.mult), reads=[xt_key(tag)], writes=[("sq", k) for k in range(8)])
            P.op("pe", [MM(ps[:, ss_bank, :N], ones, sq[:, kc, :N], start=(kc == 0), stop=(kc == 7)) for kc in range(8)],
                 reads=[("sq", k) for k in range(8)] + ["ones"], writes=[("ps", ss_bank)])
            P.op("dve", TS(rt[:, :N], ps[:, ss_bank, :N], 1.0 / D, EPS, ALU.mult, ALU.add),
                 reads=[("ps", ss_bank)], writes=["rt"])
            P.op("act", ACTF(rt[:, :N], rt[:, :N], AF.Ln), reads=["rt"], writes=["rt"])
            P.op("act", ACTF(rstd[:, :N], rt[:, :N], AF.Exp, scale=-0.5), reads=["rt"], writes=["rstd"])
            for kc in range(8):
                P.op("dve", STT(hT[:, kc, :N], xt[:, kc, :N], gains[:, gidx, kc:kc + 1], rstd[:, :N], ALU.mult, ALU.mult),
                     reads=[xt_key(tag), "rstd", "gains"], writes=[("hT", hp, kc)])

        def xt_key(tag):
            return ("xt", tag)

        with contextlib.ExitStack() as E1:
            win = sb(E1, "win", [128, 8, 3072], BF16)
            xts = [sb(E1, "xt%d" % i, [128, 8, 512], F32) for i in range(2)]
            sq = sb(E1, "sq1", [128, 8, 512], BF16)
            hTs = [sb(E1, "hT1_%d" % i, [128, 8, 512], BF16) for i in range(2)]
            cur = {"hT": hTs[0], "hp": 0}
            rt = sb(E1, "rt1", [128, 512], F32)
            rstd = sb(E1, "rstd1", [128, 512], F32)
            NSTG = 6
            stg = [sb(E1, "stg%d" % i, [128, 512], BF16) for i in range(NSTG)]
            w_in_v = w_in.rearrange("(kc p) n -> p kc n", p=128)
            cast_load_all([(win[:, kc, :], w_in_v[:, kc, 0:3072], 3072) for kc in range(8)], "win")
            xT_v = xT.rearrange("(kc p) t -> p kc t", p=128)
            banks = [1, 2, 3, 4, 5, 6, 7]
            st = {"b": 0, "s": 0, "ev": 0}

            def load_x(t):
                P.op("sp", DMA(xts[t % 2], xT_v[:, :, t * 512:(t + 1) * 512]), writes=[xt_key(t % 2)], dma_key=("xt", t % 2))

            def evac_store(bank, dst, scale=None):
                k = st["s"] % NSTG
                st["s"] += 1
                eng = "act" if st["ev"] % 2 == 0 else "dve"
                st["ev"] += 1
                if eng == "act":
                    fn = ACTF(stg[k], ps[:, bank, :], AF.Copy, scale=(1.0 if scale is None else scale))
                else:
                    if scale is None:
                        fn = COPY(stg[k], ps[:, bank, :])
                    else:
                        fn = TS(stg[k], ps[:, bank, :], scale, None, ALU.mult)
                P.op(eng, fn, reads=[("ps", bank)], writes=[("stg", k)])
                P.op("sp", DMA(dst, stg[k]), reads=[("stg", k)], dma_key=("stg", k))

            def fm_chunk(col0, dst, scale=None):
                bank = banks[st["b"] % len(banks)]
                st["b"] += 1
                hT, hp = cur["hT"], cur["hp"]
                P.op("pe", [MM(ps[:, bank, :], win[:, kc, col0:col0 + 128], hT[:, kc, :], start=(kc == 0), stop=(kc == 7))
                            for kc in range(8)],
                     reads=[("hT", hp, kc) for kc in range(8)] + ["win"], writes=[("ps", bank)])
                evac_store(bank, dst, scale)

            def tm_chunk(col0, sub, dst):
                bank = banks[st["b"] % len(banks)]
                st["b"] += 1
                hT, hp = cur["hT"], cur["hp"]
                P.op("pe", [MM(ps[:, bank, :], hT[:, kc, sub * 128:(sub + 1) * 128], win[:, kc, col0:col0 + 512],
                               start=(kc == 0), stop=(kc == 7)) for kc in range(8)],
                     reads=[("hT", hp, kc) for kc in range(8)] + ["win"], writes=[("ps", bank)])
                evac_store(bank, dst)

            load_x(0)
            load_x(1)
            rmsnorm_in(xts[0], 0, hTs[0], sq, rt, rstd, 512, 0, 0, hp=0)
            for t in range(16):
                tok = slice(t * 512, (t + 1) * 512)
                mine = t < 8
                nb = t <= 8
                work = []
                for h in range(4):
                    if mine:
                        work.append(lambda h=h, tok=tok: fm_chunk(h * 128, QaT[h, :, tok], scale=0.125))
                    work.append(lambda h=h, tok=tok: fm_chunk(512 + h * 128, KaT[h, :, tok]))
                for sub in range(4):
                    work.append(lambda sub=sub, t=t: tm_chunk(1024, sub, Va[t * 512 + sub * 128: t * 512 + (sub + 1) * 128, :]))
                if nb:
                    for hp_ in range(4):
                        if mine:
                            work.append(lambda hp_=hp_, tok=tok: fm_chunk(1536 + hp_ * 128, QbT[hp_, :, tok], scale=0.125))
                        work.append(lambda hp_=hp_, tok=tok: fm_chunk(2048 + hp_ * 128, KbT[hp_, :, tok]))
                    for sub in range(4):
                        work.append(lambda sub=sub, t=t: tm_chunk(2560, sub, Vb[t * 512 + sub * 128: t * 512 + (sub + 1) * 128, :]))
                cur["hT"], cur["hp"] = hTs[t % 2], t % 2
                half = len(work) // 2
                for w in work[:half]:
                    w()
                if t + 1 < 16:
                    rmsnorm_in(xts[(t + 1) % 2], 0, hTs[(t + 1) % 2], sq, rt, rstd, 512, (t + 1) % 2, 0, hp=(t + 1) % 2)
                for w in work[half:]:
                    w()
                if t + 2 < 16:
                    load_x(t + 2)
            P.barrier()
            P.emit()
            if stop_after == 1:
                return nc

        with contextlib.ExitStack() as E2:
            kb = sb(E2, "kb", [128, 4, NA_LOC_TOK], BF16)
            vb = sb(E2, "vb", [128, NA_LOC_TOK // 128, 512], BF16)
            qb = sb(E2, "qb", [128, 4, HALF], BF16)
            nbias = [sb(E2, "nbias%d" % i, [128, 5, 1024], F32) for i in range(2)]
            tmpa = [sb(E2, "tmpa%d" % i, [128, 1024], F32) for i in range(2)]
            pTa = [sb(E2, "pTa%d" % i, [128, 8, 128], BF16) for i in range(3)]
            lnz = sb(E2, "lnz", [64, 1024], F32)
            rz = sb(E2, "rz", [64, 1024], F32)
            ystg = [sb(E2, "ystg%d" % i, [64, 8, 256], BF16) for i in range(2)]
            P.op("sp", DMA(kb, KbT.rearrange("h p t -> p h t")), writes=["kb"], dma_key="kb")
            P.op("sp", DMA(qb, QbT.rearrange("h p t -> p h t")), writes=["qb"], dma_key="qb")
            Vb_v = Vb.rearrange("(t p) c -> p t c", p=128)
            P.op("sp", [DMA(vb[:, i * 9:(i + 1) * 9, :], Vb_v[:, i * 9:(i + 1) * 9, :]) for i in range(4)], writes=["vb"], dma_key="vb")
            nab_v = nab_d.rearrange("c j p n -> c p j n")
            P.op("sp", DMA(nbias[0], nab_v[2]), writes=[("nbias", 0)], dma_key="nb0")
            ybT_pv = ybT.rearrange("(hp par) d q -> par d hp q", par=2)
            nsteps = [(p, j) for p in range(32) for j in range(5)]

            def na_qk(i):
                p, j = nsteps[i]
                sset = i % 2
                t0 = max(2 * p - 4, 0)
                ktok = (t0 + 2 * j) * 64
                qs = slice(p * 128, (p + 1) * 128)
                fns = []
                for hp in range(4):
                    fns.append(MM(ps[:, 2 * sset, hp * 128:(hp + 1) * 128], kb[0:64, hp, ktok:ktok + 128], qb[0:64, hp, qs]))
                    fns.append(MM(ps[:, 2 * sset + 1, hp * 128:(hp + 1) * 128], kb[64:128, hp, ktok:ktok + 128], qb[64:128, hp, qs]))
                P.op("pe", fns, reads=["kb", "qb"], writes=[("S", sset)])

            def na_av(i):
                p, j = nsteps[i]
                sset = i % 2
                t0 = max(2 * p - 4, 0)
                ktile = t0 // 2 + j
                if p < 2:
                    if j == 0:
                        P.op("sp", DMA(nbias[1], nab_v[p]), writes=[("nbias", 1)], dma_key="nb1")
                    bslot = 1
                else:
                    bslot = 0
                S2 = ps[:, 2 * sset:2 * sset + 2, :]
                pt = pTa[i % 3]
                tm = tmpa[i % 2]
                ptk = ("pTa", i % 3)
                tmk = ("tmpa", i % 2)
                P.op("dve", TT(tm.rearrange("p (a b) -> p a b", a=2), S2, nbias[bslot][:, j, :].rearrange("p (a b) -> p a b", a=2), ALU.add),
                     reads=[("S", sset), ("nbias", bslot)], writes=[tmk])
                P.op("act", ACTF(pt.rearrange("p h q -> p (h q)"), tm, AF.Exp), reads=[tmk], writes=[ptk])
                fns = [MM(ps[0:64, 6, :], ones[:, 0:64], pt[:, 0:4, :].rearrange("p h q -> p (h q)"), start=(j == 0), stop=(j == 4)),
                       MM(ps[0:64, 7, :], ones[:, 0:64], pt[:, 4:8, :].rearrange("p h q -> p (h q)"), start=(j == 0), stop=(j == 4))]
                for hq in range(8):
                    h = 2 * (hq % 4) + hq // 4
                    ob = 4 + hq // 4
                    fns.append(lambda e, hq=hq, h=h, j=j, ob=ob, ktile=ktile, pt=pt: e.matmul(
                        ps[0:64, ob, (hq % 4) * 128:(hq % 4 + 1) * 128], lhsT=vb[:, ktile, h * 64:(h + 1) * 64], rhs=pt[:, hq, :],
                        start=(j == 0 and hq % 4 == 0), stop=(j == 4), skip_group_check=True))
                P.op("pe", fns, reads=[ptk, "vb", "ones"], writes=["O", "Z"])
                if j == 4:
                    yk = ("ystg", (p // 2) % 2)
                    ys = ystg[(p // 2) % 2]
                    P.op("act", ACTF(lnz.rearrange("p (a b) -> p a b", a=2), ps[0:64, 6:8, :], AF.Ln), reads=["Z"], writes=["lnz"])
                    P.op("act", ACTF(rz, lnz, AF.Exp, scale=-1.0), reads=["lnz"], writes=["rz"])
                    for half in range(2):
                        P.op("dve", TT(ys[:, 4 * half:4 * half + 4, (p % 2) * 128:(p % 2 + 1) * 128],
                                       ps[0:64, 4 + half, :].rearrange("p (h q) -> p h q", h=4),
                                       rz[:, half * 512:(half + 1) * 512].rearrange("p (h q) -> p h q", h=4), ALU.mult),
                             reads=["O", "rz", yk], writes=[yk])
                    if p % 2 == 1:
                        q0 = (p - 1) * 128
                        P.op("sp", [DMA(ybT_pv[par, :, :, q0:q0 + 256], ys[:, par * 4:(par + 1) * 4, :]) for par in range(2)],
                             reads=[yk], dma_key=yk)

            na_qk(0)
            for i in range(len(nsteps)):
                if i + 1 < len(nsteps):
                    na_qk(i + 1)
                na_av(i)
            P.barrier()
            P.emit()
            if stop_after == 2:
                return nc

        with contextlib.ExitStack() as E3:
            ktb = [sb(E3, "ktb%d" % i, [128, S], BF16) for i in range(1)]
            vab = [sb(E3, "vab%d" % i, [128, 64, 130], BF16) for i in range(1)]
            qtb = [sb(E3, "qtb%d" % i, [128, HALF], BF16) for i in range(1)]
            btb = sb(E3, "btb", [128, 4, 32], F32)
            bta = sb(E3, "bta", [128, 4, 64], F32)
            dist_b = sb(E3, "dist_b", [128, 32], F32)
            dist_a = sb(E3, "dist_a", [128, 64], F32)
            qp_b = sb(E3, "qp_b", [128, 4], F32)
            qp_a = sb(E3, "qp_a", [128, 4], F32)
            wpb = sb(E3, "wpb", [128, 4, 4], F32)
            wpa = sb(E3, "wpa", [128, 4, 4], F32)
            dabs = sb(E3, "dabs", [128, 4, 512], F32)
            dneg = sb(E3, "dneg", [128, 4, 512], F32)
            identf = sb(E3, "identf", [128, 128], F32)
            identb = sb(E3, "identb", [128, 128], BF16)
            gsubT = sb(E3, "gsubT", [128, 128], F32)
            lamv = sb(E3, "lamv_sb", [128, 4, 64], F32)
            lprod = sb(E3, "lprod", [128, 2, 64], F32)
            lsum = sb(E3, "lsum", [128, 2], F32)
            lexp = sb(E3, "lexp", [128, 2], F32)
            pTd = [sb(E3, "pTd%d" % i, [128, 2, 512], BF16) for i in range(3)]
            tmpd = sb(E3, "tmpd", [128, 2, 512], F32)
            accs = sb(E3, "accs", [128, 4, 2, 129], F32)
            rr = sb(E3, "rr", [128, 4, 2], F32)
            nl = sb(E3, "nl", [128, 4], F32)
            t1 = sb(E3, "t1", [128, 4, 128], F32)
            o_sb = sb(E3, "o_sb", [128, 4, 128], F32)
            junk = sb(E3, "junk", [128, 128], F32)
            ssq = sb(E3, "ssq", [128, 4], F32)
            rs = sb(E3, "rs", [128, 4], F32)
            yq = sb(E3, "yq", [128, 4, 128], BF16)
            ydst = [sb(E3, "ydst%d" % i, [128, 512], BF16) for i in range(2)]

            def iota(out, pattern, base, cm):
                return lambda e: e.iota(out, pattern, base=base, channel_multiplier=cm, allow_small_or_imprecise_dtypes=True)

            P.op("pool", iota(dist_b, [[128, 32]], 0, -1), writes=["dist_b"])
            P.op("pool", iota(dist_a, [[128, 64]], -511, 1), writes=["dist_a"])
            P.op("pool", iota(qp_b, [[128, 4]], 0, 1), writes=["qp_b"])
            P.op("pool", iota(qp_a, [[-128, 4]], 511, -1), writes=["qp_a"])
            P.op("pool", iota(dabs, [[-128, 4], [1, 512]], 0, -1), writes=["dabs"])
            P.op("pool", iota(identf, [[1, 128]], 0, -1), writes=["identf"])
            P.op("dve", TS(dneg, dabs, -1.0, None, ALU.mult), reads=["dabs"], writes=["dneg"])
            P.op("dve", TT(dabs, dabs, dneg, ALU.max), reads=["dabs", "dneg"], writes=["dabs"])
            P.op("dve", TS(identb, identf, 0.0, None, ALU.is_equal), reads=["identf"], writes=["identb"])
            for h in range(4):
                P.op("dve", TS(btb[:, h, :], dist_b, -SLOPES[h], None, ALU.mult), reads=["dist_b"], writes=[("btb", h)])
                P.op("dve", TS(bta[:, h, :], dist_a, -SLOPES[h], None, ALU.mult), reads=["dist_a"], writes=[("bta", h)])
                P.op("act", ACTF(wpb[:, h, :], qp_b, AF.Exp, scale=-SLOPES[h]), reads=["qp_b"], writes=[("wpb", h)])
                P.op("act", ACTF(wpa[:, h, :], qp_a, AF.Exp, scale=-SLOPES[h]), reads=["qp_a"], writes=[("wpa", h)])
            for sl in range(1):
                P.op("pool", lambda e, sl=sl: e.memset(vab[sl][:, :, 128:130], 1.0), writes=[("vab1", sl)])
            P.op("sp", DMA(lamv, lamv_d), writes=["lamv"], dma_key="c3")
            P.op("sp", DMA(gsubT, sublnT_d), writes=["gsubT"], dma_key="c4")
            P.op("dve", TT(lprod[:, 0, :], lamv[:, 0, :], lamv[:, 1, :], ALU.mult), reads=["lamv"], writes=["lprod0"])
            P.op("dve", TT(lprod[:, 1, :], lamv[:, 2, :], lamv[:, 3, :], ALU.mult), reads=["lamv"], writes=["lprod1"])
            P.op("dve", lambda e: e.reduce_sum(lsum, lprod, AX.X), reads=["lprod0", "lprod1"], writes=["lsum"])
            P.op("act", ACTF(lexp, lsum, AF.Exp), reads=["lsum"], writes=["lexp"])
            P.op("dve", TT(neglam, lexp[:, 1:2], lexp[:, 0:1], ALU.subtract), reads=["lexp"], writes=["neglam"])
            P.op("dve", TS(neglam, neglam, -LAMBDA_INIT, None, ALU.add), reads=["neglam"], writes=["neglam"])
            P.op("dve", TS(gsubT, gsubT, 1.0 - LAMBDA_INIT, None, ALU.mult), reads=["gsubT"], writes=["gsubT"])

            Va_v = Va.rearrange("(t p) c -> p t c", p=128)

            def load_head(h):
                sl = 0
                part = "qkv"
                if "k" in part:
                    P.op("sp", DMA(ktb[sl], KaT[h]), writes=[("ktb", sl)], dma_key=("ktb", sl))
                for half in range(2 if "v" in part else 0):
                    P.op("sp", [DMA(vab[sl][:, half * 32 + q * 8:half * 32 + (q + 1) * 8, 0:128],
                                    Va_v[:, half * 32 + q * 8:half * 32 + (q + 1) * 8, h * 128:(h + 1) * 128]) for q in range(4)],
                         writes=[("vab", sl, half)], dma_key=("vab", sl, half))
                if "q" in part:
                    P.op("sp", DMA(qtb[sl], QaT[h]), writes=[("qtb", sl)], dma_key=("qtb", sl))

            steps = []
            for h in range(4):
                for c in range(8):
                    groups = [("b", list(range(0, 4 * c))), ("d", list(range(4 * c, 4 * c + 4))), ("a", list(range(4 * c + 4, 64)))]
                    groups = [g for g in groups if g[1]]
                    for gi, (g, kts) in enumerate(groups):
                        for ki, kt in enumerate(kts):
                            steps.append(dict(h=h, c=c, g=g, kt=kt, first=(ki == 0), last=(ki == len(kts) - 1),
                                              gfirst=(gi == 0), glast=(gi == len(groups) - 1)))

            def qk(i):
                s = steps[i]
                h, c, kt = s["h"], s["c"], s["kt"]
                sl = 0
                sset = i % 2
                P.op("pe", [MM(ps[:, 2 * sset, :], ktb[sl][0:64, kt * 128:(kt + 1) * 128], qtb[sl][0:64, c * 512:(c + 1) * 512]),
                            MM(ps[:, 2 * sset + 1, :], ktb[sl][64:128, kt * 128:(kt + 1) * 128], qtb[sl][64:128, c * 512:(c + 1) * 512])],
                     reads=[("ktb", sl), ("qtb", sl)], writes=[("S", sset)])

            def accv(qs):
                return ps[:, 4 + qs, :].rearrange("p (m c) -> p m c", m=2)[:, :, 0:129]

            def softmax_av(i):
                s = steps[i]
                h, c, kt, g = s["h"], s["c"], s["kt"], s["g"]
                sl = 0
                sset = i % 2
                pslot = i % 3
                pt = pTd[pslot]
                S2 = ps[:, 2 * sset:2 * sset + 2, :]
                if g == "b":
                    m = 4 * c - kt
                    P.op("act", ACTF(pt, S2, AF.Exp, bias=btb[:, h, m:m + 1]), reads=[("S", sset), ("btb", h)], writes=[("pTd", pslot)])
                elif g == "a":
                    m = kt - 4 * c
                    P.op("act", ACTF(pt, S2, AF.Exp, bias=bta[:, h, m:m + 1]), reads=[("S", sset), ("bta", h)], writes=[("pTd", pslot)])
                else:
                    j = kt - 4 * c
                    for mp in range(2):
                        P.op("dve", STT(tmpd[:, mp, :], dabs[:, j, :], -SLOPES[h], ps[:, 2 * sset + mp, :], ALU.mult, ALU.add),
                             reads=[("S", sset), "dabs"], writes=[("tmpd", mp)])
                    P.op("act", ACTF(pt, tmpd, AF.Exp), reads=[("tmpd", 0), ("tmpd", 1)], writes=[("pTd", pslot)])
                first, last = s["first"], s["last"]
                vh = ("vab", sl, kt // 32)
                fns = []
                for qs in range(4):
                    for m in range(2):
                        fns.append(lambda e, qs=qs, m=m, pt=pt, sl=sl, kt=kt, first=first, last=last: e.matmul(
                            ps[:, 4 + qs, m * 256:m * 256 + 129], lhsT=pt[:, m, qs * 128:(qs + 1) * 128], rhs=vab[sl][:, kt, 0:129],
                            start=(first and m == 0), stop=last, skip_group_check=True))
                P.op("pe", fns, reads=[("pTd", pslot), vh, ("vab1", sl)], writes=["acc"])
                if last:
                    for qs in range(4):
                        ak = ("accs", qs)
                        av = accv(qs)
                        dst = accs[:, qs, :, :]
                        if s["gfirst"]:
                            if g == "b":
                                P.op("dve", TS(dst, av, wpb[:, h, qs:qs + 1], None, ALU.mult), reads=["acc", ("wpb", h)], writes=[ak])
                            elif g == "d":
                                P.op("dve", COPY(dst, av), reads=["acc"], writes=[ak])
                            else:
                                P.op("dve", TS(dst, av, wpa[:, h, qs:qs + 1], None, ALU.mult), reads=["acc", ("wpa", h)], writes=[ak])
                        else:
                            if g == "d":
                                P.op("dve", TT(dst, av, dst, ALU.add), reads=["acc", ak], writes=[ak])
                            else:
                                P.op("dve", STT(dst, av, wpa[:, h, qs:qs + 1], dst, ALU.mult, ALU.add), reads=["acc", ("wpa", h), ak], writes=[ak])
                    if s["glast"] and DA_LVL >= 2:
                        finalize(h, c, i)

            fin = {"n": 0}
            pending = []
            allacc = [("accs", qs) for qs in range(4)]

            def finalize(h, c, i):
                ysl = fin["n"] % 2
                fin["n"] += 1
                P.op("dve", RECIP(rr, accs[:, :, :, 128]), reads=allacc, writes=["rr"])
                P.op("dve", TS(nl, rr[:, :, 1], neglam[:, 0:1], None, ALU.mult), reads=["rr", "neglam"], writes=["nl"])
                for qs in range(4):
                    P.op("dve", TS(t1[:, qs, :], accs[:, qs, 0, 0:128], rr[:, qs, 0:1], None, ALU.mult), reads=allacc + ["rr"], writes=[("t1", qs)])
                    P.op("dve", STT(o_sb[:, qs, :], accs[:, qs, 1, 0:128], nl[:, qs:qs + 1], t1[:, qs, :], ALU.mult, ALU.add),
                         reads=allacc + ["nl", ("t1", qs)], writes=[("o", qs)])
                    P.op("dve", lambda e, qs=qs: e.scalar_tensor_tensor(junk, o_sb[:, qs, :], 1.0, o_sb[:, qs, :], ALU.mult, ALU.mult,
                                                                         accum_out=ssq[:, qs:qs + 1]),
                         reads=[("o", qs)], writes=["junk", ("ssq", qs)])
                P.op("dve", TS(rs, ssq, 1.0 / 128, EPS, ALU.mult, ALU.add), reads=[("ssq", qs) for qs in range(4)], writes=["rs"])

                def stage_b(k):
                    P.op("act", ACTF(rs, rs, AF.Ln), reads=["rs"], writes=["rs"])
                    P.op("act", ACTF(rs, rs, AF.Exp, scale=-0.5), reads=["rs"], writes=["rs"])
                    for qs in range(4):
                        P.op("dve", STT(yq[:, qs, :], o_sb[:, qs, :], rs[:, qs:qs + 1], gsubT, ALU.mult, ALU.mult),
                             reads=[("o", qs), "rs", "gsubT"], writes=[("yq", qs)])

                def stage_c(k):
                    sset = k % 2
                    psb = ps[:, 2 * sset, 0:256].bitcast(BF16)
                    P.op("pe", [lambda e, qs=qs, psb=psb: e.transpose(psb[:, qs * 128:(qs + 1) * 128], yq[:, qs, :], identb) for qs in range(4)],
                         reads=[("yq", qs) for qs in range(4)] + ["identb"], writes=[("S", sset)])
                    P.op("dve", COPY(ydst[ysl], psb), reads=[("S", sset)], writes=[("ydst", ysl)])
                    P.op("sp", DMA(yaT[h, :, c * 512:(c + 1) * 512], ydst[ysl]), reads=[("ydst", ysl)], dma_key=("ydst", ysl))

                if DA_LVL >= 3:
                    pending.append((i + 3, stage_b))
                if DA_LVL >= 4:
                    pending.append((i + 6, stage_c))

            import os
            load_head(0)
            n = min(len(steps), int(os.environ.get("DA_STEPS", "100000")))
            DA_LVL = int(os.environ.get("DA_LVL", "9"))
            qk(0)
            for i in range(n):
                boundary = (i + 1 < n and steps[i + 1]["h"] != steps[i]["h"])
                if i + 1 < n and not boundary:
                    qk(i + 1)
                softmax_av(i)
                if boundary:
                    load_head(steps[i + 1]["h"])
                    qk(i + 1)
                while pending and pending[0][0] <= i:
                    pending.pop(0)[1](i)
            while pending:
                pending.pop(0)[1](n - 1)
            P.barrier()
            P.emit()
            if stop_after == 3:
                return nc

        with contextlib.ExitStack() as E4:
            wg = sb(E4, "wg", [128, 8, 2048], BF16)
            wa = sb(E4, "wa", [128, 4, 1024], BF16)
            wb = sb(E4, "wb", [64, 8, 1024], BF16)
            wo = sb(E4, "wo", [128, 8, 1024], BF16)
            xts = [sb(E4, "xta%d" % i, [128, 8, 512], F32) for i in range(2)]
            sq = sb(E4, "sq4", [128, 8, 512], BF16)
            hTs = [sb(E4, "hT4_%d" % i, [128, 8, 512], BF16) for i in range(2)]
            sqn = sb(E4, "sqn4", [128, 8, 512], BF16)
            rt = sb(E4, "rt4", [128, 512], F32)
            rstd = sb(E4, "rstd4", [128, 512], F32)
            yat = sb(E4, "yat", [128, 4, 512], BF16)
            ybt = sb(E4, "ybt", [64, 8, 512], BF16)
            ga = sb(E4, "ga", [128, 512], F32)
            gb = sb(E4, "gb", [128, 512], F32)
            m1 = sb(E4, "m1", [128, 512], F32)
            m2 = sb(E4, "m2", [128, 512], F32)
            mg = sb(E4, "mg", [128, 8, 512], BF16)
            mo = sb(E4, "mo", [128, 8, 512], F32)
            w_in_v = w_in.rearrange("(kc p) n -> p kc n", p=128)
            cast_load_all([(wg[:, kc, :], w_in_v[:, kc, 3072:5120], 2048) for kc in range(8)], "wg")
            w_a_v = w_a.rearrange("(h p) n -> p h n", p=128)
            w_b_v = w_b.rearrange("(h p) n -> p h n", p=64)
            w_o_v = w_o.rearrange("(kc p) n -> p kc n", p=128)
            cast_load_all([(wa[:, h, :], w_a_v[:, h, :], 1024) for h in range(4)], "wa")
            cast_load_all([(wb[:, h, :], w_b_v[:, h, :], 1024) for h in range(8)], "wb")
            cast_load_all([(wo[:, kc, :], w_o_v[:, kc, :], 1024) for kc in range(8)], "wo")
            xT_v = xT.rearrange("(kc p) t -> p kc t", p=128)
            x1T_v = x1T.rearrange("(kc p) t -> p kc t", p=128)
            yaT_v = yaT.rearrange("h p t -> p h t")
            ybT_v2 = ybT.rearrange("h d q -> d h q")
            wgk = ["wg"]

            def load_x3(t):
                P.op("sp", DMA(xts[t % 2], xT_v[:, :, t * 512:(t + 1) * 512]), writes=[xt_key(t % 2)], dma_key=("xta", t % 2))

            load_x3(0)
            load_x3(1)
            rmsnorm_in(xts[0], 0, hTs[0], sqn, rt, rstd, 512, 0, 0, hp=0)
            ob_i = 0
            for t in range(8):
                tok = slice(t * 512, (t + 1) * 512)
                xt = xts[t % 2]
                hT = hTs[t % 2]
                hpar = t % 2
                P.op("sp", DMA(yat, yaT_v[:, :, tok]), writes=["yat"], dma_key="yat")
                P.op("sp", DMA(ybt, ybT_v2[:, :, tok]), writes=["ybt"], dma_key="ybt")
                for oc in range(8):
                    if oc == 4 and t + 1 < 8:
                        rmsnorm_in(xts[(t + 1) % 2], 0, hTs[(t + 1) % 2], sqn, rt, rstd, 512, (t + 1) % 2, 0, hp=(t + 1) % 2)
                    ocs = slice(oc * 128, (oc + 1) * 128)
                    P.op("pe", [MM(ps[:, 1, :], wa[:, h, ocs], yat[:, h, :], start=(h == 0), stop=(h == 3)) for h in range(4)],
                         reads=["yat", "wa"], writes=[("ps", 1)])
                    P.op("pe", [MM(ps[:, 2, :], wb[:, h, ocs], ybt[:, h, :], start=(h == 0), stop=(h == 7)) for h in range(8)],
                         reads=["ybt", "wb"], writes=[("ps", 2)])
                    P.op("pe", [MM(ps[:, 3, :], wg[:, kc, oc * 128:(oc + 1) * 128], hT[:, kc, :], start=(kc == 0), stop=(kc == 7))
                                for kc in range(8)], reads=[("hT", hpar, kc) for kc in range(8)] + wgk, writes=[("ps", 3)])
                    P.op("pe", [MM(ps[:, 4, :], wg[:, kc, 1024 + oc * 128:1024 + (oc + 1) * 128], hT[:, kc, :], start=(kc == 0), stop=(kc == 7))
                                for kc in range(8)], reads=[("hT", hpar, kc) for kc in range(8)] + wgk, writes=[("ps", 4)])
                    P.op("act", ACTF(ga, ps[:, 3, :], AF.Sigmoid, bias=bgate[:, oc:oc + 1]), reads=[("ps", 3), "bgate"], writes=["ga"])
                    P.op("act", ACTF(gb, ps[:, 4, :], AF.Sigmoid, bias=bgate[:, 8 + oc:9 + oc]), reads=[("ps", 4), "bgate"], writes=["gb"])
                    P.op("dve", TT(m1, ps[:, 1, :], ga, ALU.mult), reads=[("ps", 1), "ga"], writes=["m1"])
                    P.op("dve", TT(m2, ps[:, 2, :], gb, ALU.mult), reads=[("ps", 2), "gb"], writes=["m2"])
                    P.op("pool", TT(mg[:, oc, :], m1, m2, ALU.add), reads=["m1", "m2"], writes=[("mg", oc)])
                for oc2 in range(8):
                    bank = 5 + (ob_i % 2)
                    ob_i += 1
                    P.op("pe", [MM(ps[:, bank, :], wo[:, oc, oc2 * 128:(oc2 + 1) * 128], mg[:, oc, :], start=(oc == 0), stop=(oc == 7))
                                for oc in range(8)], reads=[("mg", oc) for oc in range(8)] + ["wo"],
                         writes=[("ps", bank)])
                    P.op("act", ACTF(mo[:, oc2, :], ps[:, bank, :], AF.Copy), reads=[("ps", bank)], writes=[("mo", oc2)])
                    P.op("dve", TT(sq[:, oc2, :], mo[:, oc2, :], mo[:, oc2, :], ALU.mult), reads=[("mo", oc2)], writes=[("sq", oc2)])
                P.op("pe", [MM(ps[:, 7, :], ones, sq[:, oc2, :], start=(oc2 == 0), stop=(oc2 == 7)) for oc2 in range(8)],
                     reads=[("sq", k) for k in range(8)] + ["ones"], writes=[("ps", 7)])
                P.op("dve", TS(rt, ps[:, 7, :], 1.0 / D, EPS, ALU.mult, ALU.add), reads=[("ps", 7)], writes=["rt"])
                P.op("act", ACTF(rt, rt, AF.Ln), reads=["rt"], writes=["rt"])
                P.op("act", ACTF(rstd, rt, AF.Exp, scale=-0.5), reads=["rt"], writes=["rstd"])
                for oc2 in range(8):
                    P.op("dve", STT(mo[:, oc2, :], mo[:, oc2, :], gains[:, 1, oc2:oc2 + 1], rstd, ALU.mult, ALU.mult),
                         reads=[("mo", oc2), "rstd", "gains"], writes=[("mo", oc2)])
                    P.op("pool", TT(xt[:, oc2, :], xt[:, oc2, :], mo[:, oc2, :], ALU.add), reads=[("mo", oc2), xt_key(t % 2)],
                         writes=[xt_key(t % 2)])
                P.op("sp", DMA(x1T_v[:, :, tok], xt), reads=[xt_key(t % 2)], dma_key=("x1o", t % 2))
                if t + 2 < 8:
                    load_x3(t + 2)
            P.barrier()
            P.emit()
            if stop_after == 4:
                return nc

        NT = 256
        with contextlib.ExitStack() as E5:
            wgt = sb(E5, "wgt", [128, 8, DFF], BF16)
            wup = sb(E5, "wup", [128, 8, DFF], BF16)
            wdn = sb(E5, "wdn", [128, NJ, 1024], BF16)
            xts = [sb(E5, "xtb%d" % i, [128, 8, NT], F32) for i in range(2)]
            sq = sb(E5, "sq5", [128, 8, NT], BF16)
            hTs = [sb(E5, "hT5_%d" % i, [128, 8, NT], BF16) for i in range(2)]
            sqn = sb(E5, "sqn5", [128, 8, NT], BF16)
            rt = sb(E5, "rt5", [128, NT], F32)
            rstd = sb(E5, "rstd5", [128, NT], F32)
            sg = [sb(E5, "sg%d" % i, [128, NT], F32) for i in range(2)]
            aT = sb(E5, "aT", [128, NJ, NT], BF16)
            fo = sb(E5, "fo", [128, 8, NT], F32)
            wg_v = w_gate.rearrange("(kc p) n -> p kc n", p=128)
            wu_v = w_up.rearrange("(kc p) n -> p kc n", p=128)
            wd_v = w_down.rearrange("(j p) n -> p j n", p=128)
            cast_load_all([(wgt[:, kc, :], wg_v[:, kc, :], DFF) for kc in range(8)], "wgt")
            cast_load_all([(wup[:, kc, :], wu_v[:, kc, :], DFF) for kc in range(8)], "wup")
            cast_load_all([(wdn[:, j, :], wd_v[:, j, :], 1024) for j in range(NJ)], "wdn")
            x1T_v = x1T.rearrange("(kc p) t -> p kc t", p=128)
            outT_v = outT.rearrange("(kc p) t -> p kc t", p=128)
            NTL = HALF // NT

            def load_x4(t):
                P.op("sp", DMA(xts[t % 2], x1T_v[:, :, t * NT:(t + 1) * NT]), writes=[xt_key(t % 2)], dma_key=("xtb", t % 2))

            def pst(idx):
                return ps[:, idx, 0:NT]

            load_x4(0)
            load_x4(1)
            rmsnorm_in(xts[0], 2, hTs[0], sqn, rt, rstd, NT, 0, 0, hp=0)
            gi = 0
            di = 0
            wk_g = ["wgt"]
            wk_u = ["wup"]
            for t in range(NTL):
                xt = xts[t % 2]
                hT = hTs[t % 2]
                hpar = t % 2
                for j in range(NJ):
                    if j == 11 and t + 1 < NTL:
                        rmsnorm_in(xts[(t + 1) % 2], 2, hTs[(t + 1) % 2], sqn, rt, rstd, NT, (t + 1) % 2, 0, hp=(t + 1) % 2)
                    gslot = 1 + (gi % 2)
                    uslot = 3 + (gi % 2)
                    gi += 1
                    js = slice(j * 128, (j + 1) * 128)
                    P.op("pe", [MM(pst(gslot), wgt[:, kc, js], hT[:, kc, :], start=(kc == 0), stop=(kc == 7)) for kc in range(8)],
                         reads=[("hT", hpar, kc) for kc in range(8)] + wk_g, writes=[("pt", gslot)])
                    P.op("pe", [MM(pst(uslot), wup[:, kc, js], hT[:, kc, :], start=(kc == 0), stop=(kc == 7)) for kc in range(8)],
                         reads=[("hT", hpar, kc) for kc in range(8)] + wk_u, writes=[("pt", uslot)])
                    sgs = sg[j % 2]
                    P.op("act", ACTF(sgs, pst(gslot), AF.Silu), reads=[("pt", gslot)], writes=[("sg", j % 2)])
                    P.op("dve", TT(aT[:, j, :], pst(uslot), sgs, ALU.mult), reads=[("pt", uslot), ("sg", j % 2)],
                         writes=[("aT", j)])
                for oc in range(8):
                    dslot = 5 + (di % 2)
                    di += 1
                    P.op("pe", [MM(pst(dslot), wdn[:, j, oc * 128:(oc + 1) * 128], aT[:, j, :], start=(j == 0), stop=(j == NJ - 1))
                                for j in range(NJ)], reads=[("aT", j) for j in range(NJ)] + ["wdn"],
                         writes=[("pt", dslot)])
                    P.op("act", ACTF(fo[:, oc, :], pst(dslot), AF.Copy), reads=[("pt", dslot)], writes=[("fo", oc)])
                    P.op("dve", TT(sq[:, oc, :], fo[:, oc, :], fo[:, oc, :], ALU.mult), reads=[("fo", oc)], writes=[("sq", oc)])
                P.op("pe", [MM(ps[:, 7, :NT], ones, sq[:, oc, :], start=(oc == 0), stop=(oc == 7)) for oc in range(8)],
                     reads=[("sq", k) for k in range(8)] + ["ones"], writes=[("ps", 7)])
                P.op("dve", TS(rt, ps[:, 7, :NT], 1.0 / D, EPS, ALU.mult, ALU.add), reads=[("ps", 7)], writes=["rt"])
                P.op("act", ACTF(rt, rt, AF.Ln), reads=["rt"], writes=["rt"])
                P.op("act", ACTF(rstd, rt, AF.Exp, scale=-0.5), reads=["rt"], writes=["rstd"])
                for oc in range(8):
                    P.op("dve", STT(fo[:, oc, :], fo[:, oc, :], gains[:, 3, oc:oc + 1], rstd, ALU.mult, ALU.mult),
                         reads=[("fo", oc), "rstd", "gains"], writes=[("fo", oc)])
                    P.op("pool", TT(xt[:, oc, :], xt[:, oc, :], fo[:, oc, :], ALU.add), reads=[("fo", oc), xt_key(t % 2)],
                         writes=[xt_key(t % 2)])
                P.op("sp", DMA(outT_v[:, :, t * NT:(t + 1) * NT], xt), reads=[xt_key(t % 2)], dma_key=("outo", t % 2))
                if t + 2 < NTL:
                    load_x4(t + 2)
            P.barrier()
            P.emit()
            if stop_after == 5:
                return nc
    return nc


def _na_bias_tables(rpb, hf):
    out = np.full((3, 5, 128, 8, 2, 64), NEG, dtype=np.float32)
    lc = np.arange(64)
    qc = lc if hf == 0 else 63 - lc
    kc = qc
    cstart = np.clip(qc - 8, 0, 48)
    col_in = (kc[:, None] >= cstart[None, :]) & (kc[:, None] < cstart[None, :] + 16)
    col_off = np.clip(kc[:, None] - qc[None, :] + 15, 0, 30)
    for cls, lr0 in enumerate([0, 2, 4]):
        t0 = max(lr0 - 4, 0)
        for rr in range(2):
            lr = lr0 + rr
            r = lr if hf == 0 else 127 - lr
            rs = min(max(r - 4, 0), 120)
            for j in range(5):
                for i in range(2):
                    krow_l = t0 + 2 * j + i
                    g = krow_l if hf == 0 else 127 - krow_l
                    if not (rs <= g < rs + 8):
                        continue
                    row_off = g - r + 7
                    vals = rpb[:, row_off, :][:, col_off]
                    vals = np.transpose(vals, (1, 0, 2))
                    vals = vals[:, [0, 2, 4, 6, 1, 3, 5, 7], :]
                    blk = np.where(col_in[:, None, :], vals, np.float32(NEG))
                    out[cls, j, i * 64:(i + 1) * 64, :, rr, :] = blk
    return out.reshape(3, 5, 128, 1024)


_CACHE = {}


def kernel(x, pre_mix_w, w_in, b_gate, lambda_q1, lambda_k1, lambda_q2, lambda_k2,
           subln_w, rpb, w_branch_a, w_branch_b, w_out, post_mix_w, pre_ffn_w,
           w_gate, w_up, w_down, post_ffn_w):
    f32 = np.float32
    x = np.asarray(x, dtype=f32)
    if "nc" not in _CACHE:
        _CACHE["nc"] = build_program()
    nc = _CACHE["nc"]

    def vec8(v):
        return np.asarray(v, f32).reshape(8, 128).T

    gains = np.ascontiguousarray(np.stack([vec8(pre_mix_w[0]), vec8(post_mix_w[0]), vec8(pre_ffn_w[0]), vec8(post_ffn_w[0])], axis=1))
    bgate = np.ascontiguousarray(np.asarray(b_gate[0], f32).reshape(16, 128).T)
    subln = np.ascontiguousarray(np.asarray(subln_w[0], f32).reshape(128, 1))
    lamv = np.stack([np.asarray(v[0], f32) for v in (lambda_q1, lambda_k1, lambda_q2, lambda_k2)], axis=0)
    lamv = np.ascontiguousarray(np.broadcast_to(lamv[None], (128, 4, 64)))
    rpb0 = np.asarray(rpb[0], f32)
    nab = [_na_bias_tables(rpb0, 0), _na_bias_tables(rpb0, 1)]
    common = {
        "w_in": np.ascontiguousarray(np.asarray(w_in[0], f32)),
        "w_a": np.ascontiguousarray(np.asarray(w_branch_a[0], f32)),
        "w_b": np.ascontiguousarray(np.asarray(w_branch_b[0], f32)),
        "w_o": np.ascontiguousarray(np.asarray(w_out[0], f32)),
        "w_gate": np.ascontiguousarray(np.asarray(w_gate[0], f32)),
        "w_up": np.ascontiguousarray(np.asarray(w_up[0], f32)),
        "w_down": np.ascontiguousarray(np.asarray(w_down[0], f32)),
        "gains": gains, "bgate": bgate, "subln": subln, "lamv": lamv,
        "sublnT": np.ascontiguousarray(np.broadcast_to(np.asarray(subln_w[0], f32)[None, :], (128, 128))),
    }
    in_maps = []
    for core in range(8):
        b, hf = core // 2, core % 2
        xb = x[b] if hf == 0 else x[b, ::-1]
        m = dict(common)
        m["xT"] = np.ascontiguousarray(xb.T)
        m["nab"] = nab[hf]
        in_maps.append(m)
    res = run_bass_kernel_spmd(nc, in_maps, core_ids=list(range(8)))
    out = np.empty((B, S, D), dtype=f32)
    for core in range(8):
        b, hf = core // 2, core % 2
        o = res.results[core]["outT"].T
        if hf == 0:
            out[b, :HALF] = o
        else:
            out[b, HALF:] = o[::-1]
    return out
```

```python
import contextlib
import math
import numpy as np
import concourse.bass as bass
import concourse.mybir as mybir
from concourse.bass_utils import run_bass_kernel_spmd

F32 = mybir.dt.float32
BF16 = mybir.dt.bfloat16
AF = mybir.ActivationFunctionType
ALU = mybir.AluOpType
AX = mybir.AxisListType

D = 1024
S = 8192
HALF = 4096
B = 4
DFF = 2816
NJ = DFF // 128
INC = 5120
EPS = 1e-6
NEG = -1e30
SLOPES = [2.0 ** (-8.0 * (i + 1) / 4) for i in range(4)]
LAMBDA_INIT = 0.8 - 0.6 * math.exp(-0.3 * 0)
NA_LOC_TOK = 4608

ENGS = ("pe", "act", "dve", "pool", "sp")
SAME_ENGINE_SYNC = ("act", "dve", "pool")


class Prog:
    def __init__(self, nc):
        self.nc = nc
        self.ops = {e: [] for e in ENGS}
        self.eng_cnt = {e: 0 for e in ENGS}
        self.dma_cnt = {}
        self.waited = {e: {} for e in ENGS}
        self.res = {}
        self.sems = {}
        self.nins = 0

    def _sem(self, key):
        if key not in self.sems:
            self.sems[key] = self.nc.alloc_semaphore(name="s%d" % len(self.sems))
        return self.sems[key]

    def op(self, eng, fns, reads=(), writes=(), dma_key=None):
        if callable(fns):
            fns = [fns]
        fns = list(fns)
        need = {}

        def want(t):
            if t is None:
                return
            k, v = t
            if k == ("eng", eng) and eng not in SAME_ENGINE_SYNC:
                return
            if self.waited[eng].get(k, 0) >= v:
                return
            if need.get(k, 0) < v:
                need[k] = v

        for r in reads:
            st = self.res.get(r)
            if st:
                want(st[0])
        for w in writes:
            st = self.res.get(w)
            if st:
                want(st[0])
                for t in st[1]:
                    want(t)
        for k, v in need.items():
            self.waited[eng][k] = v
        if dma_key is not None:
            k = ("dma", dma_key)
            self.dma_cnt[k] = self.dma_cnt.get(k, 0) + 16 * len(fns)
            ticket = (k, self.dma_cnt[k])
            mode = "dma"
        else:
            k = ("eng", eng)
            self.eng_cnt[eng] += 1
            ticket = (k, self.eng_cnt[eng])
            mode = "cmp"
        self._sem(k)
        for kk in need:
            self._sem(kk)
        for r in reads:
            st = self.res.setdefault(r, [None, []])
            st[1].append(ticket)
        for w in writes:
            self.res[w] = [ticket, []]
        self.ops[eng].append((sorted(need.items(), key=str), fns, k, mode))
        self.nins += len(fns)
        return ticket

    def barrier(self):
        allt = []
        for e in ENGS:
            if self.eng_cnt[e]:
                allt.append((("eng", e), self.eng_cnt[e]))
        for k, v in self.dma_cnt.items():
            allt.append((k, v))
        self.res = {"__bar__": [None, list(allt)]}
        self.op("sp", lambda e: e.nop(), writes=["__bar__"])
        for e in ENGS:
            if e != "sp":
                self.op(e, lambda en: en.nop(), reads=["__bar__"])
        self.res = {}

    def _emit_engine(self, name, eng):
        for waits, fns, k, mode in self.ops[name]:
            for sk, v in waits:
                eng.wait_ge(self.sems[sk], v)
            n = len(fns)
            for i, fn in enumerate(fns):
                ins = fn(eng)
                if mode == "dma":
                    ins.then_inc(self.sems[k], 16)
                elif i == n - 1:
                    ins.then_inc(self.sems[k], 1)
        self.ops[name] = []

    def emit(self):
        with self.nc.Block() as block:
            @block.tensor
            def _(e):
                self._emit_engine("pe", e)

            @block.scalar
            def _(e):
                self._emit_engine("act", e)

            @block.vector
            def _(e):
                self._emit_engine("dve", e)

            @block.gpsimd
            def _(e):
                self._emit_engine("pool", e)

            @block.sync
            def _(e):
                self._emit_engine("sp", e)


def MM(out, lhsT, rhs, start=True, stop=True):
    return lambda e: e.matmul(out, lhsT=lhsT, rhs=rhs, start=start, stop=stop)


def DMA(out, in_):
    return lambda e: e.dma_start(out=out, in_=in_)


def TT(out, a, b, op):
    return lambda e: e.tensor_tensor(out, a, b, op)


def TS(out, a, s1, s2, op0, op1=None):
    if op1 is None:
        return lambda e: e.tensor_scalar(out, a, s1, None, op0)
    return lambda e: e.tensor_scalar(out, a, s1, s2, op0, op1)


def STT(out, in0, scalar, in1, op0, op1):
    return lambda e: e.scalar_tensor_tensor(out, in0, scalar, in1, op0, op1)


def ACTF(out, in_, func, bias=None, scale=1.0):
    if bias is None:
        return lambda e: e.activation(out, in_, func, scale=scale)
    return lambda e: e.activation(out, in_, func, bias=bias, scale=scale)


def COPY(out, in_):
    return lambda e: e.tensor_copy(out, in_)


def RECIP(out, in_):
    return lambda e: e.reciprocal(out, in_)


def build_program(stop_after=99, dbg=False):
    nc = bass.Bass("TRN2", target_bir_lowering=False)

    def din(name, shape, dt=F32):
        return nc.dram_tensor(name, list(shape), dt, kind="ExternalInput").ap()

    def dscr(name, shape, dt):
        return nc.dram_tensor(name, list(shape), dt, kind=("ExternalOutput" if dbg else "Internal")).ap()

    xT = din("xT", [D, S])
    w_in = din("w_in", [D, INC])
    w_a = din("w_a", [512, D])
    w_b = din("w_b", [512, D])
    w_o = din("w_o", [D, D])
    w_gate = din("w_gate", [D, DFF])
    w_up = din("w_up", [D, DFF])
    w_down = din("w_down", [DFF, D])
    gains_d = din("gains", [128, 4, 8])
    bgate_d = din("bgate", [128, 16])
    subln_d = din("subln", [128, 1])
    lamv_d = din("lamv", [128, 4, 64])
    sublnT_d = din("sublnT", [128, 128])
    nab_d = din("nab", [3, 5, 128, 1024])
    outT = nc.dram_tensor("outT", [D, HALF], F32, kind="ExternalOutput").ap()

    QaT = dscr("QaT", [4, 128, HALF], BF16)
    KaT = dscr("KaT", [4, 128, S], BF16)
    Va = dscr("Va", [S, 512], BF16)
    QbT = dscr("QbT", [4, 128, HALF], BF16)
    KbT = dscr("KbT", [4, 128, NA_LOC_TOK], BF16)
    Vb = dscr("Vb", [NA_LOC_TOK, 512], BF16)
    yaT = dscr("yaT", [4, 128, HALF], BF16)
    ybT = dscr("ybT", [8, 64, HALF], BF16)
    x1T = dscr("x1T", [D, HALF], F32)

    P = Prog(nc)
    ps = nc.alloc_psum_tensor("ps", [128, 8, 512], F32).ap()

    def sb(es, name, shape, dt):
        return es.enter_context(nc.sbuf_tensor(name, list(shape), dt)).ap()

    with contextlib.ExitStack() as G:
        ones = sb(G, "ones", [128, 128], BF16)
        neghalf = sb(G, "neghalf", [128, 512], F32)
        gains = sb(G, "gains_sb", [128, 4, 8], F32)
        bgate = sb(G, "bgate_sb", [128, 16], F32)
        subln = sb(G, "subln_sb", [128, 1], F32)
        gsub = sb(G, "gsub", [128, 1], F32)
        neglam = sb(G, "neglam", [128, 1], F32)

        P.op("pool", lambda e: e.memset(ones, 1.0), writes=["ones"])
        P.op("pool", lambda e: e.memset(neghalf, -0.5), writes=["neghalf"])
        P.op("sp", DMA(gains, gains_d), writes=["gains"], dma_key="c0")
        P.op("sp", DMA(bgate, bgate_d), writes=["bgate"], dma_key="c1")
        P.op("sp", DMA(subln, subln_d), writes=["subln"], dma_key="c2")

        def cast_fns(dst, src, ncols):
            fns = []
            for c0 in range(0, ncols, 2048):
                c1 = min(ncols, c0 + 2048)
                fns.append(DMA(dst[:, c0:c1], src[:, c0:c1]))
            return fns

        def cast_load_all(pairs, key):
            fns = []
            for dst, src, ncols in pairs:
                fns += cast_fns(dst, src, ncols)
            P.op("pool", fns, writes=[key], dma_key=key)

        def rmsnorm_in(xt, gidx, hT, sq, rt, rstd, N, tag, ss_bank, hp=0):
            P.op("dve", TT(sq[:, :, :N], xt[:, :, :N], xt[:, :, :N], ALU.mult), reads=[xt_key(tag)], writes=[("sq", k) for k in range(8)])
            P.op("pe", [MM(ps[:, ss_bank, :N], ones, sq[:, kc, :N], start=(kc == 0), stop=(kc == 7)) for kc in range(8)],
                 reads=[("sq", k) for k in range(8)] + ["ones"], writes=[("ps", ss_bank)])
            P.op("dve", TS(rt[:, :N], ps[:, ss_bank, :N], 1.0 / D, EPS, ALU.mult, ALU.add),
                 reads=[("ps", ss_bank)], writes=["rt"])
            P.op("act", ACTF(rt[:, :N], rt[:, :N], AF.Ln), reads=["rt"], writes=["rt"])
            P.op("act", ACTF(rstd[:, :N], rt[:, :N], AF.Exp, scale=-0.5), reads=["rt"], writes=["rstd"])
            for kc in range(8):
                P.op("dve", STT(hT[:, kc, :N], xt[:, kc, :N], gains[:, gidx, kc:kc + 1], rstd[:, :N], ALU.mult, ALU.mult),
                     reads=[xt_key(tag), "rstd", "gains"], writes=[("hT", hp, kc)])

        def xt_key(tag):
            return ("xt", tag)

        with contextlib.ExitStack() as E1:
            win = sb(E1, "win", [128, 8, 3072], BF16)
            xts = [sb(E1, "xt%d" % i, [128, 8, 512], F32) for i in range(2)]
            sq = sb(E1, "sq1", [128, 8, 512], BF16)
            hTs = [sb(E1, "hT1_%d" % i, [128, 8, 512], BF16) for i in range(2)]
            cur = {"hT": hTs[0], "hp": 0}
            rt = sb(E1, "rt1", [128, 512], F32)
            rstd = sb(E1, "rstd1", [128, 512], F32)
            NSTG = 6
            stg = [sb(E1, "stg%d" % i, [128, 512], BF16) for i in range(NSTG)]
            w_in_v = w_in.rearrange("(kc p) n -> p kc n", p=128)
            cast_load_all([(win[:, kc, :], w_in_v[:, kc, 0:3072], 3072) for kc in range(8)], "win")
            xT_v = xT.rearrange("(kc p) t -> p kc t", p=128)
            banks = [1, 2, 3, 4, 5, 6, 7]
            st = {"b": 0, "s": 0, "ev": 0}

            def load_x(t):
                P.op("sp", DMA(xts[t % 2], xT_v[:, :, t * 512:(t + 1) * 512]), writes=[xt_key(t % 2)], dma_key=("xt", t % 2))

            def evac_store(bank, dst, scale=None):
                k = st["s"] % NSTG
                st["s"] += 1
                eng = "act" if st["ev"] % 2 == 0 else "dve"
                st["ev"] += 1
                if eng == "act":
                    fn = ACTF(stg[k], ps[:, bank, :], AF.Copy, scale=(1.0 if scale is None else scale))
                else:
                    if scale is None:
                        fn = COPY(stg[k], ps[:, bank, :])
                    else:
                        fn = TS(stg[k], ps[:, bank, :], scale, None, ALU.mult)
                P.op(eng, fn, reads=[("ps", bank)], writes=[("stg", k)])
                P.op("sp", DMA(dst, stg[k]), reads=[("stg", k)], dma_key=("stg", k))

            def fm_chunk(col0, dst, scale=None):
                bank = banks[st["b"] % len(banks)]
                st["b"] += 1
                hT, hp = cur["hT"], cur["hp"]
                P.op("pe", [MM(ps[:, bank, :], win[:, kc, col0:col0 + 128], hT[:, kc, :], start=(kc == 0), stop=(kc == 7))
                            for kc in range(8)],
                     reads=[("hT", hp, kc) for kc in range(8)] + ["win"], writes=[("ps", bank)])
                evac_store(bank, dst, scale)

            def tm_chunk(col0, sub, dst):
                bank = banks[st["b"] % len(banks)]
                st["b"] += 1
                hT, hp = cur["hT"], cur["hp"]
                P.op("pe", [MM(ps[:, bank, :], hT[:, kc, sub * 128:(sub + 1) * 128], win[:, kc, col0:col0 + 512],
                               start=(kc == 0), stop=(kc == 7)) for kc in range(8)],
                     reads=[("hT", hp, kc) for kc in range(8)] + ["win"], writes=[("ps", bank)])
                evac_store(bank, dst)

            load_x(0)
            load_x(1)
            rmsnorm_in(xts[0], 0, hTs[0], sq, rt, rstd, 512, 0, 0, hp=0)
            for t in range(16):
                tok = slice(t * 512, (t + 1) * 512)
                mine = t < 8
                nb = t <= 8
                work = []
                for h in range(4):
                    if mine:
                        work.append(lambda h=h, tok=tok: fm_chunk(h * 128, QaT[h, :, tok], scale=0.125))
                    work.append(lambda h=h, tok=tok: fm_chunk(512 + h * 128, KaT[h, :, tok]))
                for sub in range(4):
                    work.append(lambda sub=sub, t=t: tm_chunk(1024, sub, Va[t * 512 + sub * 128: t * 512 + (sub + 1) * 128, :]))
                if nb:
                    for hp_ in range(4):
                        if mine:
                            work.append(lambda hp_=hp_, tok=tok: fm_chunk(1536 + hp_ * 128, QbT[hp_, :, tok], scale=0.125))
                        work.append(lambda hp_=hp_, tok=tok: fm_chunk(2048 + hp_ * 128, KbT[hp_, :, tok]))
                    for sub in range(4):
                        work.append(lambda sub=sub, t=t: tm_chunk(2560, sub, Vb[t * 512 + sub * 128: t * 512 + (sub + 1) * 128, :]))
                cur["hT"], cur["hp"] = hTs[t % 2], t % 2
                half = len(work) // 2
                for w in work[:half]:
                    w()
                if t + 1 < 16:
                    rmsnorm_in(xts[(t + 1) % 2], 0, hTs[(t + 1) % 2], sq, rt, rstd, 512, (t + 1) % 2, 0, hp=(t + 1) % 2)
                for w in work[half:]:
                    w()
                if t + 2 < 16:
                    load_x(t + 2)
            P.barrier()
            P.emit()
            if stop_after == 1:
                return nc

        with contextlib.ExitStack() as E2:
            kb = sb(E2, "kb", [128, 4, NA_LOC_TOK], BF16)
            vb = sb(E2, "vb", [128, NA_LOC_TOK // 128, 512], BF16)
            qb = sb(E2, "qb", [128, 4, HALF], BF16)
            nbias = [sb(E2, "nbias%d" % i, [128, 5, 1024], F32) for i in range(2)]
            tmpa = [sb(E2, "tmpa%d" % i, [128, 1024], F32) for i in range(2)]
            pTa = [sb(E2, "pTa%d" % i, [128, 8, 128], BF16) for i in range(3)]
            lnz = sb(E2, "lnz", [64, 1024], F32)
            rz = sb(E2, "rz", [64, 1024], F32)
            ystg = [sb(E2, "ystg%d" % i, [64, 8, 256], BF16) for i in range(2)]
            P.op("sp", DMA(kb, KbT.rearrange("h p t -> p h t")), writes=["kb"], dma_key="kb")
            P.op("sp", DMA(qb, QbT.rearrange("h p t -> p h t")), writes=["qb"], dma_key="qb")
            Vb_v = Vb.rearrange("(t p) c -> p t c", p=128)
            P.op("sp", [DMA(vb[:, i * 9:(i + 1) * 9, :], Vb_v[:, i * 9:(i + 1) * 9, :]) for i in range(4)], writes=["vb"], dma_key="vb")
            nab_v = nab_d.rearrange("c j p n -> c p j n")
            P.op("sp", DMA(nbias[0], nab_v[2]), writes=[("nbias", 0)], dma_key="nb0")
            ybT_pv = ybT.rearrange("(hp par) d q -> par d hp q", par=2)
            nsteps = [(p, j) for p in range(32) for j in range(5)]

            def na_qk(i):
                p, j = nsteps[i]
                sset = i % 2
                t0 = max(2 * p - 4, 0)
                ktok = (t0 + 2 * j) * 64
                qs = slice(p * 128, (p + 1) * 128)
                fns = []
                for hp in range(4):
                    fns.append(MM(ps[:, 2 * sset, hp * 128:(hp + 1) * 128], kb[0:64, hp, ktok:ktok + 128], qb[0:64, hp, qs]))
                    fns.append(MM(ps[:, 2 * sset + 1, hp * 128:(hp + 1) * 128], kb[64:128, hp, ktok:ktok + 128], qb[64:128, hp, qs]))
                P.op("pe", fns, reads=["kb", "qb"], writes=[("S", sset)])

            def na_av(i):
                p, j = nsteps[i]
                sset = i % 2
                t0 = max(2 * p - 4, 0)
                ktile = t0 // 2 + j
                if p < 2:
                    if j == 0:
                        P.op("sp", DMA(nbias[1], nab_v[p]), writes=[("nbias", 1)], dma_key="nb1")
                    bslot = 1
                else:
                    bslot = 0
                S2 = ps[:, 2 * sset:2 * sset + 2, :]
                pt = pTa[i % 3]
                tm = tmpa[i % 2]
                ptk = ("pTa", i % 3)
                tmk = ("tmpa", i % 2)
                P.op("dve", TT(tm.rearrange("p (a b) -> p a b", a=2), S2, nbias[bslot][:, j, :].rearrange("p (a b) -> p a b", a=2), ALU.add),
                     reads=[("S", sset), ("nbias", bslot)], writes=[tmk])
                P.op("act", ACTF(pt.rearrange("p h q -> p (h q)"), tm, AF.Exp), reads=[tmk], writes=[ptk])
                fns = [MM(ps[0:64, 6, :], ones[:, 0:64], pt[:, 0:4, :].rearrange("p h q -> p (h q)"), start=(j == 0), stop=(j == 4)),
                       MM(ps[0:64, 7, :], ones[:, 0:64], pt[:, 4:8, :].rearrange("p h q -> p (h q)"), start=(j == 0), stop=(j == 4))]
                for hq in range(8):
                    h = 2 * (hq % 4) + hq // 4
                    ob = 4 + hq // 4
                    fns.append(lambda e, hq=hq, h=h, j=j, ob=ob, ktile=ktile, pt=pt: e.matmul(
                        ps[0:64, ob, (hq % 4) * 128:(hq % 4 + 1) * 128], lhsT=vb[:, ktile, h * 64:(h + 1) * 64], rhs=pt[:, hq, :],
                        start=(j == 0 and hq % 4 == 0), stop=(j == 4), skip_group_check=True))
                P.op("pe", fns, reads=[ptk, "vb", "ones"], writes=["O", "Z"])
                if j == 4:
                    yk = ("ystg", (p // 2) % 2)
                    ys = ystg[(p // 2) % 2]
                    P.op("act", ACTF(lnz.rearrange("p (a b) -> p a b", a=2), ps[0:64, 6:8, :], AF.Ln), reads=["Z"], writes=["lnz"])
                    P.op("act", ACTF(rz, lnz, AF.Exp, scale=-1.0), reads=["lnz"], writes=["rz"])
                    for half in range(2):
                        P.op("dve", TT(ys[:, 4 * half:4 * half + 4, (p % 2) * 128:(p % 2 + 1) * 128],
                                       ps[0:64, 4 + half, :].rearrange("p (h q) -> p h q", h=4),
                                       rz[:, half * 512:(half + 1) * 512].rearrange("p (h q) -> p h q", h=4), ALU.mult),
                             reads=["O", "rz", yk], writes=[yk])
                    if p % 2 == 1:
                        q0 = (p - 1) * 128
                        P.op("sp", [DMA(ybT_pv[par, :, :, q0:q0 + 256], ys[:, par * 4:(par + 1) * 4, :]) for par in range(2)],
                             reads=[yk], dma_key=yk)

            na_qk(0)
            for i in range(len(nsteps)):
                if i + 1 < len(nsteps):
                    na_qk(i + 1)
                na_av(i)
            P.barrier()
            P.emit()
            if stop_after == 2:
                return nc

        with contextlib.ExitStack() as E3:
            ktb = [sb(E3, "ktb%d" % i, [128, S], BF16) for i in range(1)]
            vab = [sb(E3, "vab%d" % i, [128, 64, 130], BF16) for i in range(1)]
            qtb = [sb(E3, "qtb%d" % i, [128, HALF], BF16) for i in range(1)]
            btb = sb(E3, "btb", [128, 4, 32], F32)
            bta = sb(E3, "bta", [128, 4, 64], F32)
            dist_b = sb(E3, "dist_b", [128, 32], F32)
            dist_a = sb(E3, "dist_a", [128, 64], F32)
            qp_b = sb(E3, "qp_b", [128, 4], F32)
            qp_a = sb(E3, "qp_a", [128, 4], F32)
            wpb = sb(E3, "wpb", [128, 4, 4], F32)
            wpa = sb(E3, "wpa", [128, 4, 4], F32)
            dabs = sb(E3, "dabs", [128, 4, 512], F32)
            dneg = sb(E3, "dneg", [128, 4, 512], F32)
            identf = sb(E3, "identf", [128, 128], F32)
            identb = sb(E3, "identb", [128, 128], BF16)
            gsubT = sb(E3, "gsubT", [128, 128], F32)
            lamv = sb(E3, "lamv_sb", [128, 4, 64], F32)
            lprod = sb(E3, "lprod", [128, 2, 64], F32)
            lsum = sb(E3, "lsum", [128, 2], F32)
            lexp = sb(E3, "lexp", [128, 2], F32)
            pTd = [sb(E3, "pTd%d" % i, [128, 2, 512], BF16) for i in range(3)]
            tmpd = sb(E3, "tmpd", [128, 2, 512], F32)
            accs = sb(E3, "accs", [128, 4, 2, 129], F32)
            rr = sb(E3, "rr", [128, 4, 2], F32)
            nl = sb(E3, "nl", [128, 4], F32)
            t1 = sb(E3, "t1", [128, 4, 128], F32)
            o_sb = sb(E3, "o_sb", [128, 4, 128], F32)
            junk = sb(E3, "junk", [128, 128], F32)
            ssq = sb(E3, "ssq", [128, 4], F32)
            rs = sb(E3, "rs", [128, 4], F32)
            yq = sb(E3, "yq", [128, 4, 128], BF16)
            ydst = [sb(E3, "ydst%d" % i, [128, 512], BF16) for i in range(2)]

            def iota(out, pattern, base, cm):
                return lambda e: e.iota(out, pattern, base=base, channel_multiplier=cm, allow_small_or_imprecise_dtypes=True)

            P.op("pool", iota(dist_b, [[128, 32]], 0, -1), writes=["dist_b"])
            P.op("pool", iota(dist_a, [[128, 64]], -511, 1), writes=["dist_a"])
            P.op("pool", iota(qp_b, [[128, 4]], 0, 1), writes=["qp_b"])
            P.op("pool", iota(qp_a, [[-128, 4]], 511, -1), writes=["qp_a"])
            P.op("pool", iota(dabs, [[-128, 4], [1, 512]], 0, -1), writes=["dabs"])
            P.op("pool", iota(identf, [[1, 128]], 0, -1), writes=["identf"])
            P.op("dve", TS(dneg, dabs, -1.0, None, ALU.mult), reads=["dabs"], writes=["dneg"])
            P.op("dve", TT(dabs, dabs, dneg, ALU.max), reads=["dabs", "dneg"], writes=["dabs"])
            P.op("dve", TS(identb, identf, 0.0, None, ALU.is_equal), reads=["identf"], writes=["identb"])
            for h in range(4):
                P.op("dve", TS(btb[:, h, :], dist_b, -SLOPES[h], None, ALU.mult), reads=["dist_b"], writes=[("btb", h)])
                P.op("dve", TS(bta[:, h, :], dist_a, -SLOPES[h], None, ALU.mult), reads=["dist_a"], writes=[("bta", h)])
                P.op("act", ACTF(wpb[:, h, :], qp_b, AF.Exp, scale=-SLOPES[h]), reads=["qp_b"], writes=[("wpb", h)])
                P.op("act", ACTF(wpa[:, h, :], qp_a, AF.Exp, scale=-SLOPES[h]), reads=["qp_a"], writes=[("wpa", h)])
            for sl in range(1):
                P.op("pool", lambda e, sl=sl: e.memset(vab[sl][:, :, 128:130], 1.0), writes=[("vab1", sl)])
            P.op("sp", DMA(lamv, lamv_d), writes=["lamv"], dma_key="c3")
            P.op("sp", DMA(gsubT, sublnT_d), writes=["gsubT"], dma_key="c4")
            P.op("dve", TT(lprod[:, 0, :], lamv[:, 0, :], lamv[:, 1, :], ALU.mult), reads=["lamv"], writes=["lprod0"])
            P.op("dve", TT(lprod[:, 1, :], lamv[:, 2, :], lamv[:, 3, :], ALU.mult), reads=["lamv"], writes=["lprod1"])
            P.op("dve", lambda e: e.reduce_sum(lsum, lprod, AX.X), reads=["lprod0", "lprod1"], writes=["lsum"])
            P.op("act", ACTF(lexp, lsum, AF.Exp), reads=["lsum"], writes=["lexp"])
            P.op("dve", TT(neglam, lexp[:, 1:2], lexp[:, 0:1], ALU.subtract), reads=["lexp"], writes=["neglam"])
            P.op("dve", TS(neglam, neglam, -LAMBDA_INIT, None, ALU.add), reads=["neglam"], writes=["neglam"])
            P.op("dve", TS(gsubT, gsubT, 1.0 - LAMBDA_INIT, None, ALU.mult), reads=["gsubT"], writes=["gsubT"])

            Va_v = Va.rearrange("(t p) c -> p t c", p=128)

            def load_head(h):
                sl = 0
                part = "qkv"
                if "k" in part:
                    P.op("sp", DMA(ktb[sl], KaT[h]), writes=[("ktb", sl)], dma_key=("ktb", sl))
                for half in range(2 if "v" in part else 0):
                    P.op("sp", [DMA(vab[sl][:, half * 32 + q * 8:half * 32 + (q + 1) * 8, 0:128],
                                    Va_v[:, half * 32 + q * 8:half * 32 + (q + 1) * 8, h * 128:(h + 1) * 128]) for q in range(4)],
                         writes=[("vab", sl, half)], dma_key=("vab", sl, half))
                if "q" in part:
                    P.op("sp", DMA(qtb[sl], QaT[h]), writes=[("qtb", sl)], dma_key=("qtb", sl))

            steps = []
            for h in range(4):
                for c in range(8):
                    groups = [("b", list(range(0, 4 * c))), ("d", list(range(4 * c, 4 * c + 4))), ("a", list(range(4 * c + 4, 64)))]
                    groups = [g for g in groups if g[1]]
                    for gi, (g, kts) in enumerate(groups):
                        for ki, kt in enumerate(kts):
                            steps.append(dict(h=h, c=c, g=g, kt=kt, first=(ki == 0), last=(ki == len(kts) - 1),
                                              gfirst=(gi == 0), glast=(gi == len(groups) - 1)))

            def qk(i):
                s = steps[i]
                h, c, kt = s["h"], s["c"], s["kt"]
                sl = 0
                sset = i % 2
                P.op("pe", [MM(ps[:, 2 * sset, :], ktb[sl][0:64, kt * 128:(kt + 1) * 128], qtb[sl][0:64, c * 512:(c + 1) * 512]),
                            MM(ps[:, 2 * sset + 1, :], ktb[sl][64:128, kt * 128:(kt + 1) * 128], qtb[sl][64:128, c * 512:(c + 1) * 512])],
                     reads=[("ktb", sl), ("qtb", sl)], writes=[("S", sset)])

            def accv(qs):
                return ps[:, 4 + qs, :].rearrange("p (m c) -> p m c", m=2)[:, :, 0:129]

            def softmax_av(i):
                s = steps[i]
                h, c, kt, g = s["h"], s["c"], s["kt"], s["g"]
                sl = 0
                sset = i % 2
                pslot = i % 3
                pt = pTd[pslot]
                S2 = ps[:, 2 * sset:2 * sset + 2, :]
                if g == "b":
                    m = 4 * c - kt
                    P.op("act", ACTF(pt, S2, AF.Exp, bias=btb[:, h, m:m + 1]), reads=[("S", sset), ("btb", h)], writes=[("pTd", pslot)])
                elif g == "a":
                    m = kt - 4 * c
                    P.op("act", ACTF(pt, S2, AF.Exp, bias=bta[:, h, m:m + 1]), reads=[("S", sset), ("bta", h)], writes=[("pTd", pslot)])
                else:
                    j = kt - 4 * c
                    for mp in range(2):
                        P.op("dve", STT(tmpd[:, mp, :], dabs[:, j, :], -SLOPES[h], ps[:, 2 * sset + mp, :], ALU.mult, ALU.add),
                             reads=[("S", sset), "dabs"], writes=[("tmpd", mp)])
                    P.op("act", ACTF(pt, tmpd, AF.Exp), reads=[("tmpd", 0), ("tmpd", 1)], writes=[("pTd", pslot)])
                first, last = s["first"], s["last"]
                vh = ("vab", sl, kt // 32)
                fns = []
                for qs in range(4):
                    for m in range(2):
                        fns.append(lambda e, qs=qs, m=m, pt=pt, sl=sl, kt=kt, first=first, last=last: e.matmul(
                            ps[:, 4 + qs, m * 256:m * 256 + 129], lhsT=pt[:, m, qs * 128:(qs + 1) * 128], rhs=vab[sl][:, kt, 0:129],
                            start=(first and m == 0), stop=last, skip_group_check=True))
                P.op("pe", fns, reads=[("pTd", pslot), vh, ("vab1", sl)], writes=["acc"])
                if last:
                    for qs in range(4):
                        ak = ("accs", qs)
                        av = accv(qs)
                        dst = accs[:, qs, :, :]
                        if s["gfirst"]:
                            if g == "b":
                                P.op("dve", TS(dst, av, wpb[:, h, qs:qs + 1], None, ALU.mult), reads=["acc", ("wpb", h)], writes=[ak])
                            elif g == "d":
                                P.op("dve", COPY(dst, av), reads=["acc"], writes=[ak])
                            else:
                                P.op("dve", TS(dst, av, wpa[:, h, qs:qs + 1], None, ALU.mult), reads=["acc", ("wpa", h)], writes=[ak])
                        else:
                            if g == "d":
                                P.op("dve", TT(dst, av, dst, ALU.add), reads=["acc", ak], writes=[ak])
                            else:
                                P.op("dve", STT(dst, av, wpa[:, h, qs:qs + 1], dst, ALU.mult, ALU.add), reads=["acc", ("wpa", h), ak], writes=[ak])
                    if s["glast"] and DA_LVL >= 2:
                        finalize(h, c, i)

            fin = {"n": 0}
            pending = []
            allacc = [("accs", qs) for qs in range(4)]

            def finalize(h, c, i):
                ysl = fin["n"] % 2
                fin["n"] += 1
                P.op("dve", RECIP(rr, accs[:, :, :, 128]), reads=allacc, writes=["rr"])
                P.op("dve", TS(nl, rr[:, :, 1], neglam[:, 0:1], None, ALU.mult), reads=["rr", "neglam"], writes=["nl"])
                for qs in range(4):
                    P.op("dve", TS(t1[:, qs, :], accs[:, qs, 0, 0:128], rr[:, qs, 0:1], None, ALU.mult), reads=allacc + ["rr"], writes=[("t1", qs)])
                    P.op("dve", STT(o_sb[:, qs, :], accs[:, qs, 1, 0:128], nl[:, qs:qs + 1], t1[:, qs, :], ALU.mult, ALU.add),
                         reads=allacc + ["nl", ("t1", qs)], writes=[("o", qs)])
                    P.op("dve", lambda e, qs=qs: e.scalar_tensor_tensor(junk, o_sb[:, qs, :], 1.0, o_sb[:, qs, :], ALU.mult, ALU.mult,
                                                                         accum_out=ssq[:, qs:qs + 1]),
                         reads=[("o", qs)], writes=["junk", ("ssq", qs)])
                P.op("dve", TS(rs, ssq, 1.0 / 128, EPS, ALU.mult, ALU.add), reads=[("ssq", qs) for qs in range(4)], writes=["rs"])

                def stage_b(k):
                    P.op("act", ACTF(rs, rs, AF.Ln), reads=["rs"], writes=["rs"])
                    P.op("act", ACTF(rs, rs, AF.Exp, scale=-0.5), reads=["rs"], writes=["rs"])
                    for qs in range(4):
                        P.op("dve", STT(yq[:, qs, :], o_sb[:, qs, :], rs[:, qs:qs + 1], gsubT, ALU.mult, ALU.mult),
                             reads=[("o", qs), "rs", "gsubT"], writes=[("yq", qs)])

                def stage_c(k):
                    sset = k % 2
                    psb = ps[:, 2 * sset, 0:256].bitcast(BF16)
                    P.op("pe", [lambda e, qs=qs, psb=psb: e.transpose(psb[:, qs * 128:(qs + 1) * 128], yq[:, qs, :], identb) for qs in range(4)],
                         reads=[("yq", qs) for qs in range(4)] + ["identb"], writes=[("S", sset)])
                    P.op("dve", COPY(ydst[ysl], psb), reads=[("S", sset)], writes=[("ydst", ysl)])
                    P.op("sp", DMA(yaT[h, :, c * 512:(c + 1) * 512], ydst[ysl]), reads=[("ydst", ysl)], dma_key=("ydst", ysl))

                if DA_LVL >= 3:
                    pending.append((i + 3, stage_b))
                if DA_LVL >= 4:
                    pending.append((i + 6, stage_c))

            import os
            load_head(0)
            n = min(len(steps), int(os.environ.get("DA_STEPS", "100000")))
            DA_LVL = int(os.environ.get("DA_LVL", "9"))
            qk(0)
            for i in range(n):
                boundary = (i + 1 < n and steps[i + 1]["h"] != steps[i]["h"])
                if i + 1 < n and not boundary:
                    qk(i + 1)
                softmax_av(i)
                if boundary:
                    load_head(steps[i + 1]["h"])
                    qk(i + 1)
                while pending and pending[0][0] <= i:
                    pending.pop(0)[1](i)
            while pending:
                pending.pop(0)[1](n - 1)
            P.barrier()
            P.emit()
            if stop_after == 3:
                return nc

        with contextlib.ExitStack() as E4:
            wg = sb(E4, "wg", [128, 8, 2048], BF16)
            wa = sb(E4, "wa", [128, 4, 1024], BF16)
            wb = sb(E4, "wb", [64, 8, 1024], BF16)
            wo = sb(E4, "wo", [128, 8, 1024], BF16)
            xts = [sb(E4, "xta%d" % i, [128, 8, 512], F32) for i in range(2)]
            sq = sb(E4, "sq4", [128, 8, 512], BF16)
            hTs = [sb(E4, "hT4_%d" % i, [128, 8, 512], BF16) for i in range(2)]
            sqn = sb(E4, "sqn4", [128, 8, 512], BF16)
            rt = sb(E4, "rt4", [128, 512], F32)
            rstd = sb(E4, "rstd4", [128, 512], F32)
            yats = [sb(E4, "yat%d" % i, [128, 4, 512], BF16) for i in range(2)]
            ybts = [sb(E4, "ybt%d" % i, [64, 8, 512], BF16) for i in range(2)]
            ga = sb(E4, "ga", [128, 512], F32)
            gb = sb(E4, "gb", [128, 512], F32)
            m1 = sb(E4, "m1", [128, 512], F32)
            m2 = sb(E4, "m2", [128, 512], F32)
            mg = sb(E4, "mg", [128, 8, 512], BF16)
            mo = sb(E4, "mo", [128, 8, 512], F32)
            w_in_v = w_in.rearrange("(kc p) n -> p kc n", p=128)
            cast_load_all([(wg[:, kc, :], w_in_v[:, kc, 3072:5120], 2048) for kc in range(8)], "wg")
            w_a_v = w_a.rearrange("(h p) n -> p h n", p=128)
            w_b_v = w_b.rearrange("(h p) n -> p h n", p=64)
            w_o_v = w_o.rearrange("(kc p) n -> p kc n", p=128)
            cast_load_all([(wa[:, h, :], w_a_v[:, h, :], 1024) for h in range(4)], "wa")
            cast_load_all([(wb[:, h, :], w_b_v[:, h, :], 1024) for h in range(8)], "wb")
            cast_load_all([(wo[:, kc, :], w_o_v[:, kc, :], 1024) for kc in range(8)], "wo")
            xT_v = xT.rearrange("(kc p) t -> p kc t", p=128)
            x1T_v = x1T.rearrange("(kc p) t -> p kc t", p=128)
            yaT_v = yaT.rearrange("h p t -> p h t")
            ybT_v2 = ybT.rearrange("h d q -> d h q")
            wgk = ["wg"]

            def load_x3(t):
                P.op("sp", DMA(xts[t % 2], xT_v[:, :, t * 512:(t + 1) * 512]), writes=[xt_key(t % 2)], dma_key=("xta", t % 2))

            def load_y(t):
                tk = slice(t * 512, (t + 1) * 512)
                P.op("sp", DMA(yats[t % 2], yaT_v[:, :, tk]), writes=[("yat", t % 2)], dma_key=("yat", t % 2))
                P.op("sp", DMA(ybts[t % 2], ybT_v2[:, :, tk]), writes=[("ybt", t % 2)], dma_key=("ybt", t % 2))

            load_x3(0)
            load_y(0)
            load_x3(1)
            rmsnorm_in(xts[0], 0, hTs[0], sqn, rt, rstd, 512, 0, 0, hp=0)
            ob_i = 0
            for t in range(8):
                tok = slice(t * 512, (t + 1) * 512)
                xt = xts[t % 2]
                hT = hTs[t % 2]
                hpar = t % 2
                yat, ybt = yats[t % 2], ybts[t % 2]
                yak, ybk = ("yat", t % 2), ("ybt", t % 2)
                if t + 1 < 8:
                    load_y(t + 1)
                for oc in range(8):
                    if oc == 4 and t + 1 < 8:
                        rmsnorm_in(xts[(t + 1) % 2], 0, hTs[(t + 1) % 2], sqn, rt, rstd, 512, (t + 1) % 2, 0, hp=(t + 1) % 2)
                    ocs = slice(oc * 128, (oc + 1) * 128)
                    P.op("pe", [MM(ps[:, 3, :], wg[:, kc, oc * 128:(oc + 1) * 128], hT[:, kc, :], start=(kc == 0), stop=(kc == 7))
                                for kc in range(8)], reads=[("hT", hpar, kc) for kc in range(8)] + wgk, writes=[("ps", 3)])
                    P.op("pe", [MM(ps[:, 4, :], wg[:, kc, 1024 + oc * 128:1024 + (oc + 1) * 128], hT[:, kc, :], start=(kc == 0), stop=(kc == 7))
                                for kc in range(8)], reads=[("hT", hpar, kc) for kc in range(8)] + wgk, writes=[("ps", 4)])
                    P.op("act", ACTF(ga, ps[:, 3, :], AF.Sigmoid, bias=bgate[:, oc:oc + 1]), reads=[("ps", 3), "bgate"], writes=["ga"])
                    P.op("act", ACTF(gb, ps[:, 4, :], AF.Sigmoid, bias=bgate[:, 8 + oc:9 + oc]), reads=[("ps", 4), "bgate"], writes=["gb"])
                    P.op("pe", [MM(ps[:, 1, :], wa[:, h, ocs], yat[:, h, :], start=(h == 0), stop=(h == 3)) for h in range(4)],
                         reads=[yak, "wa"], writes=[("ps", 1)])
                    P.op("pe", [MM(ps[:, 2, :], wb[:, h, ocs], ybt[:, h, :], start=(h == 0), stop=(h == 7)) for h in range(8)],
                         reads=[ybk, "wb"], writes=[("ps", 2)])
                    P.op("dve", TT(m1, ps[:, 1, :], ga, ALU.mult), reads=[("ps", 1), "ga"], writes=["m1"])
                    P.op("dve", TT(m2, ps[:, 2, :], gb, ALU.mult), reads=[("ps", 2), "gb"], writes=["m2"])
                    P.op("dve", TT(mg[:, oc, :], m1, m2, ALU.add), reads=["m1", "m2"], writes=[("mg", oc)])
                for oc2 in range(8):
                    bank = 5 + (ob_i % 2)
                    ob_i += 1
                    P.op("pe", [MM(ps[:, bank, :], wo[:, oc, oc2 * 128:(oc2 + 1) * 128], mg[:, oc, :], start=(oc == 0), stop=(oc == 7))
                                for oc in range(8)], reads=[("mg", oc) for oc in range(8)] + ["wo"],
                         writes=[("ps", bank)])
                    P.op("act", ACTF(mo[:, oc2, :], ps[:, bank, :], AF.Copy), reads=[("ps", bank)], writes=[("mo", oc2)])
                    P.op("dve", TT(sq[:, oc2, :], mo[:, oc2, :], mo[:, oc2, :], ALU.mult), reads=[("mo", oc2)], writes=[("sq", oc2)])
                P.op("pe", [MM(ps[:, 7, :], ones, sq[:, oc2, :], start=(oc2 == 0), stop=(oc2 == 7)) for oc2 in range(8)],
                     reads=[("sq", k) for k in range(8)] + ["ones"], writes=[("ps", 7)])
                P.op("dve", TS(rt, ps[:, 7, :], 1.0 / D, EPS, ALU.mult, ALU.add), reads=[("ps", 7)], writes=["rt"])
                P.op("act", ACTF(rt, rt, AF.Ln), reads=["rt"], writes=["rt"])
                P.op("act", ACTF(rstd, rt, AF.Exp, scale=-0.5), reads=["rt"], writes=["rstd"])
                for oc2 in range(8):
                    P.op("dve", STT(mo[:, oc2, :], mo[:, oc2, :], gains[:, 1, oc2:oc2 + 1], rstd, ALU.mult, ALU.mult),
                         reads=[("mo", oc2), "rstd", "gains"], writes=[("mo", oc2)])
                    P.op("pool", TT(xt[:, oc2, :], xt[:, oc2, :], mo[:, oc2, :], ALU.add), reads=[("mo", oc2), xt_key(t % 2)],
                         writes=[xt_key(t % 2)])
                P.op("sp", DMA(x1T_v[:, :, tok], xt), reads=[xt_key(t % 2)], dma_key=("x1o", t % 2))
                if t + 2 < 8:
                    load_x3(t + 2)
            P.barrier()
            P.emit()
            if stop_after == 4:
                return nc

        NT = 256
        with contextlib.ExitStack() as E5:
            wgt = sb(E5, "wgt", [128, 8, DFF], BF16)
            wup = sb(E5, "wup", [128, 8, DFF], BF16)
            wdn = sb(E5, "wdn", [128, NJ, 1024], BF16)
            xts = [sb(E5, "xtb%d" % i, [128, 8, NT], F32) for i in range(2)]
            sq = sb(E5, "sq5", [128, 8, NT], BF16)
            hTs = [sb(E5, "hT5_%d" % i, [128, 8, NT], BF16) for i in range(2)]
            sqn = sb(E5, "sqn5", [128, 8, NT], BF16)
            rt = sb(E5, "rt5", [128, NT], F32)
            rstd = sb(E5, "rstd5", [128, NT], F32)
            sg = [sb(E5, "sg%d" % i, [128, NT], F32) for i in range(2)]
            aT = sb(E5, "aT", [128, NJ, NT], BF16)
            fo = sb(E5, "fo", [128, 8, NT], F32)
            wg_v = w_gate.rearrange("(kc p) n -> p kc n", p=128)
            wu_v = w_up.rearrange("(kc p) n -> p kc n", p=128)
            wd_v = w_down.rearrange("(j p) n -> p j n", p=128)
            cast_load_all([(wgt[:, kc, :], wg_v[:, kc, :], DFF) for kc in range(8)], "wgt")
            cast_load_all([(wup[:, kc, :], wu_v[:, kc, :], DFF) for kc in range(8)], "wup")
            cast_load_all([(wdn[:, j, :], wd_v[:, j, :], 1024) for j in range(NJ)], "wdn")
            x1T_v = x1T.rearrange("(kc p) t -> p kc t", p=128)
            outT_v = outT.rearrange("(kc p) t -> p kc t", p=128)
            NTL = HALF // NT

            def load_x4(t):
                P.op("sp", DMA(xts[t % 2], x1T_v[:, :, t * NT:(t + 1) * NT]), writes=[xt_key(t % 2)], dma_key=("xtb", t % 2))

            def pst(idx):
                return ps[:, idx, 0:NT]

            load_x4(0)
            load_x4(1)
            rmsnorm_in(xts[0], 2, hTs[0], sqn, rt, rstd, NT, 0, 0, hp=0)
            gi = 0
            di = 0
            wk_g = ["wgt"]
            wk_u = ["wup"]
            for t in range(NTL):
                xt = xts[t % 2]
                hT = hTs[t % 2]
                hpar = t % 2
                for j in range(NJ):
                    if j == 11 and t + 1 < NTL:
                        rmsnorm_in(xts[(t + 1) % 2], 2, hTs[(t + 1) % 2], sqn, rt, rstd, NT, (t + 1) % 2, 0, hp=(t + 1) % 2)
                    gslot = 1 + (gi % 2)
                    uslot = 3 + (gi % 2)
                    gi += 1
                    js = slice(j * 128, (j + 1) * 128)
                    P.op("pe", [MM(pst(gslot), wgt[:, kc, js], hT[:, kc, :], start=(kc == 0), stop=(kc == 7)) for kc in range(8)],
                         reads=[("hT", hpar, kc) for kc in range(8)] + wk_g, writes=[("pt", gslot)])
                    P.op("pe", [MM(pst(uslot), wup[:, kc, js], hT[:, kc, :], start=(kc == 0), stop=(kc == 7)) for kc in range(8)],
                         reads=[("hT", hpar, kc) for kc in range(8)] + wk_u, writes=[("pt", uslot)])
                    sgs = sg[j % 2]
                    P.op("act", ACTF(sgs, pst(gslot), AF.Silu), reads=[("pt", gslot)], writes=[("sg", j % 2)])
                    P.op("dve", TT(aT[:, j, :], pst(uslot), sgs, ALU.mult), reads=[("pt", uslot), ("sg", j % 2)],
                         writes=[("aT", j)])
                for oc in range(8):
                    dslot = 5 + (di % 2)
                    di += 1
                    P.op("pe", [MM(pst(dslot), wdn[:, j, oc * 128:(oc + 1) * 128], aT[:, j, :], start=(j == 0), stop=(j == NJ - 1))
                                for j in range(NJ)], reads=[("aT", j) for j in range(NJ)] + ["wdn"],
                         writes=[("pt", dslot)])
                    P.op("act", ACTF(fo[:, oc, :], pst(dslot), AF.Copy), reads=[("pt", dslot)], writes=[("fo", oc)])
                    P.op("dve", TT(sq[:, oc, :], fo[:, oc, :], fo[:, oc, :], ALU.mult), reads=[("fo", oc)], writes=[("sq", oc)])
                P.op("pe", [MM(ps[:, 7, :NT], ones, sq[:, oc, :], start=(oc == 0), stop=(oc == 7)) for oc in range(8)],
                     reads=[("sq", k) for k in range(8)] + ["ones"], writes=[("ps", 7)])
                P.op("dve", TS(rt, ps[:, 7, :NT], 1.0 / D, EPS, ALU.mult, ALU.add), reads=[("ps", 7)], writes=["rt"])
                P.op("act", ACTF(rt, rt, AF.Ln), reads=["rt"], writes=["rt"])
                P.op("act", ACTF(rstd, rt, AF.Exp, scale=-0.5), reads=["rt"], writes=["rstd"])
                for oc in range(8):
                    P.op("dve", STT(fo[:, oc, :], fo[:, oc, :], gains[:, 3, oc:oc + 1], rstd, ALU.mult, ALU.mult),
                         reads=[("fo", oc), "rstd", "gains"], writes=[("fo", oc)])
                    P.op("pool", TT(xt[:, oc, :], xt[:, oc, :], fo[:, oc, :], ALU.add), reads=[("fo", oc), xt_key(t % 2)],
                         writes=[xt_key(t % 2)])
                P.op("sp", DMA(outT_v[:, :, t * NT:(t + 1) * NT], xt), reads=[xt_key(t % 2)], dma_key=("outo", t % 2))
                if t + 2 < NTL:
                    load_x4(t + 2)
            P.barrier()
            P.emit()
            if stop_after == 5:
                return nc
    return nc


def _na_bias_tables(rpb, hf):
    out = np.full((3, 5, 128, 8, 2, 64), NEG, dtype=np.float32)
    lc = np.arange(64)
    qc = lc if hf == 0 else 63 - lc
    kc = qc
    cstart = np.clip(qc - 8, 0, 48)
    col_in = (kc[:, None] >= cstart[None, :]) & (kc[:, None] < cstart[None, :] + 16)
    col_off = np.clip(kc[:, None] - qc[None, :] + 15, 0, 30)
    for cls, lr0 in enumerate([0, 2, 4]):
        t0 = max(lr0 - 4, 0)
        for rr in range(2):
            lr = lr0 + rr
            r = lr if hf == 0 else 127 - lr
            rs = min(max(r - 4, 0), 120)
            for j in range(5):
                for i in range(2):
                    krow_l = t0 + 2 * j + i
                    g = krow_l if hf == 0 else 127 - krow_l
                    if not (rs <= g < rs + 8):
                        continue
                    row_off = g - r + 7
                    vals = rpb[:, row_off, :][:, col_off]
                    vals = np.transpose(vals, (1, 0, 2))
                    vals = vals[:, [0, 2, 4, 6, 1, 3, 5, 7], :]
                    blk = np.where(col_in[:, None, :], vals, np.float32(NEG))
                    out[cls, j, i * 64:(i + 1) * 64, :, rr, :] = blk
    return out.reshape(3, 5, 128, 1024)


_CACHE = {}


def kernel(x, pre_mix_w, w_in, b_gate, lambda_q1, lambda_k1, lambda_q2, lambda_k2,
           subln_w, rpb, w_branch_a, w_branch_b, w_out, post_mix_w, pre_ffn_w,
           w_gate, w_up, w_down, post_ffn_w):
    f32 = np.float32
    x = np.asarray(x, dtype=f32)
    if "nc" not in _CACHE:
        _CACHE["nc"] = build_program()
    nc = _CACHE["nc"]

    def vec8(v):
        return np.asarray(v, f32).reshape(8, 128).T

    gains = np.ascontiguousarray(np.stack([vec8(pre_mix_w[0]), vec8(post_mix_w[0]), vec8(pre_ffn_w[0]), vec8(post_ffn_w[0])], axis=1))
    bgate = np.ascontiguousarray(np.asarray(b_gate[0], f32).reshape(16, 128).T)
    subln = np.ascontiguousarray(np.asarray(subln_w[0], f32).reshape(128, 1))
    lamv = np.stack([np.asarray(v[0], f32) for v in (lambda_q1, lambda_k1, lambda_q2, lambda_k2)], axis=0)
    lamv = np.ascontiguousarray(np.broadcast_to(lamv[None], (128, 4, 64)))
    rpb0 = np.asarray(rpb[0], f32)
    nab = [_na_bias_tables(rpb0, 0), _na_bias_tables(rpb0, 1)]
    common = {
        "w_in": np.ascontiguousarray(np.asarray(w_in[0], f32)),
        "w_a": np.ascontiguousarray(np.asarray(w_branch_a[0], f32)),
        "w_b": np.ascontiguousarray(np.asarray(w_branch_b[0], f32)),
        "w_o": np.ascontiguousarray(np.asarray(w_out[0], f32)),
        "w_gate": np.ascontiguousarray(np.asarray(w_gate[0], f32)),
        "w_up": np.ascontiguousarray(np.asarray(w_up[0], f32)),
        "w_down": np.ascontiguousarray(np.asarray(w_down[0], f32)),
        "gains": gains, "bgate": bgate, "subln": subln, "lamv": lamv,
        "sublnT": np.ascontiguousarray(np.broadcast_to(np.asarray(subln_w[0], f32)[None, :], (128, 128))),
    }
    in_maps = []
    for core in range(8):
        b, hf = core // 2, core % 2
        xb = x[b] if hf == 0 else x[b, ::-1]
        m = dict(common)
        m["xT"] = np.ascontiguousarray(xb.T)
        m["nab"] = nab[hf]
        in_maps.append(m)
    res = run_bass_kernel_spmd(nc, in_maps, core_ids=list(range(8)))
    out = np.empty((B, S, D), dtype=f32)
    for core in range(8):
        b, hf = core // 2, core % 2
        o = res.results[core]["outT"].T
        if hf == 0:
            out[b, :HALF] = o
        else:
            out[b, HALF:] = o[::-1]
    return out
```

```python
import contextlib
import math
import numpy as np
import concourse.bass as bass
import concourse.mybir as mybir
from concourse.bass_utils import run_bass_kernel_spmd

F32 = mybir.dt.float32
BF16 = mybir.dt.bfloat16
AF = mybir.ActivationFunctionType
ALU = mybir.AluOpType
AX = mybir.AxisListType

D = 1024
S = 8192
HALF = 4096
B = 4
DFF = 2816
NJ = DFF // 128
INC = 5120
EPS = 1e-6
NEG = -1e30
SLOPES = [2.0 ** (-8.0 * (i + 1) / 4) for i in range(4)]
LAMBDA_INIT = 0.8 - 0.6 * math.exp(-0.3 * 0)
NA_LOC_TOK = 4608

ENGS = ("pe", "act", "dve", "pool", "sp")
SAME_ENGINE_SYNC = ("act", "dve", "pool")


class Prog:
    def __init__(self, nc):
        self.nc = nc
        self.ops = {e: [] for e in ENGS}
        self.eng_cnt = {e: 0 for e in ENGS}
        self.dma_cnt = {}
        self.waited = {e: {} for e in ENGS}
        self.res = {}
        self.sems = {}
        self.nins = 0

    def _sem(self, key):
        if key not in self.sems:
            self.sems[key] = self.nc.alloc_semaphore(name="s%d" % len(self.sems))
        return self.sems[key]

    def op(self, eng, fns, reads=(), writes=(), dma_key=None):
        if callable(fns):
            fns = [fns]
        fns = list(fns)
        need = {}

        def want(t):
            if t is None:
                return
            k, v = t
            if k == ("eng", eng) and eng not in SAME_ENGINE_SYNC:
                return
            if self.waited[eng].get(k, 0) >= v:
                return
            if need.get(k, 0) < v:
                need[k] = v

        for r in reads:
            st = self.res.get(r)
            if st:
                want(st[0])
        for w in writes:
            st = self.res.get(w)
            if st:
                want(st[0])
                for t in st[1]:
                    want(t)
        for k, v in need.items():
            self.waited[eng][k] = v
        if dma_key is not None:
            k = ("dma", dma_key)
            self.dma_cnt[k] = self.dma_cnt.get(k, 0) + 16 * len(fns)
            ticket = (k, self.dma_cnt[k])
            mode = "dma"
        else:
            k = ("eng", eng)
            self.eng_cnt[eng] += 1
            ticket = (k, self.eng_cnt[eng])
            mode = "cmp"
        self._sem(k)
        for kk in need:
            self._sem(kk)
        for r in reads:
            st = self.res.setdefault(r, [None, []])
            st[1].append(ticket)
        for w in writes:
            self.res[w] = [ticket, []]
        self.ops[eng].append((sorted(need.items(), key=str), fns, k, mode))
        self.nins += len(fns)
        return ticket

    def barrier(self):
        allt = []
        for e in ENGS:
            if self.eng_cnt[e]:
                allt.append((("eng", e), self.eng_cnt[e]))
        for k, v in self.dma_cnt.items():
            allt.append((k, v))
        self.res = {"__bar__": [None, list(allt)]}
        self.op("sp", lambda e: e.nop(), writes=["__bar__"])
        for e in ENGS:
            if e != "sp":
                self.op(e, lambda en: en.nop(), reads=["__bar__"])
        self.res = {}

    def _emit_engine(self, name, eng):
        for waits, fns, k, mode in self.ops[name]:
            for sk, v in waits:
                eng.wait_ge(self.sems[sk], v)
            n = len(fns)
            for i, fn in enumerate(fns):
                ins = fn(eng)
                if mode == "dma":
                    ins.then_inc(self.sems[k], 16)
                elif i == n - 1:
                    ins.then_inc(self.sems[k], 1)
        self.ops[name] = []

    def emit(self):
        with self.nc.Block() as block:
            @block.tensor
            def _(e):
                self._emit_engine("pe", e)

            @block.scalar
            def _(e):
                self._emit_engine("act", e)

            @block.vector
            def _(e):
                self._emit_engine("dve", e)

            @block.gpsimd
            def _(e):
                self._emit_engine("pool", e)

            @block.sync
            def _(e):
                self._emit_engine("sp", e)


def MM(out, lhsT, rhs, start=True, stop=True):
    return lambda e: e.matmul(out, lhsT=lhsT, rhs=rhs, start=start, stop=stop)


def DMA(out, in_):
    return lambda e: e.dma_start(out=out, in_=in_)


def TT(out, a, b, op):
    return lambda e: e.tensor_tensor(out, a, b, op)


def TS(out, a, s1, s2, op0, op1=None):
    if op1 is None:
        return lambda e: e.tensor_scalar(out, a, s1, None, op0)
    return lambda e: e.tensor_scalar(out, a, s1, s2, op0, op1)


def STT(out, in0, scalar, in1, op0, op1):
    return lambda e: e.scalar_tensor_tensor(out, in0, scalar, in1, op0, op1)


def ACTF(out, in_, func, bias=None, scale=1.0):
    if bias is None:
        return lambda e: e.activation(out, in_, func, scale=scale)
    return lambda e: e.activation(out, in_, func, bias=bias, scale=scale)


def COPY(out, in_):
    return lambda e: e.tensor_copy(out, in_)


def RECIP(out, in_):
    return lambda e: e.reciprocal(out, in_)


def build_program(stop_after=99, dbg=False):
    nc = bass.Bass("TRN2", target_bir_lowering=False)

    def din(name, shape, dt=F32):
        return nc.dram_tensor(name, list(shape), dt, kind="ExternalInput").ap()

    def dscr(name, shape, dt):
        return nc.dram_tensor(name, list(shape), dt, kind=("ExternalOutput" if dbg else "Internal")).ap()

    xT = din("xT", [D, S])
    w_in = din("w_in", [D, INC])
    w_a = din("w_a", [512, D])
    w_b = din("w_b", [512, D])
    w_o = din("w_o", [D, D])
    w_gate = din("w_gate", [D, DFF])
    w_up = din("w_up", [D, DFF])
    w_down = din("w_down", [DFF, D])
    gains_d = din("gains", [128, 4, 8])
    bgate_d = din("bgate", [128, 16])
    subln_d = din("subln", [128, 1])
    lamv_d = din("lamv", [128, 4, 64])
    sublnT_d = din("sublnT", [128, 128])
    nab_d = din("nab", [3, 5, 128, 1024])
    outT = nc.dram_tensor("outT", [D, HALF], F32, kind="ExternalOutput").ap()

    QaT = dscr("QaT", [4, 128, HALF], BF16)
    KaT = dscr("KaT", [4, 128, S], BF16)
    Va = dscr("Va", [S, 512], BF16)
    QbT = dscr("QbT", [4, 128, HALF], BF16)
    KbT = dscr("KbT", [4, 128, NA_LOC_TOK], BF16)
    Vb = dscr("Vb", [NA_LOC_TOK, 512], BF16)
    yaT = dscr("yaT", [4, 128, HALF], BF16)
    ybT = dscr("ybT", [8, 64, HALF], BF16)
    x1T = dscr("x1T", [D, HALF], F32)

    P = Prog(nc)
    ps = nc.alloc_psum_tensor("ps", [128, 8, 512], F32).ap()

    def sb(es, name, shape, dt):
        return es.enter_context(nc.sbuf_tensor(name, list(shape), dt)).ap()

    with contextlib.ExitStack() as G:
        ones = sb(G, "ones", [128, 128], BF16)
        neghalf = sb(G, "neghalf", [128, 512], F32)
        gains = sb(G, "gains_sb", [128, 4, 8], F32)
        bgate = sb(G, "bgate_sb", [128, 16], F32)
        subln = sb(G, "subln_sb", [128, 1], F32)
        gsub = sb(G, "gsub", [128, 1], F32)
        neglam = sb(G, "neglam", [128, 1], F32)

        P.op("pool", lambda e: e.memset(ones, 1.0), writes=["ones"])
        P.op("pool", lambda e: e.memset(neghalf, -0.5), writes=["neghalf"])
        P.op("sp", DMA(gains, gains_d), writes=["gains"], dma_key="c0")
        P.op("sp", DMA(bgate, bgate_d), writes=["bgate"], dma_key="c1")
        P.op("sp", DMA(subln, subln_d), writes=["subln"], dma_key="c2")

        def cast_fns(dst, src, ncols):
            fns = []
            for c0 in range(0, ncols, 2048):
                c1 = min(ncols, c0 + 2048)
                fns.append(DMA(dst[:, c0:c1], src[:, c0:c1]))
            return fns

        def cast_load_all(pairs, key):
            fns = []
            for dst, src, ncols in pairs:
                fns += cast_fns(dst, src, ncols)
            P.op("pool", fns, writes=[key], dma_key=key)

        def rmsnorm_in(xt, gidx, hT, sq, rt, rstd, N, tag, ss_bank, hp=0):
            P.op("dve", TT(sq[:, :, :N], xt[:, :, :N], xt[:, :, :N], ALU.mult), reads=[xt_key(tag)], writes=[("sq", k) for k in range(8)])
            P.op("pe", [MM(ps[:, ss_bank, :N], ones, sq[:, kc, :N], start=(kc == 0), stop=(kc == 7)) for kc in range(8)],
                 reads=[("sq", k) for k in range(8)] + ["ones"], writes=[("ps", ss_bank)])
            P.op("dve", TS(rt[:, :N], ps[:, ss_bank, :N], 1.0 / D, EPS, ALU.mult, ALU.add),
                 reads=[("ps", ss_bank)], writes=["rt"])
            P.op("act", ACTF(rt[:, :N], rt[:, :N], AF.Ln), reads=["rt"], writes=["rt"])
            P.op("act", ACTF(rstd[:, :N], rt[:, :N], AF.Exp, scale=-0.5), reads=["rt"], writes=["rstd"])
            for kc in range(8):
                P.op("dve", STT(hT[:, kc, :N], xt[:, kc, :N], gains[:, gidx, kc:kc + 1], rstd[:, :N], ALU.mult, ALU.mult),
                     reads=[xt_key(tag), "rstd", "gains"], writes=[("hT", hp, kc)])

        def xt_key(tag):
            return ("xt", tag)

        with contextlib.ExitStack() as E1:
            win = sb(E1, "win", [128, 8, 3072], BF16)
            xts = [sb(E1, "xt%d" % i, [128, 8, 512], F32) for i in range(2)]
            sq = sb(E1, "sq1", [128, 8, 512], BF16)
            hTs = [sb(E1, "hT1_%d" % i, [128, 8, 512], BF16) for i in range(2)]
            cur = {"hT": hTs[0], "hp": 0}
            rt = sb(E1, "rt1", [128, 512], F32)
            rstd = sb(E1, "rstd1", [128, 512], F32)
            NSTG = 6
            stg = [sb(E1, "stg%d" % i, [128, 512], BF16) for i in range(NSTG)]
            w_in_v = w_in.rearrange("(kc p) n -> p kc n", p=128)
            cast_load_all([(win[:, kc, :], w_in_v[:, kc, 0:3072], 3072) for kc in range(8)], "win")
            xT_v = xT.rearrange("(kc p) t -> p kc t", p=128)
            banks = [1, 2, 3, 4, 5, 6, 7]
            st = {"b": 0, "s": 0, "ev": 0}

            def load_x(t):
                P.op("sp", DMA(xts[t % 2], xT_v[:, :, t * 512:(t + 1) * 512]), writes=[xt_key(t % 2)], dma_key=("xt", t % 2))

            def evac_store(bank, dst, scale=None):
                k = st["s"] % NSTG
                st["s"] += 1
                eng = "act" if st["ev"] % 2 == 0 else "dve"
                st["ev"] += 1
                if eng == "act":
                    fn = ACTF(stg[k], ps[:, bank, :], AF.Copy, scale=(1.0 if scale is None else scale))
                else:
                    if scale is None:
                        fn = COPY(stg[k], ps[:, bank, :])
                    else:
                        fn = TS(stg[k], ps[:, bank, :], scale, None, ALU.mult)
                P.op(eng, fn, reads=[("ps", bank)], writes=[("stg", k)])
                P.op("sp", DMA(dst, stg[k]), reads=[("stg", k)], dma_key=("stg", k))

            def fm_chunk(col0, dst, scale=None):
                bank = banks[st["b"] % len(banks)]
                st["b"] += 1
                hT, hp = cur["hT"], cur["hp"]
                P.op("pe", [MM(ps[:, bank, :], win[:, kc, col0:col0 + 128], hT[:, kc, :], start=(kc == 0), stop=(kc == 7))
                            for kc in range(8)],
                     reads=[("hT", hp, kc) for kc in range(8)] + ["win"], writes=[("ps", bank)])
                evac_store(bank, dst, scale)

            def tm_chunk(col0, sub, dst):
                bank = banks[st["b"] % len(banks)]
                st["b"] += 1
                hT, hp = cur["hT"], cur["hp"]
                P.op("pe", [MM(ps[:, bank, :], hT[:, kc, sub * 128:(sub + 1) * 128], win[:, kc, col0:col0 + 512],
                               start=(kc == 0), stop=(kc == 7)) for kc in range(8)],
                     reads=[("hT", hp, kc) for kc in range(8)] + ["win"], writes=[("ps", bank)])
                evac_store(bank, dst)

            load_x(0)
            load_x(1)
            rmsnorm_in(xts[0], 0, hTs[0], sq, rt, rstd, 512, 0, 0, hp=0)
            for t in range(16):
                tok = slice(t * 512, (t + 1) * 512)
                mine = t < 8
                nb = t <= 8
                work = []
                for h in range(4):
                    if mine:
                        work.append(lambda h=h, tok=tok: fm_chunk(h * 128, QaT[h, :, tok], scale=0.125))
                    work.append(lambda h=h, tok=tok: fm_chunk(512 + h * 128, KaT[h, :, tok]))
                for sub in range(4):
                    work.append(lambda sub=sub, t=t: tm_chunk(1024, sub, Va[t * 512 + sub * 128: t * 512 + (sub + 1) * 128, :]))
                if nb:
                    for hp_ in range(4):
                        if mine:
                            work.append(lambda hp_=hp_, tok=tok: fm_chunk(1536 + hp_ * 128, QbT[hp_, :, tok], scale=0.125))
                        work.append(lambda hp_=hp_, tok=tok: fm_chunk(2048 + hp_ * 128, KbT[hp_, :, tok]))
                    for sub in range(4):
                        work.append(lambda sub=sub, t=t: tm_chunk(2560, sub, Vb[t * 512 + sub * 128: t * 512 + (sub + 1) * 128, :]))
                cur["hT"], cur["hp"] = hTs[t % 2], t % 2
                half = len(work) // 2
                for w in work[:half]:
                    w()
                if t + 1 < 16:
                    rmsnorm_in(xts[(t + 1) % 2], 0, hTs[(t + 1) % 2], sq, rt, rstd, 512, (t + 1) % 2, 0, hp=(t + 1) % 2)
                for w in work[half:]:
                    w()
                if t + 2 < 16:
                    load_x(t + 2)
            P.barrier()
            P.emit()
            if stop_after == 1:
                return nc

        with contextlib.ExitStack() as E2:
            kb = sb(E2, "kb", [128, 4, NA_LOC_TOK], BF16)
            vb = sb(E2, "vb", [128, NA_LOC_TOK // 128, 512], BF16)
            qb = sb(E2, "qb", [128, 4, HALF], BF16)
            nbias = [sb(E2, "nbias%d" % i, [128, 5, 1024], F32) for i in range(2)]
            tmpa = [sb(E2, "tmpa%d" % i, [128, 1024], F32) for i in range(2)]
            pTa = [sb(E2, "pTa%d" % i, [128, 8, 128], BF16) for i in range(3)]
            lnz = sb(E2, "lnz", [64, 1024], F32)
            rz = sb(E2, "rz", [64, 1024], F32)
            ystg = [sb(E2, "ystg%d" % i, [64, 8, 256], BF16) for i in range(2)]
            P.op("sp", DMA(kb, KbT.rearrange("h p t -> p h t")), writes=["kb"], dma_key="kb")
            P.op("sp", DMA(qb, QbT.rearrange("h p t -> p h t")), writes=["qb"], dma_key="qb")
            Vb_v = Vb.rearrange("(t p) c -> p t c", p=128)
            P.op("sp", [DMA(vb[:, i * 9:(i + 1) * 9, :], Vb_v[:, i * 9:(i + 1) * 9, :]) for i in range(4)], writes=["vb"], dma_key="vb")
            nab_v = nab_d.rearrange("c j p n -> c p j n")
            P.op("sp", DMA(nbias[0], nab_v[2]), writes=[("nbias", 0)], dma_key="nb0")
            ybT_pv = ybT.rearrange("(hp par) d q -> par d hp q", par=2)
            nsteps = [(p, j) for p in range(32) for j in range(5)]

            def na_qk(i):
                p, j = nsteps[i]
                sset = i % 2
                t0 = max(2 * p - 4, 0)
                ktok = (t0 + 2 * j) * 64
                qs = slice(p * 128, (p + 1) * 128)
                fns = []
                for hp in range(4):
                    fns.append(MM(ps[:, 2 * sset, hp * 128:(hp + 1) * 128], kb[0:64, hp, ktok:ktok + 128], qb[0:64, hp, qs]))
                    fns.append(MM(ps[:, 2 * sset + 1, hp * 128:(hp + 1) * 128], kb[64:128, hp, ktok:ktok + 128], qb[64:128, hp, qs]))
                P.op("pe", fns, reads=["kb", "qb"], writes=[("S", sset)])

            def na_soft(i):
                p, j = nsteps[i]
                sset = i % 2
                if p < 2:
                    if j == 0:
                        P.op("sp", DMA(nbias[1], nab_v[p]), writes=[("nbias", 1)], dma_key="nb1")
                    bslot = 1
                else:
                    bslot = 0
                S2 = ps[:, 2 * sset:2 * sset + 2, :]
                pt = pTa[i % 3]
                tm = tmpa[i % 2]
                ptk = ("pTa", i % 3)
                tmk = ("tmpa", i % 2)
                P.op("dve", TT(tm.rearrange("p (a b) -> p a b", a=2), S2, nbias[bslot][:, j, :].rearrange("p (a b) -> p a b", a=2), ALU.add),
                     reads=[("S", sset), ("nbias", bslot)], writes=[tmk])
                P.op("act", ACTF(pt.rearrange("p h q -> p (h q)"), tm, AF.Exp), reads=[tmk], writes=[ptk])

            def na_av(i):
                p, j = nsteps[i]
                t0 = max(2 * p - 4, 0)
                ktile = t0 // 2 + j
                pt = pTa[i % 3]
                ptk = ("pTa", i % 3)
                fns = [MM(ps[0:64, 6, :], ones[:, 0:64], pt[:, 0:4, :].rearrange("p h q -> p (h q)"), start=(j == 0), stop=(j == 4)),
                       MM(ps[0:64, 7, :], ones[:, 0:64], pt[:, 4:8, :].rearrange("p h q -> p (h q)"), start=(j == 0), stop=(j == 4))]
                for hq in range(8):
                    h = 2 * (hq % 4) + hq // 4
                    ob = 4 + hq // 4
                    fns.append(lambda e, hq=hq, h=h, j=j, ob=ob, ktile=ktile, pt=pt: e.matmul(
                        ps[0:64, ob, (hq % 4) * 128:(hq % 4 + 1) * 128], lhsT=vb[:, ktile, h * 64:(h + 1) * 64], rhs=pt[:, hq, :],
                        start=(j == 0 and hq % 4 == 0), stop=(j == 4), skip_group_check=True))
                P.op("pe", fns, reads=[ptk, "vb", "ones"], writes=["O", "Z"])
                if j == 4:
                    yk = ("ystg", (p // 2) % 2)
                    ys = ystg[(p // 2) % 2]
                    P.op("act", ACTF(lnz.rearrange("p (a b) -> p a b", a=2), ps[0:64, 6:8, :], AF.Ln), reads=["Z"], writes=["lnz"])
                    P.op("act", ACTF(rz, lnz, AF.Exp, scale=-1.0), reads=["lnz"], writes=["rz"])
                    for half in range(2):
                        P.op("dve", TT(ys[:, 4 * half:4 * half + 4, (p % 2) * 128:(p % 2 + 1) * 128],
                                       ps[0:64, 4 + half, :].rearrange("p (h q) -> p h q", h=4),
                                       rz[:, half * 512:(half + 1) * 512].rearrange("p (h q) -> p h q", h=4), ALU.mult),
                             reads=["O", "rz", yk], writes=[yk])
                    if p % 2 == 1:
                        q0 = (p - 1) * 128
                        P.op("sp", [DMA(ybT_pv[par, :, :, q0:q0 + 256], ys[:, par * 4:(par + 1) * 4, :]) for par in range(2)],
                             reads=[yk], dma_key=yk)

            na_qk(0)
            na_qk(1)
            for i in range(len(nsteps)):
                na_soft(i)
                if i + 2 < len(nsteps):
                    na_qk(i + 2)
                na_av(i)
            P.barrier()
            P.emit()
            if stop_after == 2:
                return nc

        with contextlib.ExitStack() as E3:
            ktb = [sb(E3, "ktb%d" % i, [128, S], BF16) for i in range(1)]
            vab = [sb(E3, "vab%d" % i, [128, 64, 130], BF16) for i in range(1)]
            qtb = [sb(E3, "qtb%d" % i, [128, HALF], BF16) for i in range(1)]
            btb = sb(E3, "btb", [128, 4, 32], F32)
            bta = sb(E3, "bta", [128, 4, 64], F32)
            dist_b = sb(E3, "dist_b", [128, 32], F32)
            dist_a = sb(E3, "dist_a", [128, 64], F32)
            qp_b = sb(E3, "qp_b", [128, 4], F32)
            qp_a = sb(E3, "qp_a", [128, 4], F32)
            wpb = sb(E3, "wpb", [128, 4, 4], F32)
            wpa = sb(E3, "wpa", [128, 4, 4], F32)
            dabs = sb(E3, "dabs", [128, 4, 512], F32)
            dneg = sb(E3, "dneg", [128, 4, 512], F32)
            identf = sb(E3, "identf", [128, 128], F32)
            identb = sb(E3, "identb", [128, 128], BF16)
            gsubT = sb(E3, "gsubT", [128, 128], F32)
            lamv = sb(E3, "lamv_sb", [128, 4, 64], F32)
            lprod = sb(E3, "lprod", [128, 2, 64], F32)
            lsum = sb(E3, "lsum", [128, 2], F32)
            lexp = sb(E3, "lexp", [128, 2], F32)
            pTd = [sb(E3, "pTd%d" % i, [128, 2, 512], BF16) for i in range(3)]
            tmpds = [sb(E3, "tmpd%d" % i, [128, 2, 512], F32) for i in range(2)]
            accs = sb(E3, "accs", [128, 4, 2, 129], F32)
            rr = sb(E3, "rr", [128, 4, 2], F32)
            nl = sb(E3, "nl", [128, 4], F32)
            t1 = sb(E3, "t1", [128, 4, 128], F32)
            o_sb = sb(E3, "o_sb", [128, 4, 128], F32)
            junk = sb(E3, "junk", [128, 128], F32)
            ssq = sb(E3, "ssq", [128, 4], F32)
            rs = sb(E3, "rs", [128, 4], F32)
            yq = sb(E3, "yq", [128, 4, 128], BF16)
            ydst = [sb(E3, "ydst%d" % i, [128, 512], BF16) for i in range(2)]

            def iota(out, pattern, base, cm):
                return lambda e: e.iota(out, pattern, base=base, channel_multiplier=cm, allow_small_or_imprecise_dtypes=True)

            P.op("pool", iota(dist_b, [[128, 32]], 0, -1), writes=["dist_b"])
            P.op("pool", iota(dist_a, [[128, 64]], -511, 1), writes=["dist_a"])
            P.op("pool", iota(qp_b, [[128, 4]], 0, 1), writes=["qp_b"])
            P.op("pool", iota(qp_a, [[-128, 4]], 511, -1), writes=["qp_a"])
            P.op("pool", iota(dabs, [[-128, 4], [1, 512]], 0, -1), writes=["dabs"])
            P.op("pool", iota(identf, [[1, 128]], 0, -1), writes=["identf"])
            P.op("dve", TS(dneg, dabs, -1.0, None, ALU.mult), reads=["dabs"], writes=["dneg"])
            P.op("dve", TT(dabs, dabs, dneg, ALU.max), reads=["dabs", "dneg"], writes=["dabs"])
            P.op("dve", TS(identb, identf, 0.0, None, ALU.is_equal), reads=["identf"], writes=["identb"])
            for h in range(4):
                P.op("dve", TS(btb[:, h, :], dist_b, -SLOPES[h], None, ALU.mult), reads=["dist_b"], writes=[("btb", h)])
                P.op("dve", TS(bta[:, h, :], dist_a, -SLOPES[h], None, ALU.mult), reads=["dist_a"], writes=[("bta", h)])
                P.op("act", ACTF(wpb[:, h, :], qp_b, AF.Exp, scale=-SLOPES[h]), reads=["qp_b"], writes=[("wpb", h)])
                P.op("act", ACTF(wpa[:, h, :], qp_a, AF.Exp, scale=-SLOPES[h]), reads=["qp_a"], writes=[("wpa", h)])
            for sl in range(1):
                P.op("pool", lambda e, sl=sl: e.memset(vab[sl][:, :, 128:130], 1.0), writes=[("vab1", sl)])
            P.op("sp", DMA(lamv, lamv_d), writes=["lamv"], dma_key="c3")
            P.op("sp", DMA(gsubT, sublnT_d), writes=["gsubT"], dma_key="c4")
            P.op("dve", TT(lprod[:, 0, :], lamv[:, 0, :], lamv[:, 1, :], ALU.mult), reads=["lamv"], writes=["lprod0"])
            P.op("dve", TT(lprod[:, 1, :], lamv[:, 2, :], lamv[:, 3, :], ALU.mult), reads=["lamv"], writes=["lprod1"])
            P.op("dve", lambda e: e.reduce_sum(lsum, lprod, AX.X), reads=["lprod0", "lprod1"], writes=["lsum"])
            P.op("act", ACTF(lexp, lsum, AF.Exp), reads=["lsum"], writes=["lexp"])
            P.op("dve", TT(neglam, lexp[:, 1:2], lexp[:, 0:1], ALU.subtract), reads=["lexp"], writes=["neglam"])
            P.op("dve", TS(neglam, neglam, -LAMBDA_INIT, None, ALU.add), reads=["neglam"], writes=["neglam"])
            P.op("dve", TS(gsubT, gsubT, 1.0 - LAMBDA_INIT, None, ALU.mult), reads=["gsubT"], writes=["gsubT"])

            Va_v = Va.rearrange("(t p) c -> p t c", p=128)

            def load_head(h):
                sl = 0
                part = "qkv"
                if "k" in part:
                    P.op("sp", DMA(ktb[sl], KaT[h]), writes=[("ktb", sl)], dma_key=("ktb", sl))
                for half in range(2 if "v" in part else 0):
                    P.op("sp", [DMA(vab[sl][:, half * 32 + q * 8:half * 32 + (q + 1) * 8, 0:128],
                                    Va_v[:, half * 32 + q * 8:half * 32 + (q + 1) * 8, h * 128:(h + 1) * 128]) for q in range(4)],
                         writes=[("vab", sl, half)], dma_key=("vab", sl, half))
                if "q" in part:
                    P.op("sp", DMA(qtb[sl], QaT[h]), writes=[("qtb", sl)], dma_key=("qtb", sl))

            steps = []
            for h in range(4):
                for c in range(8):
                    groups = [("b", list(range(0, 4 * c))), ("d", list(range(4 * c, 4 * c + 4))), ("a", list(range(4 * c + 4, 64)))]
                    groups = [g for g in groups if g[1]]
                    for gi, (g, kts) in enumerate(groups):
                        for ki, kt in enumerate(kts):
                            steps.append(dict(h=h, c=c, g=g, kt=kt, first=(ki == 0), last=(ki == len(kts) - 1),
                                              gfirst=(gi == 0), glast=(gi == len(groups) - 1)))

            def qk(i):
                s = steps[i]
                h, c, kt = s["h"], s["c"], s["kt"]
                sl = 0
                sset = i % 2
                P.op("pe", [MM(ps[:, 2 * sset, :], ktb[sl][0:64, kt * 128:(kt + 1) * 128], qtb[sl][0:64, c * 512:(c + 1) * 512]),
                            MM(ps[:, 2 * sset + 1, :], ktb[sl][64:128, kt * 128:(kt + 1) * 128], qtb[sl][64:128, c * 512:(c + 1) * 512])],
                     reads=[("ktb", sl), ("qtb", sl)], writes=[("S", sset)])

            def accv(qs):
                return ps[:, 4 + qs, :].rearrange("p (m c) -> p m c", m=2)[:, :, 0:129]

            def soft(i):
                s = steps[i]
                h, c, kt, g = s["h"], s["c"], s["kt"], s["g"]
                sl = 0
                sset = i % 2
                pslot = i % 3
                pt = pTd[pslot]
                S2 = ps[:, 2 * sset:2 * sset + 2, :]
                if g == "b":
                    m = 4 * c - kt
                    P.op("act", ACTF(pt, S2, AF.Exp, bias=btb[:, h, m:m + 1]), reads=[("S", sset), ("btb", h)], writes=[("pTd", pslot)])
                elif g == "a":
                    m = kt - 4 * c
                    P.op("act", ACTF(pt, S2, AF.Exp, bias=bta[:, h, m:m + 1]), reads=[("S", sset), ("bta", h)], writes=[("pTd", pslot)])
                else:
                    j = kt - 4 * c
                    tmpd = tmpds[j % 2]
                    for mp in range(2):
                        P.op("dve", STT(tmpd[:, mp, :], dabs[:, j, :], -SLOPES[h], ps[:, 2 * sset + mp, :], ALU.mult, ALU.add),
                             reads=[("S", sset), "dabs"], writes=[("tmpd", j % 2, mp)])
                    P.op("act", ACTF(pt, tmpd, AF.Exp), reads=[("tmpd", j % 2, 0), ("tmpd", j % 2, 1)], writes=[("pTd", pslot)])

            def av(i):
                s = steps[i]
                h, c, kt, g = s["h"], s["c"], s["kt"], s["g"]
                sl = 0
                pslot = i % 3
                pt = pTd[pslot]
                first, last = s["first"], s["last"]
                vh = ("vab", sl, kt // 32)
                fns = []
                for qs in range(4):
                    for m in range(2):
                        fns.append(lambda e, qs=qs, m=m, pt=pt, sl=sl, kt=kt, first=first, last=last: e.matmul(
                            ps[:, 4 + qs, m * 256:m * 256 + 129], lhsT=pt[:, m, qs * 128:(qs + 1) * 128], rhs=vab[sl][:, kt, 0:129],
                            start=(first and m == 0), stop=last, skip_group_check=True))
                P.op("pe", fns, reads=[("pTd", pslot), vh, ("vab1", sl)], writes=["acc"])
                if last:
                    for qs in range(4):
                        ak = ("accs", qs)
                        av = accv(qs)
                        dst = accs[:, qs, :, :]
                        if s["gfirst"]:
                            if g == "b":
                                P.op("dve", TS(dst, av, wpb[:, h, qs:qs + 1], None, ALU.mult), reads=["acc", ("wpb", h)], writes=[ak])
                            elif g == "d":
                                P.op("dve", COPY(dst, av), reads=["acc"], writes=[ak])
                            else:
                                P.op("dve", TS(dst, av, wpa[:, h, qs:qs + 1], None, ALU.mult), reads=["acc", ("wpa", h)], writes=[ak])
                        else:
                            if g == "d":
                                P.op("dve", TT(dst, av, dst, ALU.add), reads=["acc", ak], writes=[ak])
                            else:
                                P.op("dve", STT(dst, av, wpa[:, h, qs:qs + 1], dst, ALU.mult, ALU.add), reads=["acc", ("wpa", h), ak], writes=[ak])
                    if s["glast"] and DA_LVL >= 2:
                        finalize(h, c, i)

            fin = {"n": 0}
            pending = []
            allacc = [("accs", qs) for qs in range(4)]

            def finalize(h, c, i):
                ysl = fin["n"] % 2
                fin["n"] += 1
                P.op("dve", RECIP(rr, accs[:, :, :, 128]), reads=allacc, writes=["rr"])
                P.op("dve", TS(nl, rr[:, :, 1], neglam[:, 0:1], None, ALU.mult), reads=["rr", "neglam"], writes=["nl"])
                for qs in range(4):
                    P.op("dve", TS(t1[:, qs, :], accs[:, qs, 0, 0:128], rr[:, qs, 0:1], None, ALU.mult), reads=allacc + ["rr"], writes=[("t1", qs)])
                    P.op("dve", STT(o_sb[:, qs, :], accs[:, qs, 1, 0:128], nl[:, qs:qs + 1], t1[:, qs, :], ALU.mult, ALU.add),
                         reads=allacc + ["nl", ("t1", qs)], writes=[("o", qs)])
                    P.op("dve", lambda e, qs=qs: e.scalar_tensor_tensor(junk, o_sb[:, qs, :], 1.0, o_sb[:, qs, :], ALU.mult, ALU.mult,
                                                                         accum_out=ssq[:, qs:qs + 1]),
                         reads=[("o", qs)], writes=["junk", ("ssq", qs)])
                P.op("dve", TS(rs, ssq, 1.0 / 128, EPS, ALU.mult, ALU.add), reads=[("ssq", qs) for qs in range(4)], writes=["rs"])

                def stage_b(k):
                    P.op("act", ACTF(rs, rs, AF.Ln), reads=["rs"], writes=["rs"])
                    P.op("act", ACTF(rs, rs, AF.Exp, scale=-0.5), reads=["rs"], writes=["rs"])
                    for qs in range(4):
                        P.op("dve", STT(yq[:, qs, :], o_sb[:, qs, :], rs[:, qs:qs + 1], gsubT, ALU.mult, ALU.mult),
                             reads=[("o", qs), "rs", "gsubT"], writes=[("yq", qs)])

                def stage_c(k):
                    sset = k % 2
                    psb = ps[:, 2 * sset, 0:256].bitcast(BF16)
                    P.op("pe", [lambda e, qs=qs, psb=psb: e.transpose(psb[:, qs * 128:(qs + 1) * 128], yq[:, qs, :], identb) for qs in range(4)],
                         reads=[("yq", qs) for qs in range(4)] + ["identb"], writes=[("S", sset)])
                    P.op("dve", COPY(ydst[ysl], psb), reads=[("S", sset)], writes=[("ydst", ysl)])
                    P.op("sp", DMA(yaT[h, :, c * 512:(c + 1) * 512], ydst[ysl]), reads=[("ydst", ysl)], dma_key=("ydst", ysl))

                if DA_LVL >= 3:
                    pending.append((i + 3, stage_b))
                if DA_LVL >= 4:
                    pending.append((i + 6, stage_c))

            import os
            load_head(0)
            n = min(len(steps), int(os.environ.get("DA_STEPS", "100000")))
            DA_LVL = int(os.environ.get("DA_LVL", "9"))
            issued = 0
            for i in range(n):
                while issued <= min(i + 1, n - 1) and steps[issued]["h"] == steps[i]["h"]:
                    qk(issued)
                    issued += 1
                soft(i)
                av(i)
                if i + 1 < n and steps[i + 1]["h"] != steps[i]["h"]:
                    load_head(steps[i + 1]["h"])
                while pending and pending[0][0] <= i:
                    pending.pop(0)[1](i)
            while pending:
                pending.pop(0)[1](n - 1)
            P.barrier()
            P.emit()
            if stop_after == 3:
                return nc

        with contextlib.ExitStack() as E4:
            wg = sb(E4, "wg", [128, 8, 2048], BF16)
            wa = sb(E4, "wa", [128, 4, 1024], BF16)
            wb = sb(E4, "wb", [64, 8, 1024], BF16)
            wo = sb(E4, "wo", [128, 8, 1024], BF16)
            xts = [sb(E4, "xta%d" % i, [128, 8, 512], F32) for i in range(2)]
            sq = sb(E4, "sq4", [128, 8, 512], BF16)
            hTs = [sb(E4, "hT4_%d" % i, [128, 8, 512], BF16) for i in range(2)]
            sqn = sb(E4, "sqn4", [128, 8, 512], BF16)
            rt = sb(E4, "rt4", [128, 512], F32)
            rstd = sb(E4, "rstd4", [128, 512], F32)
            yats = [sb(E4, "yat%d" % i, [128, 4, 512], BF16) for i in range(2)]
            ybts = [sb(E4, "ybt%d" % i, [64, 8, 512], BF16) for i in range(2)]
            ga = sb(E4, "ga", [128, 512], F32)
            gb = sb(E4, "gb", [128, 512], F32)
            m1 = sb(E4, "m1", [128, 512], F32)
            m2 = sb(E4, "m2", [128, 512], F32)
            mg = sb(E4, "mg", [128, 8, 512], BF16)
            mo = sb(E4, "mo", [128, 8, 512], F32)
            w_in_v = w_in.rearrange("(kc p) n -> p kc n", p=128)
            cast_load_all([(wg[:, kc, :], w_in_v[:, kc, 3072:5120], 2048) for kc in range(8)], "wg")
            w_a_v = w_a.rearrange("(h p) n -> p h n", p=128)
            w_b_v = w_b.rearrange("(h p) n -> p h n", p=64)
            w_o_v = w_o.rearrange("(kc p) n -> p kc n", p=128)
            cast_load_all([(wa[:, h, :], w_a_v[:, h, :], 1024) for h in range(4)], "wa")
            cast_load_all([(wb[:, h, :], w_b_v[:, h, :], 1024) for h in range(8)], "wb")
            cast_load_all([(wo[:, kc, :], w_o_v[:, kc, :], 1024) for kc in range(8)], "wo")
            xT_v = xT.rearrange("(kc p) t -> p kc t", p=128)
            x1T_v = x1T.rearrange("(kc p) t -> p kc t", p=128)
            yaT_v = yaT.rearrange("h p t -> p h t")
            ybT_v2 = ybT.rearrange("h d q -> d h q")
            wgk = ["wg"]

            def load_x3(t):
                P.op("sp", DMA(xts[t % 2], xT_v[:, :, t * 512:(t + 1) * 512]), writes=[xt_key(t % 2)], dma_key=("xta", t % 2))

            def load_y(t):
                tk = slice(t * 512, (t + 1) * 512)
                P.op("sp", DMA(yats[t % 2], yaT_v[:, :, tk]), writes=[("yat", t % 2)], dma_key=("yat", t % 2))
                P.op("sp", DMA(ybts[t % 2], ybT_v2[:, :, tk]), writes=[("ybt", t % 2)], dma_key=("ybt", t % 2))

            load_x3(0)
            load_y(0)
            load_x3(1)
            rmsnorm_in(xts[0], 0, hTs[0], sqn, rt, rstd, 512, 0, 0, hp=0)
            ob_i = 0
            for t in range(8):
                tok = slice(t * 512, (t + 1) * 512)
                xt = xts[t % 2]
                hT = hTs[t % 2]
                hpar = t % 2
                yat, ybt = yats[t % 2], ybts[t % 2]
                yak, ybk = ("yat", t % 2), ("ybt", t % 2)
                if t + 1 < 8:
                    load_y(t + 1)
                for oc in range(8):
                    if oc == 4 and t + 1 < 8:
                        rmsnorm_in(xts[(t + 1) % 2], 0, hTs[(t + 1) % 2], sqn, rt, rstd, 512, (t + 1) % 2, 0, hp=(t + 1) % 2)
                    ocs = slice(oc * 128, (oc + 1) * 128)
                    P.op("pe", [MM(ps[:, 3, :], wg[:, kc, oc * 128:(oc + 1) * 128], hT[:, kc, :], start=(kc == 0), stop=(kc == 7))
                                for kc in range(8)], reads=[("hT", hpar, kc) for kc in range(8)] + wgk, writes=[("ps", 3)])
                    P.op("pe", [MM(ps[:, 4, :], wg[:, kc, 1024 + oc * 128:1024 + (oc + 1) * 128], hT[:, kc, :], start=(kc == 0), stop=(kc == 7))
                                for kc in range(8)], reads=[("hT", hpar, kc) for kc in range(8)] + wgk, writes=[("ps", 4)])
                    P.op("act", ACTF(ga, ps[:, 3, :], AF.Sigmoid, bias=bgate[:, oc:oc + 1]), reads=[("ps", 3), "bgate"], writes=["ga"])
                    P.op("act", ACTF(gb, ps[:, 4, :], AF.Sigmoid, bias=bgate[:, 8 + oc:9 + oc]), reads=[("ps", 4), "bgate"], writes=["gb"])
                    P.op("pe", [MM(ps[:, 1, :], wa[:, h, ocs], yat[:, h, :], start=(h == 0), stop=(h == 3)) for h in range(4)],
                         reads=[yak, "wa"], writes=[("ps", 1)])
                    P.op("pe", [MM(ps[:, 2, :], wb[:, h, ocs], ybt[:, h, :], start=(h == 0), stop=(h == 7)) for h in range(8)],
                         reads=[ybk, "wb"], writes=[("ps", 2)])
                    P.op("dve", TT(m1, ps[:, 1, :], ga, ALU.mult), reads=[("ps", 1), "ga"], writes=["m1"])
                    P.op("dve", TT(m2, ps[:, 2, :], gb, ALU.mult), reads=[("ps", 2), "gb"], writes=["m2"])
                    P.op("dve", TT(mg[:, oc, :], m1, m2, ALU.add), reads=["m1", "m2"], writes=[("mg", oc)])
                for oc2 in range(8):
                    bank = 5 + (ob_i % 2)
                    ob_i += 1
                    P.op("pe", [MM(ps[:, bank, :], wo[:, oc, oc2 * 128:(oc2 + 1) * 128], mg[:, oc, :], start=(oc == 0), stop=(oc == 7))
                                for oc in range(8)], reads=[("mg", oc) for oc in range(8)] + ["wo"],
                         writes=[("ps", bank)])
                    P.op("act", ACTF(mo[:, oc2, :], ps[:, bank, :], AF.Copy), reads=[("ps", bank)], writes=[("mo", oc2)])
                    P.op("dve", TT(sq[:, oc2, :], mo[:, oc2, :], mo[:, oc2, :], ALU.mult), reads=[("mo", oc2)], writes=[("sq", oc2)])
                P.op("pe", [MM(ps[:, 7, :], ones, sq[:, oc2, :], start=(oc2 == 0), stop=(oc2 == 7)) for oc2 in range(8)],
                     reads=[("sq", k) for k in range(8)] + ["ones"], writes=[("ps", 7)])
                P.op("dve", TS(rt, ps[:, 7, :], 1.0 / D, EPS, ALU.mult, ALU.add), reads=[("ps", 7)], writes=["rt"])
                P.op("act", ACTF(rt, rt, AF.Ln), reads=["rt"], writes=["rt"])
                P.op("act", ACTF(rstd, rt, AF.Exp, scale=-0.5), reads=["rt"], writes=["rstd"])
                for oc2 in range(8):
                    P.op("dve", STT(mo[:, oc2, :], mo[:, oc2, :], gains[:, 1, oc2:oc2 + 1], rstd, ALU.mult, ALU.mult),
                         reads=[("mo", oc2), "rstd", "gains"], writes=[("mo", oc2)])
                    P.op("pool", TT(xt[:, oc2, :], xt[:, oc2, :], mo[:, oc2, :], ALU.add), reads=[("mo", oc2), xt_key(t % 2)],
                         writes=[xt_key(t % 2)])
                P.op("sp", DMA(x1T_v[:, :, tok], xt), reads=[xt_key(t % 2)], dma_key=("x1o", t % 2))
                if t + 2 < 8:
                    load_x3(t + 2)
            P.barrier()
            P.emit()
            if stop_after == 4:
                return nc

        NT = 256
        with contextlib.ExitStack() as E5:
            wgt = sb(E5, "wgt", [128, 8, DFF], BF16)
            wup = sb(E5, "wup", [128, 8, DFF], BF16)
            wdn = sb(E5, "wdn", [128, NJ, 1024], BF16)
            xts = [sb(E5, "xtb%d" % i, [128, 8, NT], F32) for i in range(2)]
            sq = sb(E5, "sq5", [128, 8, NT], BF16)
            hTs = [sb(E5, "hT5_%d" % i, [128, 8, NT], BF16) for i in range(2)]
            sqn = sb(E5, "sqn5", [128, 8, NT], BF16)
            rt = sb(E5, "rt5", [128, NT], F32)
            rstd = sb(E5, "rstd5", [128, NT], F32)
            sg = [sb(E5, "sg%d" % i, [128, NT], F32) for i in range(2)]
            aT = sb(E5, "aT", [128, NJ, NT], BF16)
            fo = sb(E5, "fo", [128, 8, NT], F32)
            wg_v = w_gate.rearrange("(kc p) n -> p kc n", p=128)
            wu_v = w_up.rearrange("(kc p) n -> p kc n", p=128)
            wd_v = w_down.rearrange("(j p) n -> p j n", p=128)
            cast_load_all([(wgt[:, kc, :], wg_v[:, kc, :], DFF) for kc in range(8)], "wgt")
            cast_load_all([(wup[:, kc, :], wu_v[:, kc, :], DFF) for kc in range(8)], "wup")
            cast_load_all([(wdn[:, j, :], wd_v[:, j, :], 1024) for j in range(NJ)], "wdn")
            x1T_v = x1T.rearrange("(kc p) t -> p kc t", p=128)
            outT_v = outT.rearrange("(kc p) t -> p kc t", p=128)
            NTL = HALF // NT

            def load_x4(t):
                P.op("sp", DMA(xts[t % 2], x1T_v[:, :, t * NT:(t + 1) * NT]), writes=[xt_key(t % 2)], dma_key=("xtb", t % 2))

            def pst(idx):
                return ps[:, idx, 0:NT]

            load_x4(0)
            load_x4(1)
            rmsnorm_in(xts[0], 2, hTs[0], sqn, rt, rstd, NT, 0, 0, hp=0)
            gi = 0
            di = 0
            wk_g = ["wgt"]
            wk_u = ["wup"]
            for t in range(NTL):
                xt = xts[t % 2]
                hT = hTs[t % 2]
                hpar = t % 2
                for j in range(NJ):
                    if j == 11 and t + 1 < NTL:
                        rmsnorm_in(xts[(t + 1) % 2], 2, hTs[(t + 1) % 2], sqn, rt, rstd, NT, (t + 1) % 2, 0, hp=(t + 1) % 2)
                    gslot = 1 + (gi % 2)
                    uslot = 3 + (gi % 2)
                    gi += 1
                    js = slice(j * 128, (j + 1) * 128)
                    P.op("pe", [MM(pst(gslot), wgt[:, kc, js], hT[:, kc, :], start=(kc == 0), stop=(kc == 7)) for kc in range(8)],
                         reads=[("hT", hpar, kc) for kc in range(8)] + wk_g, writes=[("pt", gslot)])
                    P.op("pe", [MM(pst(uslot), wup[:, kc, js], hT[:, kc, :], start=(kc == 0), stop=(kc == 7)) for kc in range(8)],
                         reads=[("hT", hpar, kc) for kc in range(8)] + wk_u, writes=[("pt", uslot)])
                    sgs = sg[j % 2]
                    P.op("act", ACTF(sgs, pst(gslot), AF.Silu), reads=[("pt", gslot)], writes=[("sg", j % 2)])
                    P.op("dve", TT(aT[:, j, :], pst(uslot), sgs, ALU.mult), reads=[("pt", uslot), ("sg", j % 2)],
                         writes=[("aT", j)])
                for oc in range(8):
                    dslot = 5 + (di % 2)
                    di += 1
                    P.op("pe", [MM(pst(dslot), wdn[:, j, oc * 128:(oc + 1) * 128], aT[:, j, :], start=(j == 0), stop=(j == NJ - 1))
                                for j in range(NJ)], reads=[("aT", j) for j in range(NJ)] + ["wdn"],
                         writes=[("pt", dslot)])
                    P.op("act", ACTF(fo[:, oc, :], pst(dslot), AF.Copy), reads=[("pt", dslot)], writes=[("fo", oc)])
                    P.op("dve", TT(sq[:, oc, :], fo[:, oc, :], fo[:, oc, :], ALU.mult), reads=[("fo", oc)], writes=[("sq", oc)])
                P.op("pe", [MM(ps[:, 7, :NT], ones, sq[:, oc, :], start=(oc == 0), stop=(oc == 7)) for oc in range(8)],
                     reads=[("sq", k) for k in range(8)] + ["ones"], writes=[("ps", 7)])
                P.op("dve", TS(rt, ps[:, 7, :NT], 1.0 / D, EPS, ALU.mult, ALU.add), reads=[("ps", 7)], writes=["rt"])
                P.op("act", ACTF(rt, rt, AF.Ln), reads=["rt"], writes=["rt"])
                P.op("act", ACTF(rstd, rt, AF.Exp, scale=-0.5), reads=["rt"], writes=["rstd"])
                for oc in range(8):
                    P.op("dve", STT(fo[:, oc, :], fo[:, oc, :], gains[:, 3, oc:oc + 1], rstd, ALU.mult, ALU.mult),
                         reads=[("fo", oc), "rstd", "gains"], writes=[("fo", oc)])
                    P.op("pool", TT(xt[:, oc, :], xt[:, oc, :], fo[:, oc, :], ALU.add), reads=[("fo", oc), xt_key(t % 2)],
                         writes=[xt_key(t % 2)])
                P.op("sp", DMA(outT_v[:, :, t * NT:(t + 1) * NT], xt), reads=[xt_key(t % 2)], dma_key=("outo", t % 2))
                if t + 2 < NTL:
                    load_x4(t + 2)
            P.barrier()
            P.emit()
            if stop_after == 5:
                return nc
    return nc


def _na_bias_tables(rpb, hf):
    out = np.full((3, 5, 128, 8, 2, 64), NEG, dtype=np.float32)
    lc = np.arange(64)
    qc = lc if hf == 0 else 63 - lc
    kc = qc
    cstart = np.clip(qc - 8, 0, 48)
    col_in = (kc[:, None] >= cstart[None, :]) & (kc[:, None] < cstart[None, :] + 16)
    col_off = np.clip(kc[:, None] - qc[None, :] + 15, 0, 30)
    for cls, lr0 in enumerate([0, 2, 4]):
        t0 = max(lr0 - 4, 0)
        for rr in range(2):
            lr = lr0 + rr
            r = lr if hf == 0 else 127 - lr
            rs = min(max(r - 4, 0), 120)
            for j in range(5):
                for i in range(2):
                    krow_l = t0 + 2 * j + i
                    g = krow_l if hf == 0 else 127 - krow_l
                    if not (rs <= g < rs + 8):
                        continue
                    row_off = g - r + 7
                    vals = rpb[:, row_off, :][:, col_off]
                    vals = np.transpose(vals, (1, 0, 2))
                    vals = vals[:, [0, 2, 4, 6, 1, 3, 5, 7], :]
                    blk = np.where(col_in[:, None, :], vals, np.float32(NEG))
                    out[cls, j, i * 64:(i + 1) * 64, :, rr, :] = blk
    return out.reshape(3, 5, 128, 1024)


_CACHE = {}


def kernel(x, pre_mix_w, w_in, b_gate, lambda_q1, lambda_k1, lambda_q2, lambda_k2,
           subln_w, rpb, w_branch_a, w_branch_b, w_out, post_mix_w, pre_ffn_w,
           w_gate, w_up, w_down, post_ffn_w):
    f32 = np.float32
    x = np.asarray(x, dtype=f32)
    if "nc" not in _CACHE:
        _CACHE["nc"] = build_program()
    nc = _CACHE["nc"]

    def vec8(v):
        return np.asarray(v, f32).reshape(8, 128).T

    gains = np.ascontiguousarray(np.stack([vec8(pre_mix_w[0]), vec8(post_mix_w[0]), vec8(pre_ffn_w[0]), vec8(post_ffn_w[0])], axis=1))
    bgate = np.ascontiguousarray(np.asarray(b_gate[0], f32).reshape(16, 128).T)
    subln = np.ascontiguousarray(np.asarray(subln_w[0], f32).reshape(128, 1))
    lamv = np.stack([np.asarray(v[0], f32) for v in (lambda_q1, lambda_k1, lambda_q2, lambda_k2)], axis=0)
    lamv = np.ascontiguousarray(np.broadcast_to(lamv[None], (128, 4, 64)))
    rpb0 = np.asarray(rpb[0], f32)
    nab = [_na_bias_tables(rpb0, 0), _na_bias_tables(rpb0, 1)]
    common = {
        "w_in": np.ascontiguousarray(np.asarray(w_in[0], f32)),
        "w_a": np.ascontiguousarray(np.asarray(w_branch_a[0], f32)),
        "w_b": np.ascontiguousarray(np.asarray(w_branch_b[0], f32)),
        "w_o": np.ascontiguousarray(np.asarray(w_out[0], f32)),
        "w_gate": np.ascontiguousarray(np.asarray(w_gate[0], f32)),
        "w_up": np.ascontiguousarray(np.asarray(w_up[0], f32)),
        "w_down": np.ascontiguousarray(np.asarray(w_down[0], f32)),
        "gains": gains, "bgate": bgate, "subln": subln, "lamv": lamv,
        "sublnT": np.ascontiguousarray(np.broadcast_to(np.asarray(subln_w[0], f32)[None, :], (128, 128))),
    }
    in_maps = []
    for core in range(8):
        b, hf = core // 2, core % 2
        xb = x[b] if hf == 0 else x[b, ::-1]
        m = dict(common)
        m["xT"] = np.ascontiguousarray(xb.T)
        m["nab"] = nab[hf]
        in_maps.append(m)
    res = run_bass_kernel_spmd(nc, in_maps, core_ids=list(range(8)))
    out = np.empty((B, S, D), dtype=f32)
    for core in range(8):
        b, hf = core // 2, core % 2
        o = res.results[core]["outT"].T
        if hf == 0:
            out[b, :HALF] = o
        else:
            out[b, HALF:] = o[::-1]
    return out
```
